# Optimizing a Trainium2 kernel written in Bass

```python
import jax, jax.numpy as jnp
from jax import lax
import numpy as np

D_MODEL = 2048
BATCH = 2
SEQ = 4096
DEPTH = 1

HEAD_DIM = 128
HEADS_PER_GROUP = 8
DILATED_GROUPS = ((128, 1), (512, 4), (2048, 16))
N_ATTN_GROUPS = len(DILATED_GROUPS)
ATTN_QKV = N_ATTN_GROUPS * HEADS_PER_GROUP * HEAD_DIM
ATTN_OUT = HEADS_PER_GROUP * HEAD_DIM
BLK = 128
ROPE_THETA = 500000.0
ROT_DIM = HEAD_DIM // 4
POOL_SIZES = (2, 4, 8, 16)
POOL_WIDTH = D_MODEL // 2
POOL_GROUP = POOL_WIDTH // len(POOL_SIZES)
IN_COLS = 3 * ATTN_QKV + ATTN_OUT + 2 * POOL_WIDTH + 2 * D_MODEL
NORM_EPS = 1e-6

kernel_name = "hybrid_dilated_attn_pool_gated_merge"


def rms_norm(x, g):
    xf = x.astype(jnp.float32)
    y = xf * lax.rsqrt(jnp.mean(xf * xf, axis=-1, keepdims=True) + NORM_EPS)
    return (y * g.astype(jnp.float32)).astype(x.dtype)


def partial_rope(t, pos):
    half = ROT_DIM // 2
    inv_freq = ROPE_THETA ** (-jnp.arange(0, ROT_DIM, 2, dtype=jnp.float32) / ROT_DIM)
    ang = pos.astype(jnp.float32)[:, None] * inv_freq[None, :]
    cos = jnp.concatenate([jnp.cos(ang), jnp.cos(ang)], axis=-1)[None, :, None, :]
    sin = jnp.concatenate([jnp.sin(ang), jnp.sin(ang)], axis=-1)[None, :, None, :]
    tr = t[..., :ROT_DIM].astype(jnp.float32)
    rot_half = jnp.concatenate([-tr[..., half:], tr[..., :half]], axis=-1)
    tr = (tr * cos + rot_half * sin).astype(t.dtype)
    return jnp.concatenate([tr, t[..., ROT_DIM:]], axis=-1)


def dilated_window_attention(q, k, v, window, dilation):
    B, S, H, C = q.shape
    w_sub = window // dilation
    L = S // dilation
    nb = -(-L // BLK)
    Lp = nb * BLK

    def to_blocks(t):
        t = t.reshape(B, L, dilation, H, C).transpose(0, 2, 1, 3, 4)
        t = jnp.pad(t, ((0, 0), (0, 0), (0, Lp - L), (0, 0), (0, 0)))
        return t.reshape(B, dilation, nb, BLK, H, C)

    def with_prev(t):
        prev = jnp.pad(t, ((0, 0), (0, 0), (1, 0), (0, 0), (0, 0), (0, 0)))[:, :, :nb]
        return jnp.concatenate([prev, t], axis=3)

    qb = to_blocks(q)
    kk = with_prev(to_blocks(k))
    vv = with_prev(to_blocks(v))
    s = jnp.einsum('brnqhc,brnkhc->brnhqk', qb, kk).astype(jnp.float32) * (C ** -0.5)
    blk = jnp.arange(nb)[:, None, None]
    qpos = blk * BLK + jnp.arange(BLK)[None, :, None]
    kpos = (blk - 1) * BLK + jnp.arange(2 * BLK)[None, None, :]
    dist = qpos - kpos
    mask = (kpos >= 0) & (dist >= 0) & (dist <= w_sub)
    s = jnp.where(mask[None, None, :, None], s, -jnp.inf)
    lse = jax.nn.logsumexp(s, axis=-1)
    p = jnp.exp(s - lse[..., None])
    o = jnp.einsum('brnhqk,brnkhc->brnqhc', p.astype(v.dtype), vv)
    o = o.reshape(B, dilation, Lp, H, C)[:, :, :L].transpose(0, 2, 1, 3, 4).reshape(B, S, H, C)
    lse = lse.transpose(0, 1, 2, 4, 3).reshape(B, dilation, Lp, H)[:, :, :L]
    lse = lse.transpose(0, 2, 1, 3).reshape(B, S, H)
    return o, lse


def causal_mean(u, k):
    S = u.shape[1]
    cs = jnp.cumsum(u.astype(jnp.float32), axis=1)
    prev = jnp.pad(cs, ((0, 0), (k, 0), (0, 0)))[:, :S]
    cnt = jnp.minimum(jnp.arange(S) + 1, k).astype(jnp.float32)
    return (cs - prev) / cnt[None, :, None]


def setup_inputs(seed: int = 0) -> dict:
    key = jax.random.key(seed)
    ks = jax.random.split(key, 12)
    f = jnp.float32
    x = jax.random.normal(ks[0], (BATCH, SEQ, D_MODEL), f)
    norm_gain = 1.0 + 0.02 * jax.random.normal(ks[1], (D_MODEL,), f)
    w_in = jax.random.normal(ks[2], (D_MODEL, IN_COLS), f) * D_MODEL ** -0.5
    b_gates = 0.02 * jax.random.normal(ks[3], (2 * D_MODEL,), f)
    q_norm_gain = 1.0 + 0.02 * jax.random.normal(ks[4], (HEAD_DIM,), f)
    k_norm_gain = 1.0 + 0.02 * jax.random.normal(ks[5], (HEAD_DIM,), f)
    pool_maps = jax.random.normal(ks[6], (len(POOL_SIZES), POOL_GROUP, POOL_GROUP), f) * POOL_GROUP ** -0.5
    pool_scale = 1.0 + 0.1 * jax.random.normal(ks[7], (POOL_WIDTH,), f)
    w_branch_attn = jax.random.normal(ks[8], (ATTN_OUT, D_MODEL), f) * ATTN_OUT ** -0.5
    w_branch_pool = jax.random.normal(ks[9], (POOL_WIDTH, D_MODEL), f) * POOL_WIDTH ** -0.5
    w_out = jax.random.normal(ks[10], (D_MODEL, D_MODEL), f) * D_MODEL ** -0.5
    return {"x": x, "norm_gain": norm_gain, "w_in": w_in, "b_gates": b_gates,
            "q_norm_gain": q_norm_gain, "k_norm_gain": k_norm_gain,
            "pool_maps": pool_maps, "pool_scale": pool_scale,
            "w_branch_attn": w_branch_attn, "w_branch_pool": w_branch_pool, "w_out": w_out}


def reference(x, norm_gain, w_in, b_gates, q_norm_gain, k_norm_gain, pool_maps, pool_scale,
              w_branch_attn, w_branch_pool, w_out):
    B, S, _ = x.shape
    pos = jnp.arange(S, dtype=jnp.int32)
    for _layer in range(DEPTH):
        h = rms_norm(x, norm_gain)
        proj = jnp.einsum('bsd,de->bse', h, w_in)
        splits = np.cumsum([ATTN_QKV, ATTN_QKV, ATTN_QKV, ATTN_OUT, POOL_WIDTH, POOL_WIDTH]).tolist()
        q, k, v, z_attn, u_pool, z_pool, gates = jnp.split(proj, splits, axis=-1)
        shp = (B, S, N_ATTN_GROUPS, HEADS_PER_GROUP, HEAD_DIM)
        q = rms_norm(q.reshape(shp), q_norm_gain)
        k = rms_norm(k.reshape(shp), k_norm_gain)
        v = v.reshape(shp)

        outs, lses = [], []
        for g, (window, dilation) in enumerate(DILATED_GROUPS):
            qg = partial_rope(q[:, :, g], pos)
            kg = partial_rope(k[:, :, g], pos)
            o_g, lse_g = dilated_window_attention(qg, kg, v[:, :, g], window, dilation)
            outs.append(o_g)
            lses.append(lse_g)
        mix_w = jax.nn.softmax(jnp.stack(lses, axis=0), axis=0)
        attn = jnp.sum(mix_w[..., None] * jnp.stack(outs, axis=0).astype(jnp.float32), axis=0)
        attn = attn.astype(x.dtype).reshape(B, S, ATTN_OUT)
        y_attn = jnp.einsum('bsc,cd->bsd', attn * jax.nn.silu(z_attn), w_branch_attn)

        pooled = []
        for g, ksz in enumerate(POOL_SIZES):
            u_g = u_pool[..., g * POOL_GROUP:(g + 1) * POOL_GROUP]
            d_g = (causal_mean(u_g, ksz) - u_g.astype(jnp.float32)).astype(x.dtype)
            pooled.append(jnp.einsum('bsc,ce->bse', d_g, pool_maps[g]))
        pool = jnp.concatenate(pooled, axis=-1) * pool_scale
        y_pool = jnp.einsum('bsc,cd->bsd', pool * jax.nn.silu(z_pool), w_branch_pool)

        gate = jax.nn.sigmoid((gates + b_gates).astype(jnp.float32)).astype(x.dtype)
        g_attn, g_pool = jnp.split(gate, 2, axis=-1)
        merged = g_attn * y_attn + g_pool * y_pool
        x = x + jnp.einsum('bsd,de->bse', merged, w_out)
    return x
```

```python
import numpy as np
import concourse.bass as bass
import concourse.mybir as mybir
from concourse.bass_utils import run_bass_kernel_spmd

F32 = mybir.dt.float32
BF16 = mybir.dt.bfloat16
AF = mybir.ActivationFunctionType
ALU = mybir.AluOpType

D = 2048
KC = 16
TOWN = 1024
THALO = 2048
NTOK = TOWN + THALO
SEQ = 4096
EPS = 1e-6
NHEAD = 8
SQ128 = float(np.sqrt(128.0))

CQ, CK, CV, CZA, CU, CZP, CG = 0, 3072, 6144, 9216, 10240, 11264, 12288

C_QG, C_KG, C_PS, C_BG, C_IC = 0, 1, 2, 10, 42
NCST = 42 + 64
M_A, M_B, M_C, M_ID, M_RT = 0, 512, 1024, 1536, 1664
NMSK = 1696


class Buf:
    __slots__ = ("name", "w", "r")

    def __init__(self, name):
        self.name = name
        self.w = None
        self.r = {}


class Op:
    __slots__ = ("eng", "fn", "deps", "signal", "semval", "key", "seq", "isdma", "phase")


class Sched:
    ENGS = ("pe", "act", "dve", "pool", "sp")

    def __init__(self, nc, sems):
        self.nc = nc
        self.sems = sems
        self.streams = {e: [] for e in self.ENGS}
        self.seqctr = {k: 0 for k in sems}
        self.semcnt = {k: 0 for k in sems}
        self.waited = {e: {k: 0 for k in sems} for e in self.ENGS}
        self.lastop = {k: None for k in sems}
        self.phase = 0

    def op(self, eng, fn, reads=(), writes=(), dsem=None):
        o = Op()
        o.eng = eng
        o.fn = fn
        o.signal = False
        o.semval = None
        o.phase = self.phase
        o.isdma = dsem is not None
        o.key = dsem if dsem is not None else eng
        o.seq = self.seqctr[o.key]
        self.seqctr[o.key] += 1
        deps = {}

        def add(d):
            if d is None or d is o:
                return
            cur = deps.get(d.key)
            if cur is None or cur.seq < d.seq:
                deps[d.key] = d

        for b in reads:
            add(b.w)
        for b in writes:
            add(b.w)
            for d in b.r.values():
                add(d)
        for b in reads:
            b.r[o.key] = o
        for b in writes:
            b.w = o
            b.r = {}
        o.deps = deps
        self.streams[eng].append(o)
        self.lastop[o.key] = o
        return o

    def emit_phase(self):
        nc = self.nc
        streams = self.streams
        for e in self.ENGS:
            for o in streams[e]:
                if o.isdma:
                    o.signal = True
                for k, d in o.deps.items():
                    if d.phase != self.phase or (d.eng == "pe" and o.eng == "pe" and not d.isdma):
                        continue
                    d.signal = True
        for k, o in self.lastop.items():
            if o is not None:
                o.signal = True
        for e in self.ENGS:
            for o in streams[e]:
                if o.signal:
                    self.semcnt[o.key] += 16 if o.isdma else 1
                    o.semval = self.semcnt[o.key]
        final = dict(self.semcnt)
        sems = self.sems
        waited = self.waited

        def run(ename, eng):
            w = waited[ename]
            for o in streams[ename]:
                for k, d in o.deps.items():
                    if d.phase != self.phase or (d.eng == "pe" and ename == "pe" and not d.isdma):
                        continue
                    if w[k] < d.semval:
                        eng.wait_ge(sems[k], d.semval)
                        w[k] = d.semval
                ins = o.fn(eng)
                if o.signal:
                    ins.then_inc(sems[o.key], 16 if o.isdma else 1)
            for k, v in final.items():
                if w[k] < v:
                    eng.wait_ge(sems[k], v)
                    w[k] = v

        with nc.Block() as block:
            @block.tensor
            def _(t):
                run("pe", t)

            @block.scalar
            def _(a):
                run("act", a)

            @block.vector
            def _(v):
                run("dve", v)

            @block.gpsimd
            def _(g):
                run("pool", g)

            @block.sync
            def _(s):
                run("sp", s)
        self.streams = {e: [] for e in self.ENGS}
        self.lastop = {k: None for k in self.sems}
        self.phase += 1


def build_program(debug=False):
    nc = bass.Bass("TRN2", target_bir_lowering=False)
    xh = nc.dram_tensor("xh", [NTOK, D], F32, kind="ExternalInput").ap()
    w_in = nc.dram_tensor("w_in", [D, 16384], F32, kind="ExternalInput").ap()
    w_ba = nc.dram_tensor("w_ba", [1024, D], F32, kind="ExternalInput").ap()
    w_bp = nc.dram_tensor("w_bp", [1024, D], F32, kind="ExternalInput").ap()
    w_out = nc.dram_tensor("w_out", [D, D], F32, kind="ExternalInput").ap()
    pmaps = nc.dram_tensor("pmaps", [4, 256, 256], F32, kind="ExternalInput").ap()
    gain = nc.dram_tensor("gain", [128, D], F32, kind="ExternalInput").ap()
    cst_d = nc.dram_tensor("cst", [128, NCST], F32, kind="ExternalInput").ap()
    msk_d = nc.dram_tensor("msk", [128, NMSK], F32, kind="ExternalInput").ap()
    tab_d = nc.dram_tensor("tab", [32, 2, NTOK], F32, kind="ExternalInput").ap()
    out_d = nc.dram_tensor("out", [TOWN, D], F32, kind="ExternalOutput").ap()

    w_in_v = w_in.rearrange("(kc p) n -> p kc n", p=128)

    from contextlib import ExitStack
    es = ExitStack()
    with es:
        def sb(name, shape, dt):
            return es.enter_context(nc.sbuf_tensor("s_" + name, shape, dt))

        semnames = (["pe", "act", "dve", "pool", "sp"] + [f"dw{i}" for i in range(4)] + [f"dx{i}" for i in range(3)]
                    + [f"do{i}" for i in range(3)] + [f"dc{i}" for i in range(5)] + [f"db{i}" for i in range(4)]
                    + [f"dwo{i}" for i in range(2)] + [f"dr{i}" for i in range(3)] + ["drot0", "drot1"])
        sems = {k: es.enter_context(nc.semaphore("s_" + k)) for k in semnames}
        S = Sched(nc, sems)

        hTo = sb("hTo", [128, KC, TOWN], BF16)
        aT = sb("aT", [128, NHEAD, TOWN], BF16)
        cst = sb("cst", [128, NCST], F32)
        mskb = sb("mskb", [128, NMSK], BF16)
        NW = 3
        wring = [sb(f"wr{i}", [128, KC, 128], BF16) for i in range(NW)]
        ones = sb("ones", [128, 128], BF16)
        pshalf = sb("pshalf", [128, 8], F32)
        halfb = sb("halfb", [128, 32], F32)
        uh = sb("uh", [128, 8, 16], F32)
        B_hTh, B_hTo, B_cst, B_mskb, B_ones = Buf("hTh"), Buf("hTo"), Buf("cst"), Buf("mskb"), Buf("ones")
        B_aT = [Buf(f"aT{h}") for h in range(NHEAD)]
        B_wr = [Buf(f"wr{i}") for i in range(NW)]
        B_misc = Buf("misc")
        B_uh = Buf("uh")
        wctr = [0]

        def load_wblock(col0):
            i = wctr[0] % NW
            wctr[0] += 1
            t = wring[i]
            S.op("pool", lambda g, t=t, col0=col0: g.dma_start(out=t[:], in_=w_in_v[:, :, col0:col0 + 128]),
                 writes=[B_wr[i]], dsem=f"dw{i}")
            return t, B_wr[i]

        ident = mskb[:, M_ID:M_ID + 128]
        rt = mskb[0:32, M_RT:M_RT + 32]

        own_q, bg_q = [], []

        def pop_own():
            if own_q:
                own_q.pop(0)()
                return True
            return False

        def pop_bg():
            if bg_q:
                bg_q.pop(0)()
                return True
            return False

        def flush_own():
            while pop_own():
                pass

        def need(tag):
            while any(tag in getattr(f, "res", ()) for f in own_q):
                pop_own()

        def flush_bg():
            while bg_q:
                flush_own()
                pop_bg()
            flush_own()

        eAB = ExitStack()
        hTh = eAB.enter_context(nc.sbuf_tensor("s_hTh", [128, KC, THALO], BF16))

        def project(pp, B_pp, ctrd, wt, Bw, mov, n):
            i = ctrd["pp"] % len(pp)
            ctrd["pp"] += 1
            need(("pp", i))
            while len(own_q) > 4:
                pop_own()
            for kc in range(KC):
                S.op("pe", lambda t, i=i, kc=kc, mov=mov, wt=wt, n=n: t.matmul(
                    pp[i][:, 0:n], lhsT=wt[:, kc, :], rhs=mov(kc), start=(kc == 0), stop=(kc == KC - 1)),
                    reads=[Bw, B_hTh, B_hTo], writes=[B_pp[i]])
                if kc in (3, 7, 11, 15):
                    pop_own()
                elif kc in (1, 5, 9, 13):
                    pop_bg()
            return i

        with ExitStack() as ea:
            def sba(name, shape, dt):
                return ea.enter_context(nc.sbuf_tensor("a_" + name, shape, dt))

            xt = [sba(f"xt{i}", [128, D], F32) for i in range(3)]
            xn = [sba(f"xn{i}", [128, D], BF16) for i in range(2)]
            junk = sba("junk", [128, D], BF16)
            gb = sba("gb", [128, D], F32)
            mskf = sba("mskf", [128, NMSK], F32)
            stat = sba("stat", [128, 3 * 24], F32)
            tpa = [ea.enter_context(nc.psum_tensor(f"tpa{i}", [128, 8, 128], BF16)) for i in range(2)]
            B_xt = [Buf("xt0"), Buf("xt1"), Buf("xt2")]
            B_xn = [Buf("xn0"), Buf("xn1")]
            B_junk, B_gb, B_mskf, B_stat = Buf("junk"), Buf("gb"), Buf("mskf"), Buf("stat")
            B_st = [Buf(f"st{j}") for j in range(NTOK // 128)]
            B_tpa = [Buf("tpa0"), Buf("tpa1")]

            S.op("sp", lambda s: s.dma_start(out=cst[:], in_=cst_d), writes=[B_cst], dsem="dc0")
            S.op("sp", lambda s: s.dma_start(out=mskf[:], in_=msk_d), writes=[B_mskf], dsem="dc1")
            S.op("sp", lambda s: s.dma_start(out=gb[:], in_=gain), writes=[B_gb], dsem="dc2")
            S.op("dve", lambda v: v.tensor_copy(out=mskb[:], in_=mskf[:]), reads=[B_mskf], writes=[B_mskb])
            S.op("dve", lambda v: v.memset(ones[:], 1.0), writes=[B_ones])
            S.op("dve", lambda v: v.memset(stat[:], 0.0), writes=[B_stat])
            S.op("dve", lambda v: v.tensor_scalar_mul(out=pshalf[:], in0=cst[:, C_PS:C_PS + 8], scalar1=0.5),
                 reads=[B_cst], writes=[B_misc])
            S.op("dve", lambda v: v.tensor_scalar_mul(out=halfb[:], in0=cst[:, C_BG:C_BG + 32], scalar1=0.5),
                 reads=[B_cst], writes=[B_misc])
            S.op("dve", lambda v: v.tensor_scalar_mul(out=gb[:], in0=gb[:], scalar1=float(np.sqrt(D))),
                 reads=[B_gb], writes=[B_gb])

            NTT = NTOK // 128

            def a_front(j):
                b, b3, c0 = j % 2, j % 3, 3 * j
                S.op("sp", lambda s: s.dma_start(out=xt[b3][:], in_=xh[j * 128:(j + 1) * 128, :]),
                     writes=[B_xt[b3]], dsem=f"dx{b3}")
                S.op("act", lambda a: a.activation(out=junk[:], in_=xt[b3][:], func=AF.Square,
                                                   accum_out=stat[:, c0:c0 + 1]),
                     reads=[B_xt[b3], B_stat], writes=[B_junk, B_st[j]])
                S.op("act", lambda a: a.activation(out=stat[:, c0 + 1:c0 + 2], in_=stat[:, c0:c0 + 1],
                                                   func=AF.Ln, bias=float(D * EPS)),
                     reads=[B_st[j]], writes=[B_st[j]])
                S.op("act", lambda a: a.activation(out=stat[:, c0 + 2:c0 + 3], in_=stat[:, c0 + 1:c0 + 2],
                                                   func=AF.Exp, scale=-0.5),
                     reads=[B_st[j]], writes=[B_st[j]])
                S.op("dve", lambda v: v.scalar_tensor_tensor(
                    out=xn[b][:], in0=xt[b3][:], scalar=stat[:, c0 + 2:c0 + 3], in1=gb[:],
                    op0=ALU.mult, op1=ALU.mult),
                    reads=[B_xt[b3], B_st[j], B_gb], writes=[B_xn[b]])

            def a_back(j):
                b = j % 2
                for half in range(2):
                    for k8 in range(8):
                        kc = half * 8 + k8
                        S.op("pe", lambda t, half=half, k8=k8, kc=kc: t.transpose(
                            tpa[half][:, k8, :], xn[b][:, kc * 128:(kc + 1) * 128], ident),
                            reads=[B_xn[b], B_mskb], writes=[B_tpa[half]])
                    if j < 16:
                        dst = hTh[:, half * 8:(half + 1) * 8, j * 128:(j + 1) * 128]
                        bd = B_hTh
                    else:
                        dst = hTo[:, half * 8:(half + 1) * 8, (j - 16) * 128:(j - 15) * 128]
                        bd = B_hTo
                    if half == 0:
                        S.op("act", lambda a, dst=dst, half=half: a.copy(out=dst, in_=tpa[half][:]),
                             reads=[B_tpa[half]], writes=[bd])
                    else:
                        S.op("dve", lambda v, dst=dst, half=half: v.tensor_copy(out=dst, in_=tpa[half][:]),
                             reads=[B_tpa[half]], writes=[bd])

            for j in range(NTT + 1):
                if j < NTT:
                    a_front(j)
                if j >= 1:
                    a_back(j - 1)
            S.emit_phase()

        with ExitStack() as eb:
            def sbb(name, shape, dt):
                return eb.enter_context(nc.sbuf_tensor("b_" + name, shape, dt))

            def psb(name, shape, dt):
                return eb.enter_context(nc.psum_tensor("bp_" + name, shape, dt))

            tab = sbb("tab", [32, 2, NTOK], BF16)
            B_tab = Buf("tab")
            NU = 2
            kS = [sbb(f"kS{i}", [128, 3072], BF16) for i in range(NU)]
            qS = [sbb(f"qS{i}", [128, 1024], BF16) for i in range(NU)]
            vS = [sbb(f"vS{i}", [128, 24, 128], BF16) for i in range(NU)]
            B_kS = [Buf(f"kS{i}") for i in range(NU)]
            B_qS = [Buf(f"qS{i}") for i in range(NU)]
            B_vS = [Buf(f"vS{i}") for i in range(NU)]
            vT = [sbb(f"vT{i}", [128, 512], BF16) for i in range(2)]
            B_vT = [Buf("vT0"), Buf("vT1")]
            vT3 = sbb("vT3", [128, 3072], BF16)
            B_vT3 = Buf("vT3")
            sq = [sbb(f"sq{i}", [128, 512], BF16) for i in range(2)]
            B_sq = [Buf("sq0"), Buf("sq1")]
            rr = [sbb(f"rr{i}", [128, 512], F32) for i in range(2)]
            B_rr = [Buf("rr0"), Buf("rr1")]
            rotb = sbb("rotb", [32, 1536], BF16)
            B_rot = [Buf("rot0"), Buf("rot1")]
            NE = 3
            Eb = [sbb(f"E{i}", [128, 512], BF16) for i in range(NE)]
            B_E = [Buf(f"E{i}") for i in range(NE)]
            NUMs = sbb("NUM", [128, TOWN], F32)
            DENs = sbb("DEN", [128, TOWN], F32)
            szs = sbb("sz", [128, TOWN], F32)
            ez = [sbb(f"ez{i}", [128, 512], F32) for i in range(2)]
            B_NUM, B_DEN, B_sz = Buf("NUM"), Buf("DEN"), Buf("sz")
            B_ez = [Buf("ez0"), Buf("ez1")]

            pp = [psb(f"pp{i}", [128, 512], F32) for i in range(3)]
            B_pp = [Buf("pp0"), Buf("pp1"), Buf("pp2")]
            prot = psb("prot", [128, 512], F32)
            B_prot = Buf("prot")
            pS = prot
            B_pS = B_prot
            paux = prot[:, :].bitcast(BF16)
            B_paux = B_prot
            psc = [psb(f"psc{i}", [128, 512], F32) for i in range(2)]
            B_psc = [Buf("psc0"), Buf("psc1")]
            pnds = [psb(f"pnd{i}", [128, 512], F32) for i in range(2)]
            B_pnds = [Buf("pnd0"), Buf("pnd1")]

            S.op("pool", lambda g: g.dma_start(out=tab[:], in_=tab_d), writes=[B_tab], dsem="dc3")

            ctr = {"pp": 0, "vT": 0, "sq": 0, "rr": 0, "rtmp": 0, "E": 0, "psc": 0, "ez": 0, "pnd": 0}

            def nxt(name, n):
                i = ctr[name] % n
                ctr[name] += 1
                return i

            hTh_v16 = lambda kc: hTh[:, kc, :].rearrange("p (l r) -> p r l", r=16)
            hTo_v16 = lambda kc: hTo[:, kc, :].rearrange("p (l r) -> p r l", r=16)

            def tabv(cs, lo, hi, r):
                a = tab[:, cs, lo:hi]
                if r > 1:
                    a = a.rearrange("p (l r) -> p r l", r=r)
                return a

            def groups_for(g, kind, u):
                res = []
                ks, qs, vs = kS[u], qS[u], vS[u]
                if g == 0:
                    lst = []
                    if kind != "q":
                        lst.append(("h", 1920, 2048))
                    lst += [("o", 0, 512), ("o", 512, 1024)]
                    for (wh, lo, hi) in lst:
                        n = hi - lo
                        if wh == "h":
                            mov = lambda kc, lo=lo, hi=hi: hTh[:, kc, lo:hi]
                            tlo = lo
                            kd = ks[:, 0:128]
                            vt0 = 0
                        else:
                            mov = lambda kc, lo=lo, hi=hi: hTo[:, kc, lo:hi]
                            tlo = THALO + lo
                            kd = ks[:, 128 + lo:128 + hi]
                            vt0 = 1 + lo // 128
                        dst = kd if kind == "k" else (qs[:, lo:hi] if kind == "q" else None)
                        res.append(dict(mov=mov, n=n, r=1, dst=dst, cos=tabv(0, tlo, tlo + n, 1),
                                        sin=tabv(1, tlo, tlo + n, 1), vdst=vs[:, vt0:vt0 + n // 128, :]))
                elif g == 1:
                    ks3 = ks[:, 0:1536].rearrange("p (r l) -> p r l", r=4)
                    qs3 = qs[:, 0:1024].rearrange("p (r l) -> p r l", r=4)
                    vs4 = vs[:, 0:12, :].rearrange("p (r t) c -> p r t c", r=4)
                    lst = []
                    if kind != "q":
                        lst.append(("h", 1536, 2048, 0))
                    lst += [("o", 0, 512, 1), ("o", 512, 1024, 2)]
                    for (wh, lo, hi, t) in lst:
                        if wh == "h":
                            mov = lambda kc, lo=lo, hi=hi: hTh[:, kc, lo:hi]
                            tlo = lo
                        else:
                            mov = lambda kc, lo=lo, hi=hi: hTo[:, kc, lo:hi]
                            tlo = THALO + lo
                        dst = ks3[:, :, t * 128:(t + 1) * 128] if kind == "k" else (
                            qs3[:, :, (t - 1) * 128:t * 128] if kind == "q" else None)
                        res.append(dict(mov=mov, n=512, r=4, dst=dst, cos=tabv(0, tlo, tlo + 512, 4),
                                        sin=tabv(1, tlo, tlo + 512, 4), vdst=vs4[:, :, t, :]))
                else:
                    ksh = ks[:, 0:2048].rearrange("p (r l) -> p r l", r=16)
                    kso = ks[:, 2048:3072].rearrange("p (r l) -> p r l", r=16)
                    qs3 = qs[:, 0:1024].rearrange("p (r l) -> p r l", r=16)
                    v3h = vT3[:, 0:2048].rearrange("p (r l) -> p r l", r=16)
                    v3o = vT3[:, 2048:3072].rearrange("p (r l) -> p r l", r=16)
                    if kind != "q":
                        for m in range(4):
                            mov = lambda kc, m=m: hTh[:, kc, 512 * m:512 * m + 512]
                            res.append(dict(mov=mov, n=512, r=16, dst=ksh[:, :, 32 * m:32 * m + 32],
                                            cos=tabv(0, 512 * m, 512 * m + 512, 16),
                                            sin=tabv(1, 512 * m, 512 * m + 512, 16),
                                            v3dst=v3h[:, :, 32 * m:32 * m + 32], last=False))
                    for m in range(2):
                        mov = lambda kc, m=m: hTo[:, kc, 512 * m:512 * m + 512]
                        dst = kso[:, :, 32 * m:32 * m + 32] if kind == "k" else qs3[:, :, 32 * m:32 * m + 32]
                        tlo = THALO + 512 * m
                        res.append(dict(mov=mov, n=512, r=16, dst=dst,
                                        cos=tabv(0, tlo, tlo + 512, 16), sin=tabv(1, tlo, tlo + 512, 16),
                                        v3dst=v3o[:, :, 32 * m:32 * m + 32], last=(m == 1)))
                return res

            def viewd(ap2, r):
                return ap2 if r == 1 else ap2.rearrange("p (l r) -> p r l", r=r)

            def view3(ap2, r):
                return ap2 if r == 1 else ap2.rearrange("p (r l) -> p r l", r=r)

            def post_qk(i, gd, gcol):
                n, r = gd["n"], gd["r"]
                P = pp[i][:, 0:n]
                BD = gd["B"]
                dst = gd["dst"]
                dst32 = dst[0:32]
                si = nxt("sq", 2)
                need(("sq", si))
                S.op("act", lambda a: a.activation(out=sq[si][:, 0:n], in_=P, func=AF.Square),
                     reads=[B_pp[i]], writes=[B_sq[si]])

                def stage_b():
                    S.op("pe", lambda t: t.matmul(pS[:, 0:n], lhsT=ones[:], rhs=sq[si][:, 0:n], start=True, stop=True),
                         reads=[B_sq[si], B_ones], writes=[B_pS])
                    ri = nxt("rr", 2)
                    S.op("act", lambda a: a.activation(out=rr[ri][:, 0:n], in_=pS[:, 0:n], func=AF.Ln,
                                                       bias=float(128 * EPS)),
                         reads=[B_pS], writes=[B_rr[ri]])
                    S.op("act", lambda a: a.activation(out=rr[ri][:, 0:n], in_=rr[ri][:, 0:n], func=AF.Exp, scale=-0.5),
                         reads=[B_rr[ri]], writes=[B_rr[ri]])
                    S.op("dve", lambda v: v.scalar_tensor_tensor(
                        out=dst, in0=viewd(P, r), scalar=cst[:, gcol:gcol + 1], in1=viewd(rr[ri][:, 0:n], r),
                        op0=ALU.mult, op1=ALU.mult),
                        reads=[B_pp[i], B_rr[ri], B_cst], writes=[BD])

                stage_b.res = (("pp", i), ("sq", si))
                own_q.append(stage_b)

            def post_v(i, gd, g):
                n, r = gd["n"], gd["r"]
                BD = gd["B"]
                if g < 2:
                    vi = nxt("vT", 2)
                    need(("vT", vi))
                    S.op("act", lambda a: a.copy(out=view3(vT[vi][:, 0:n], r), in_=viewd(pp[i][:, 0:n], r)),
                         reads=[B_pp[i]], writes=[B_vT[vi]])
                    nch = n // 128
                    vdst = gd["vdst"]

                    def stage_b():
                        for c in range(nch):
                            S.op("pe", lambda t, c=c: t.transpose(paux[:, c * 128:(c + 1) * 128],
                                                                  vT[vi][:, c * 128:(c + 1) * 128], ident),
                                 reads=[B_vT[vi], B_mskb], writes=[B_paux])
                        src = paux[:, 0:n].rearrange("p (t c) -> p t c", c=128)
                        S.op("act", lambda a: a.copy(out=vdst, in_=src), reads=[B_paux], writes=[BD])

                    stage_b.res = (("vT", vi),)
                    own_q.append(stage_b)
                else:
                    v3dst = gd["v3dst"]
                    need(("vT3",))
                    S.op("act", lambda a: a.copy(out=v3dst, in_=viewd(pp[i][:, 0:n], r)),
                         reads=[B_pp[i]], writes=[B_vT3])
                    if gd["last"]:
                        vs = vS[gd["u"]]
                        for j in range(3):
                            def stage_t(j=j):
                                for c in range(8):
                                    S.op("pe", lambda t, c=c: t.transpose(
                                        paux[:, c * 128:(c + 1) * 128],
                                        vT3[:, (8 * j + c) * 128:(8 * j + c + 1) * 128], ident),
                                        reads=[B_vT3, B_mskb], writes=[B_paux])
                                src = paux[:, 0:1024].rearrange("p (t c) -> p t c", c=128)
                                S.op("act", lambda a: a.copy(out=vs[:, 8 * j:8 * j + 8, :], in_=src),
                                     reads=[B_paux], writes=[BD])
                            stage_t.res = (("vT3",),)
                            own_q.append(stage_t)

            def queue_rope(kind, g, u):
                st = kS[u] if kind == "k" else qS[u]
                BD = B_kS[u] if kind == "k" else B_qS[u]
                if kind == "q":
                    chunks = [(0, 1024, 2048, 3072, (1, 4, 16)[g])]
                elif g == 0:
                    chunks = [(0, 1152, 1920, 3072, 1)]
                elif g == 1:
                    chunks = [(0, 1536, 1536, 3072, 4)]
                else:
                    chunks = [(0, 1024, 0, 2048, 16, 0), (1024, 2048, 0, 2048, 16, 1), (2048, 3072, 2048, 3072, 16)]
                for ch in chunks:
                    def stage_r(ch=ch):
                        c0, c1, t0, t1, r = ch[:5]
                        n = c1 - c0
                        flat = st[0:32, c0:c1]
                        dv = flat if r == 1 else flat.rearrange("p (r l) -> p r l", r=(8 if len(ch) == 6 else r))
                        rb = rotb[:, 0:n]
                        rv = rb if r == 1 else rb.rearrange("p (r l) -> p r l", r=(8 if len(ch) == 6 else r))

                        def tv(cs):
                            a_ = tab[:, cs, t0:t1]
                            if r > 1:
                                a_ = a_.rearrange("p (l r) -> p r l", r=r)
                            if len(ch) == 6:
                                a_ = a_[:, 8 * ch[5]:8 * ch[5] + 8, :]
                            return a_
                        S.op("sp", lambda s_: s_.dma_start(out=rotb[0:16, 0:n], in_=st[16:32, c0:c1]),
                             reads=[BD], writes=[B_rot[0]], dsem="drot0")
                        S.op("sp", lambda s_: s_.dma_start(out=rotb[16:32, 0:n], in_=st[0:16, c0:c1]),
                             reads=[BD], writes=[B_rot[1]], dsem="drot1")
                        S.op("dve", lambda v: v.tensor_tensor(out=dv, in0=dv, in1=tv(0), op=ALU.mult),
                             reads=[BD, B_tab], writes=[BD])
                        S.op("dve", lambda v: v.tensor_tensor(out=rv, in0=rv, in1=tv(1), op=ALU.mult),
                             reads=[B_rot[0], B_rot[1], B_tab], writes=[B_rot[0], B_rot[1]])
                        S.op("dve", lambda v: v.tensor_tensor(out=dv, in0=dv, in1=rv, op=ALU.add),
                             reads=[BD, B_rot[0], B_rot[1]], writes=[BD])
                    own_q.append(stage_r)

            MA = mskb[:, M_A:M_A + 512]
            MB = mskb[:, M_B:M_B + 512]
            MC = mskb[:, M_C:M_C + 512]

            def attention(g, u, first, unit_id):
                ks, qs, vs = kS[u], qS[u], vS[u]
                Bk, Bq, Bv = B_kS[u], B_qS[u], B_vS[u]
                batches = []
                if g == 0:
                    for bi in range(4):
                        sc, pv = [], []
                        for qq in range(2):
                            i = 2 * bi + qq + 1
                            qap = qs[:, 128 * (i - 1):128 * i]
                            e0 = qq * 256
                            sc.append((ks[:, 128 * (i - 1):128 * i], qap, e0, 128))
                            sc.append((ks[:, 128 * i:128 * (i + 1)], qap, e0 + 128, 128))
                            pv.append((qq * 128,
                                       [(vs[:, i - 1, :], e0, 0, 128), (vs[:, i, :], e0 + 128, 0, 128)]))
                        batches.append((sc, MA if bi == 0 else MB, pv))
                    numview = lambda t, p: t[:, 256 * p:256 * p + 256]
                    pview = lambda t: t
                elif g == 1:
                    ks3 = ks[:, 0:1536].rearrange("p (r l) -> p r l", r=4)
                    qs3 = qs[:, 0:1024].rearrange("p (r l) -> p r l", r=4)
                    vs4 = vs[:, 0:12, :].rearrange("p (r t) c -> p r t c", r=4)
                    for r in range(4):
                        sc, pv = [], []
                        for qq in range(2):
                            i = qq + 1
                            qap = qs3[:, r, 128 * (i - 1):128 * i]
                            e0 = qq * 256
                            sc.append((ks3[:, r, 128 * (i - 1):128 * i], qap, e0, 128))
                            sc.append((ks3[:, r, 128 * i:128 * (i + 1)], qap, e0 + 128, 128))
                            pv.append((qq * 128,
                                       [(vs4[:, r, i - 1, :], e0, 0, 128), (vs4[:, r, i, :], e0 + 128, 0, 128)]))
                        batches.append((sc, MA, pv))
                    numview = lambda t, p: t[:].rearrange("p (l r) -> p r l", r=4)[:, p, :]
                    pview = lambda t: t
                else:
                    ksh = ks[:, 0:2048].rearrange("p (r l) -> p r l", r=16)
                    kso = ks[:, 2048:3072]
                    qso = qs[:, 0:1024]
                    for bi in range(4):
                        sc, pv = [], []
                        for pq in range(2):
                            pi = 2 * bi + pq
                            e0 = pq * 256
                            r0, r1 = 2 * pi, 2 * pi + 1
                            sc.append((ksh[:, r0, :], qso[:, 64 * r0:64 * r0 + 64], e0, 64))
                            sc.append((ksh[:, r1, :], qso[:, 64 * r1:64 * r1 + 64], e0 + 64, 64))
                            sc.append((kso[:, 128 * pi:128 * pi + 128], qso[:, 128 * pi:128 * pi + 128], e0 + 128, 128))
                            pv.append((pq * 128,
                                       [(vs[:, 16 + pi, :], e0 + 128, 0, 128),
                                        (vs[:, r0, :], e0, 0, 64),
                                        (vs[:, r1, :], e0 + 64, 64, 64)]))
                        batches.append((sc, MC, pv))
                    numview = lambda t, p: t[:].rearrange("p (l r) -> p r l", r=16)[:, 4 * p:4 * p + 4, :]
                    pview = lambda t: t.rearrange("p (r l) -> p r l", r=4)

                state = {}

                def mk_s(bi):
                    sc, mask, pv = batches[bi]

                    def st():
                        si = nxt("psc", 2)
                        for (kap, qap, e0, nq) in sc:
                            S.op("pe", lambda t, kap=kap, qap=qap, e0=e0, nq=nq: t.matmul(
                                psc[si][:, e0:e0 + nq], lhsT=kap, rhs=qap, start=True, stop=True),
                                reads=[Bk, Bq], writes=[B_psc[si]])
                        ei = nxt("E", NE)
                        state[bi] = ei
                        S.op("act", lambda a: a.activation(out=Eb[ei][:], in_=psc[si][:], func=AF.Exp, scale=SQ128),
                             reads=[B_psc[si]], writes=[B_E[ei]])
                        S.op("dve", lambda v: v.tensor_tensor(out=Eb[ei][:], in0=Eb[ei][:], in1=mask, op=ALU.mult),
                             reads=[B_E[ei], B_mskb], writes=[B_E[ei]])
                    return st

                def mk_p(bi):
                    sc, mask, pv = batches[bi]

                    def st():
                        ei = state[bi]
                        pi_ = nxt("pnd", 2)
                        pnd, B_pnd = pnds[pi_], B_pnds[pi_]
                        for (c0, jobs) in pv:
                            for which in range(2):
                                nj = len(jobs)
                                for ji, (vap, e0, oc, on) in enumerate(jobs):
                                    lhs = vap if which == 0 else ones[:]
                                    cb = 256 * which + c0 + oc
                                    S.op("pe", lambda t, cb=cb, on=on, lhs=lhs, e0=e0, ji=ji, nj=nj:
                                         t.matmul(pnd[:, cb:cb + on], lhsT=lhs, rhs=Eb[ei][:, e0:e0 + on],
                                                  start=(ji == 0), stop=(ji == nj - 1)),
                                         reads=[B_E[ei], Bv, B_ones], writes=[B_pnd])
                        p = bi
                        if first:
                            S.op("act", lambda a: a.copy(out=numview(NUMs, p), in_=pview(pnd[:, 0:256])),
                                 reads=[B_pnd], writes=[B_NUM])
                            S.op("act", lambda a: a.copy(out=numview(DENs, p), in_=pview(pnd[:, 256:512])),
                                 reads=[B_pnd], writes=[B_DEN])
                        else:
                            S.op("dve", lambda v: v.tensor_tensor(out=numview(NUMs, p), in0=pview(pnd[:, 0:256]),
                                                                  in1=numview(NUMs, p), op=ALU.add),
                                 reads=[B_pnd, B_NUM], writes=[B_NUM])
                            S.op("dve", lambda v: v.tensor_tensor(out=numview(DENs, p), in0=pview(pnd[:, 256:512]),
                                                                  in1=numview(DENs, p), op=ALU.add),
                                 reads=[B_pnd, B_DEN], writes=[B_DEN])
                    return st

                def pre():
                    flush_own()
                def mk_nop():
                    def nop():
                        pass
                    return nop
                order = [mk_nop() for _ in range(6 if g == 2 else 7)] + [pre, mk_s(0), mk_s(1), mk_p(0), mk_s(2), mk_p(1), mk_s(3),
                                                        mk_p(2), mk_p(3)]
                for f in order:
                    f.unit = unit_id
                bg_q.extend(order)

            uctr = 0
            for h in range(NHEAD):
                for g in range(3):
                    u = uctr % NU
                    uctr += 1
                    first_proj = True
                    while bg_q and getattr(bg_q[0], "unit", 0) <= uctr - NU:
                        flush_own()
                        pop_bg()
                    for kind, cbase, gcol in ((("k", CK, C_KG), ("q", CQ, C_QG), ("v", CV, None)) if g == 2 else
                                              (("k", CK, C_KG), ("v", CV, None), ("q", CQ, C_QG))):
                        col0 = cbase + g * 1024 + h * 128
                        wt, Bw = load_wblock(col0)
                        for gd in groups_for(g, kind, u):
                            gd["B"] = {"k": B_kS[u], "v": B_vS[u], "q": B_qS[u]}[kind]
                            if first_proj and uctr > NU:
                                pass
                            i = project(pp, B_pp, ctr, wt, Bw, gd["mov"], gd["n"])
                            if first_proj:
                                first_proj = False
                            gd["u"] = u
                            if kind == "v":
                                post_v(i, gd, g)
                            else:
                                post_qk(i, gd, gcol)
                        if kind != "v":
                            queue_rope(kind, g, u)
                    attention(g, u, first=(g == 0), unit_id=uctr)
                wt, Bw = load_wblock(CZA + h * 128)
                for m in range(2):
                    i = project(pp, B_pp, ctr, wt, Bw, lambda kc, m=m: hTo[:, kc, 512 * m:512 * m + 512], 512)
                    zi = nxt("ez", 2)
                    need(("ez", zi))
                    S.op("act", lambda a, i=i, zi=zi: a.activation(out=ez[zi][:], in_=pp[i][:], func=AF.Exp, scale=-1.0),
                         reads=[B_pp[i]], writes=[B_ez[zi]])

                    def stage_z(i=i, zi=zi, m=m):
                        S.op("act", lambda a: a.activation(out=ez[zi][:], in_=ez[zi][:], func=AF.Ln, bias=1.0),
                             reads=[B_ez[zi]], writes=[B_ez[zi]])
                        S.op("act", lambda a: a.activation(out=ez[zi][:], in_=ez[zi][:], func=AF.Exp, scale=-1.0),
                             reads=[B_ez[zi]], writes=[B_ez[zi]])
                        S.op("dve", lambda v: v.tensor_tensor(out=szs[:, 512 * m:512 * m + 512], in0=pp[i][:],
                                                              in1=ez[zi][:], op=ALU.mult),
                             reads=[B_ez[zi], B_pp[i]], writes=[B_sz])
                    stage_z.res = (("pp", i), ("ez", zi))
                    own_q.append(stage_z)

                def fin(h=h):
                    flush_own()
                    S.op("act", lambda a: a.activation(out=DENs[:], in_=DENs[:], func=AF.Ln),
                         reads=[B_DEN], writes=[B_DEN])
                    S.op("act", lambda a: a.activation(out=DENs[:], in_=DENs[:], func=AF.Exp, scale=-1.0),
                         reads=[B_DEN], writes=[B_DEN])
                    S.op("dve", lambda v: v.tensor_tensor(out=NUMs[:], in0=NUMs[:], in1=DENs[:], op=ALU.mult),
                         reads=[B_NUM, B_DEN], writes=[B_NUM])
                    S.op("dve", lambda v: v.tensor_tensor(out=aT[:, h, :], in0=NUMs[:], in1=szs[:], op=ALU.mult),
                         reads=[B_NUM, B_sz], writes=[B_aT[h]])
                fin.unit = uctr
                bg_q.append(fin)
            for blk in range(8):
                wt, Bw = load_wblock(CU + blk * 128)
                i = project(pp, B_pp, ctr, wt, Bw, lambda kc: hTh[:, kc, THALO - 32:THALO], 32)
                S.op("act", lambda a, i=i, blk=blk: a.copy(out=uh[:, blk, :], in_=pp[i][:, 16:32]),
                     reads=[B_pp[i]], writes=[B_uh])
            flush_bg()
            S.emit_phase()
        eAB.close()

        with ExitStack() as ec:
            def sbc(name, shape, dt):
                return ec.enter_context(nc.sbuf_tensor("c_" + name, shape, dt))

            def psc_(name, shape, dt):
                return ec.enter_context(nc.psum_tensor("cp_" + name, shape, dt))

            UW = 16 + TOWN
            uext = [sbc(f"uext{i}", [128, UW], F32) for i in range(3)]
            B_ue = [Buf(f"ue{i}") for i in range(3)]
            dT = sbc("dT", [128, 8, TOWN], BF16)
            B_dT = [Buf(f"dT{i}") for i in range(8)]
            dfix = sbc("dfix", [128, 16], F32)
            B_dfix = Buf("dfix")
            pT = sbc("pT", [128, 8, TOWN], BF16)
            B_pT = [Buf(f"pT{i}") for i in range(8)]
            pm = sbc("pm", [128, 8, 256], BF16)
            B_pm = Buf("pm")
            mT = sbc("mT", [128, KC, TOWN], BF16)
            B_mT = [Buf(f"mT{i}") for i in range(KC)]
            tg = [sbc(f"tg{i}", [128, TOWN], F32) for i in range(3)]
            B_tg = [Buf(f"tg{i}") for i in range(3)]
            wb = [sbc(f"wb{i}", [128, 8, 128], BF16) for i in range(4)]
            B_wb = [Buf(f"wb{i}") for i in range(4)]
            wo = [sbc(f"wo{i}", [128, KC, 512], BF16) for i in range(2)]
            B_wo = [Buf("wo0"), Buf("wo1")]
            xr = [sbc(f"xr{i}", [128, 512], F32) for i in range(2)]
            B_xr = [Buf(f"xr{i}") for i in range(2)]
            ob = [sbc(f"ob{i}", [128, 512], F32) for i in range(2)]
            B_ob = [Buf(f"ob{i}") for i in range(2)]

            cpp = [psc_(f"cpp{i}", [128, 512], F32) for i in range(2)]
            B_cpp = [Buf("cpp0"), Buf("cpp1")]
            pya = [psc_(f"pya{i}", [128, 512], F32) for i in range(2)]
            B_pya = [Buf("pya0"), Buf("pya1")]
            pyp = [psc_(f"pyp{i}", [128, 512], F32) for i in range(2)]
            B_pyp = [Buf("pyp0"), Buf("pyp1")]
            pfo = [psc_(f"pfo{i}", [128, 512], F32) for i in range(2)]
            B_pfo = [Buf("pfo0"), Buf("pfo1")]

            cc = {"pp": 0, "wb": 0, "wo": 0, "xr": 0, "ob": 0, "pfo": 0, "pya": 0, "pyp": 0}

            def nx(name, n):
                i = cc[name] % n
                cc[name] += 1
                return i

            own_mov = lambda m: (lambda kc: hTo[:, kc, 512 * m:512 * m + 512])

            S.op("pool", lambda g: g.dma_start(out=pm[:], in_=pmaps.rearrange("g (k p) e -> p (g k) e", p=128)),
                 writes=[B_pm], dsem="dc4")

            for blk in range(8):
                gq = blk // 2
                ksz = 2 ** (gq + 1)
                wt, Bw = load_wblock(CU + blk * 128)
                S.op("act", lambda a, blk=blk: a.copy(out=uext[0][:, 0:16], in_=uh[:, blk, :]),
                     reads=[B_uh], writes=[B_ue[0]])
                for m in range(2):
                    i = project(cpp, B_cpp, cc, wt, Bw, own_mov(m), 512)
                    S.op("act", lambda a, i=i, m=m: a.copy(out=uext[0][:, 16 + 512 * m:16 + 512 * m + 512],
                                                           in_=cpp[i][:]),
                         reads=[B_cpp[i]], writes=[B_ue[0]])
                cur = 0
                sh = 1
                for step in range(gq + 1):
                    nxtb = 1 if cur != 1 else 2
                    lo = 2 * sh - 1
                    S.op("dve", lambda v, cur=cur, nxtb=nxtb, lo=lo, sh=sh: v.tensor_tensor(
                        out=uext[nxtb][:, lo:UW], in0=uext[cur][:, lo:UW], in1=uext[cur][:, lo - sh:UW - sh],
                        op=ALU.add),
                        reads=[B_ue[cur]], writes=[B_ue[nxtb]])
                    cur = nxtb
                    sh *= 2
                S.op("dve", lambda v, cur=cur, blk=blk, ksz=ksz: v.scalar_tensor_tensor(
                    out=dT[:, blk, :], in0=uext[cur][:, 16:UW], scalar=1.0 / ksz, in1=uext[0][:, 16:UW],
                    op0=ALU.mult, op1=ALU.subtract),
                    reads=[B_ue[cur], B_ue[0]], writes=[B_dT[blk]])
                S.op("dve", lambda v, cur=cur, gq=gq: v.tensor_tensor(
                    out=dfix[:], in0=uext[cur][:, 16:32], in1=cst[:, C_IC + 16 * gq:C_IC + 16 * gq + 16],
                    op=ALU.mult),
                    reads=[B_ue[cur], B_cst], writes=[B_dfix])
                S.op("dve", lambda v, blk=blk: v.tensor_tensor(
                    out=dT[:, blk, 0:16], in0=dfix[:], in1=uext[0][:, 16:32], op=ALU.subtract),
                    reads=[B_dfix, B_ue[0]], writes=[B_dT[blk]])
            for eb2 in range(8):
                gq, half = eb2 // 2, eb2 % 2
                wt, Bw = load_wblock(CZP + eb2 * 128)
                for m in range(2):
                    i = project(cpp, B_cpp, cc, wt, Bw, own_mov(m), 512)
                    fi = nx("pfo", 2)
                    for k2 in range(2):
                        S.op("pe", lambda t, fi=fi, gq=gq, k2=k2, half=half, m=m: t.matmul(
                            pfo[fi][:], lhsT=pm[:, 2 * gq + k2, 128 * half:128 * half + 128],
                            rhs=dT[:, 2 * gq + k2, 512 * m:512 * m + 512], start=(k2 == 0), stop=(k2 == 1)),
                            reads=[B_pm, B_dT[2 * gq + k2]], writes=[B_pfo[fi]])
                    S.op("act", lambda a, i=i, m=m: a.activation(out=tg[0][:, 512 * m:512 * m + 512], in_=cpp[i][:],
                                                                 func=AF.Tanh, scale=0.5),
                         reads=[B_cpp[i]], writes=[B_tg[0]])
                    S.op("dve", lambda v, i=i, m=m: v.scalar_tensor_tensor(
                        out=tg[0][:, 512 * m:512 * m + 512], in0=tg[0][:, 512 * m:512 * m + 512], scalar=1.0,
                        in1=cpp[i][:], op0=ALU.add, op1=ALU.mult),
                        reads=[B_tg[0], B_cpp[i]], writes=[B_tg[0]])
                    S.op("dve", lambda v, fi=fi, eb2=eb2, m=m: v.scalar_tensor_tensor(
                        out=pT[:, eb2, 512 * m:512 * m + 512], in0=pfo[fi][:], scalar=pshalf[:, eb2:eb2 + 1],
                        in1=tg[0][:, 512 * m:512 * m + 512], op0=ALU.mult, op1=ALU.mult),
                        reads=[B_pfo[fi], B_tg[0], B_misc], writes=[B_pT[eb2]])

            w_ba_v = w_ba.rearrange("(h p) n -> p h n", p=128)
            w_bp_v = w_bp.rearrange("(h p) n -> p h n", p=128)
            for db in range(KC):
                wa_i = nx("wb", 4)
                S.op("pool", lambda g, wa_i=wa_i, db=db: g.dma_start(out=wb[wa_i][:],
                                                                      in_=w_ba_v[:, :, 128 * db:128 * db + 128]),
                     writes=[B_wb[wa_i]], dsem=f"db{wa_i}")
                wp_i = nx("wb", 4)
                S.op("pool", lambda g, wp_i=wp_i, db=db: g.dma_start(out=wb[wp_i][:],
                                                                      in_=w_bp_v[:, :, 128 * db:128 * db + 128]),
                     writes=[B_wb[wp_i]], dsem=f"db{wp_i}")
                for which, cbase in ((0, CG), (1, CG + D)):
                    wt, Bw = load_wblock(cbase + db * 128)
                    tgi = 1 + which
                    for m in range(2):
                        i = project(cpp, B_cpp, cc, wt, Bw, own_mov(m), 512)
                        bcol = which * KC + db
                        S.op("act", lambda a, i=i, m=m, tgi=tgi, bcol=bcol: a.activation(
                            out=tg[tgi][:, 512 * m:512 * m + 512], in_=cpp[i][:], func=AF.Tanh,
                            bias=halfb[:, bcol:bcol + 1], scale=0.5),
                            reads=[B_cpp[i], B_misc], writes=[B_tg[tgi]])
                for m in range(2):
                    ya = nx("pya", 2)
                    for h in range(NHEAD):
                        S.op("pe", lambda t, ya=ya, h=h, m=m, wa_i=wa_i: t.matmul(
                            pya[ya][:], lhsT=wb[wa_i][:, h, :], rhs=aT[:, h, 512 * m:512 * m + 512],
                            start=(h == 0), stop=(h == NHEAD - 1)),
                            reads=[B_wb[wa_i], B_aT[h]], writes=[B_pya[ya]])
                    yp = nx("pyp", 2)
                    for e in range(8):
                        S.op("pe", lambda t, yp=yp, e=e, m=m, wp_i=wp_i: t.matmul(
                            pyp[yp][:], lhsT=wb[wp_i][:, e, :], rhs=pT[:, e, 512 * m:512 * m + 512],
                            start=(e == 0), stop=(e == 7)),
                            reads=[B_wb[wp_i], B_pT[e]], writes=[B_pyp[yp]])
                    S.op("dve", lambda v, ya=ya, m=m: v.scalar_tensor_tensor(
                        out=tg[1][:, 512 * m:512 * m + 512], in0=tg[1][:, 512 * m:512 * m + 512], scalar=1.0,
                        in1=pya[ya][:], op0=ALU.add, op1=ALU.mult),
                        reads=[B_tg[1], B_pya[ya]], writes=[B_tg[1]])
                    S.op("dve", lambda v, yp=yp, m=m: v.scalar_tensor_tensor(
                        out=tg[2][:, 512 * m:512 * m + 512], in0=tg[2][:, 512 * m:512 * m + 512], scalar=1.0,
                        in1=pyp[yp][:], op0=ALU.add, op1=ALU.mult),
                        reads=[B_tg[2], B_pyp[yp]], writes=[B_tg[2]])
                S.op("dve", lambda v, db=db: v.tensor_tensor(out=mT[:, db, :], in0=tg[1][:], in1=tg[2][:], op=ALU.add),
                     reads=[B_tg[1], B_tg[2]], writes=[B_mT[db]])

            w_out_v = w_out.rearrange("(k p) n -> p k n", p=128)
            for cg in range(4):
                wi = nx("wo", 2)
                S.op("pool", lambda g, wi=wi, cg=cg: g.dma_start(out=wo[wi][:],
                                                                  in_=w_out_v[:, :, 512 * cg:512 * cg + 512]),
                     writes=[B_wo[wi]], dsem=f"dwo{wi}")
                for tt in range(8):
                    xi = nx("xr", 2)
                    S.op("act", lambda a, xi=xi, tt=tt, cg=cg: a.dma_start(
                        out=xr[xi][:], in_=xh[THALO + 128 * tt:THALO + 128 * tt + 128, 512 * cg:512 * cg + 512]),
                        writes=[B_xr[xi]], dsem=f"dr{xi}")
                    fi = nx("pfo", 2)
                    for db in range(KC):
                        S.op("pe", lambda t, fi=fi, db=db, tt=tt, wi=wi: t.matmul(
                            pfo[fi][:], lhsT=mT[:, db, 128 * tt:128 * tt + 128], rhs=wo[wi][:, db, :],
                            start=(db == 0), stop=(db == KC - 1)),
                            reads=[B_mT[db], B_wo[wi]], writes=[B_pfo[fi]])
                    oi = nx("ob", 2)
                    S.op("dve", lambda v, fi=fi, xi=xi, oi=oi: v.scalar_tensor_tensor(
                        out=ob[oi][:], in0=pfo[fi][:], scalar=0.5, in1=xr[xi][:], op0=ALU.mult, op1=ALU.add),
                        reads=[B_pfo[fi], B_xr[xi]], writes=[B_ob[oi]])
                    S.op("sp", lambda s, oi=oi, tt=tt, cg=cg: s.dma_start(
                        out=out_d[128 * tt:128 * tt + 128, 512 * cg:512 * cg + 512], in_=ob[oi][:]),
                        reads=[B_ob[oi]], dsem=f"do{oi}")
            S.emit_phase()
    return nc


def _host_consts(c):
    i = np.arange(128)[:, None]
    m = np.arange(128)[None, :]
    cur = (i <= m).astype(np.float32)
    prev = (i >= m).astype(np.float32)
    halo = prev if c > 0 else np.zeros_like(prev)
    msk = np.zeros((128, NMSK), np.float32)
    msk[:, M_A:M_A + 512] = np.concatenate([halo, cur, prev, cur], axis=1)
    msk[:, M_B:M_B + 512] = np.concatenate([prev, cur, prev, cur], axis=1)
    i64 = np.arange(128)[:, None]
    m64 = np.arange(64)[None, :]
    if c == 0:
        halo3 = np.zeros((128, 64), np.float32)
    elif c == 1:
        halo3 = ((i64 >= m64) & (i64 >= 64)).astype(np.float32)
    else:
        halo3 = (i64 >= m64).astype(np.float32)
    bd = np.zeros((128, 128), np.float32)
    c64 = (np.arange(64)[:, None] <= np.arange(64)[None, :]).astype(np.float32)
    bd[0:64, 0:64] = c64
    bd[64:128, 64:128] = c64
    one = np.concatenate([halo3, halo3, bd], axis=1)
    msk[:, M_C:M_C + 512] = np.concatenate([one, one], axis=1)
    msk[:, M_ID:M_ID + 128] = np.eye(128, dtype=np.float32)
    rt = np.zeros((32, 32), np.float32)
    for cc in range(16):
        rt[cc + 16, cc] = -1.0
        rt[cc, cc + 16] = 1.0
    msk[0:32, M_RT:M_RT + 32] = rt
    start = 1024 * c
    pos = np.arange(start - THALO, start + TOWN).astype(np.float64)
    inv = 500000.0 ** (-np.arange(0, 32, 2, dtype=np.float64) / 32.0)
    ang = (pos[None, :].astype(np.float32) * inv[:, None].astype(np.float32)).astype(np.float64)
    tab = np.zeros((32, 2, NTOK), np.float32)
    tab[0:16, 0] = np.cos(ang)
    tab[16:32, 0] = np.cos(ang)
    tab[0:16, 1] = -np.sin(ang)
    tab[16:32, 1] = np.sin(ang)
    ic = np.zeros((4, 16), np.float32)
    for gq in range(4):
        k = 2 ** (gq + 1)
        if c == 0:
            ic[gq] = 1.0 / np.minimum(np.arange(16) + 1, k)
        else:
            ic[gq] = 1.0 / k
    return msk, tab, ic


_NC_CACHE = {}


def kernel(x, norm_gain, w_in, b_gates, q_norm_gain, k_norm_gain, pool_maps, pool_scale,
           w_branch_attn, w_branch_pool, w_out):
    x = np.asarray(x, np.float32)
    f = lambda a: np.ascontiguousarray(np.asarray(a, np.float32))
    w_in, w_ba, w_bp, w_o, pmaps = f(w_in), f(w_branch_attn), f(w_branch_pool), f(w_out), f(pool_maps)
    gain = np.ascontiguousarray(np.broadcast_to(f(norm_gain)[None, :], (128, D)))
    if "nc" not in _NC_CACHE:
        _NC_CACHE["nc"] = build_program()
    nc = _NC_CACHE["nc"]
    in_maps = []
    for core in range(8):
        b, c = core // 4, core % 4
        start = 1024 * c
        xh = np.zeros((NTOK, D), np.float32)
        lo = start - THALO
        src_lo = max(lo, 0)
        xh[src_lo - lo:] = x[b, src_lo:start + TOWN]
        msk, tab, ic = _host_consts(c)
        cst = np.zeros((128, NCST), np.float32)
        cst[:, C_QG] = f(q_norm_gain)
        cst[:, C_KG] = f(k_norm_gain)
        cst[:, C_PS:C_PS + 8] = f(pool_scale).reshape(8, 128).T
        cst[:, C_BG:C_BG + 32] = f(b_gates).reshape(32, 128).T
        cst[:, C_IC:C_IC + 64] = ic.reshape(1, 64)
        in_maps.append({"xh": xh, "w_in": w_in, "w_ba": w_ba, "w_bp": w_bp, "w_out": w_o, "pmaps": pmaps,
                        "gain": gain, "cst": cst, "msk": msk, "tab": tab})
    res = run_bass_kernel_spmd(nc, in_maps, core_ids=list(range(8)))
    out = np.zeros((2, SEQ, D), np.float32)
    for core in range(8):
        b, c = core // 4, core % 4
        out[b, 1024 * c:1024 * c + 1024] = res.results[core]["out"]
    return out
```

```python
import numpy as np
import concourse.bass as bass
import concourse.mybir as mybir
from concourse.bass_utils import run_bass_kernel_spmd

F32 = mybir.dt.float32
BF16 = mybir.dt.bfloat16
AF = mybir.ActivationFunctionType
ALU = mybir.AluOpType

D = 2048
KC = 16
TOWN = 1024
THALO = 2048
NTOK = TOWN + THALO
SEQ = 4096
EPS = 1e-6
NHEAD = 8
SQ128 = float(np.sqrt(128.0))

CQ, CK, CV, CZA, CU, CZP, CG = 0, 3072, 6144, 9216, 10240, 11264, 12288

C_QG, C_KG, C_PS, C_BG, C_IC = 0, 1, 2, 10, 42
NCST = 42 + 64
M_A, M_B, M_C, M_ID, M_RT = 0, 512, 1024, 1536, 1664
NMSK = 1696


class Buf:
    __slots__ = ("name", "w", "r")

    def __init__(self, name):
        self.name = name
        self.w = None
        self.r = {}


class Op:
    __slots__ = ("eng", "fn", "deps", "signal", "semval", "key", "seq", "isdma", "phase")


class Sched:
    ENGS = ("pe", "act", "dve", "pool", "sp")

    def __init__(self, nc, sems):
        self.nc = nc
        self.sems = sems
        self.streams = {e: [] for e in self.ENGS}
        self.seqctr = {k: 0 for k in sems}
        self.semcnt = {k: 0 for k in sems}
        self.waited = {e: {k: 0 for k in sems} for e in self.ENGS}
        self.lastop = {k: None for k in sems}
        self.phase = 0

    def op(self, eng, fn, reads=(), writes=(), dsem=None):
        o = Op()
        o.eng = eng
        o.fn = fn
        o.signal = False
        o.semval = None
        o.phase = self.phase
        o.isdma = dsem is not None
        o.key = dsem if dsem is not None else eng
        o.seq = self.seqctr[o.key]
        self.seqctr[o.key] += 1
        deps = {}

        def add(d):
            if d is None or d is o:
                return
            cur = deps.get(d.key)
            if cur is None or cur.seq < d.seq:
                deps[d.key] = d

        for b in reads:
            add(b.w)
        for b in writes:
            add(b.w)
            for d in b.r.values():
                add(d)
        for b in reads:
            b.r[o.key] = o
        for b in writes:
            b.w = o
            b.r = {}
        o.deps = deps
        self.streams[eng].append(o)
        self.lastop[o.key] = o
        return o

    def emit_phase(self):
        nc = self.nc
        streams = self.streams
        for e in self.ENGS:
            for o in streams[e]:
                if o.isdma:
                    o.signal = True
                for k, d in o.deps.items():
                    if d.phase != self.phase or (d.eng == "pe" and o.eng == "pe" and not d.isdma):
                        continue
                    d.signal = True
        for k, o in self.lastop.items():
            if o is not None:
                o.signal = True
        for e in self.ENGS:
            for o in streams[e]:
                if o.signal:
                    self.semcnt[o.key] += 16 if o.isdma else 1
                    o.semval = self.semcnt[o.key]
        final = dict(self.semcnt)
        sems = self.sems
        waited = self.waited

        def run(ename, eng):
            w = waited[ename]
            for o in streams[ename]:
                for k, d in o.deps.items():
                    if d.phase != self.phase or (d.eng == "pe" and ename == "pe" and not d.isdma):
                        continue
                    if w[k] < d.semval:
                        eng.wait_ge(sems[k], d.semval)
                        w[k] = d.semval
                ins = o.fn(eng)
                if o.signal:
                    ins.then_inc(sems[o.key], 16 if o.isdma else 1)
            for k, v in final.items():
                if w[k] < v:
                    eng.wait_ge(sems[k], v)
                    w[k] = v

        with nc.Block() as block:
            @block.tensor
            def _(t):
                run("pe", t)

            @block.scalar
            def _(a):
                run("act", a)

            @block.vector
            def _(v):
                run("dve", v)

            @block.gpsimd
            def _(g):
                run("pool", g)

            @block.sync
            def _(s):
                run("sp", s)
        self.streams = {e: [] for e in self.ENGS}
        self.lastop = {k: None for k in self.sems}
        self.phase += 1


def build_program(debug=False):
    nc = bass.Bass("TRN2", target_bir_lowering=False)
    xh = nc.dram_tensor("xh", [NTOK, D], F32, kind="ExternalInput").ap()
    w_in = nc.dram_tensor("w_in", [D, 16384], F32, kind="ExternalInput").ap()
    w_ba = nc.dram_tensor("w_ba", [1024, D], F32, kind="ExternalInput").ap()
    w_bp = nc.dram_tensor("w_bp", [1024, D], F32, kind="ExternalInput").ap()
    w_out = nc.dram_tensor("w_out", [D, D], F32, kind="ExternalInput").ap()
    pmaps = nc.dram_tensor("pmaps", [4, 256, 256], F32, kind="ExternalInput").ap()
    gain = nc.dram_tensor("gain", [128, D], F32, kind="ExternalInput").ap()
    cst_d = nc.dram_tensor("cst", [128, NCST], F32, kind="ExternalInput").ap()
    msk_d = nc.dram_tensor("msk", [128, NMSK], F32, kind="ExternalInput").ap()
    tab_d = nc.dram_tensor("tab", [32, 2, NTOK], F32, kind="ExternalInput").ap()
    out_d = nc.dram_tensor("out", [TOWN, D], F32, kind="ExternalOutput").ap()

    w_in_v = w_in.rearrange("(kc p) n -> p kc n", p=128)

    from contextlib import ExitStack
    es = ExitStack()
    with es:
        def sb(name, shape, dt):
            return es.enter_context(nc.sbuf_tensor("s_" + name, shape, dt))

        semnames = (["pe", "act", "dve", "pool", "sp"] + [f"dw{i}" for i in range(4)] + [f"dx{i}" for i in range(3)]
                    + [f"do{i}" for i in range(3)] + [f"dc{i}" for i in range(5)] + [f"db{i}" for i in range(4)]
                    + [f"dwo{i}" for i in range(2)] + [f"dr{i}" for i in range(3)] + ["drot0", "drot1", "drot2", "drot3"])
        sems = {k: es.enter_context(nc.semaphore("s_" + k)) for k in semnames}
        S = Sched(nc, sems)

        hTo = sb("hTo", [128, KC, TOWN], BF16)
        aT = sb("aT", [128, NHEAD, TOWN], BF16)
        cst = sb("cst", [128, NCST], F32)
        mskb = sb("mskb", [128, NMSK], BF16)
        NW = 3
        wring = [sb(f"wr{i}", [128, KC, 128], BF16) for i in range(NW)]
        ones = sb("ones", [128, 128], BF16)
        pshalf = sb("pshalf", [128, 8], F32)
        halfb = sb("halfb", [128, 32], F32)
        uh = sb("uh", [128, 8, 16], F32)
        B_hTh, B_hTo, B_cst, B_mskb, B_ones = Buf("hTh"), Buf("hTo"), Buf("cst"), Buf("mskb"), Buf("ones")
        B_aT = [Buf(f"aT{h}") for h in range(NHEAD)]
        B_wr = [Buf(f"wr{i}") for i in range(NW)]
        B_misc = Buf("misc")
        B_uh = Buf("uh")
        wctr = [0]

        def load_wblock(col0):
            i = wctr[0] % NW
            wctr[0] += 1
            t = wring[i]
            S.op("pool", lambda g, t=t, col0=col0: g.dma_start(out=t[:], in_=w_in_v[:, :, col0:col0 + 128]),
                 writes=[B_wr[i]], dsem=f"dw{i}")
            return t, B_wr[i]

        ident = mskb[:, M_ID:M_ID + 128]
        rt = mskb[0:32, M_RT:M_RT + 32]

        own_q, bg_q, late_q = [], [], []
        pcount = [0]

        def pop_late(force=False):
            if late_q and (force or late_q[0].due <= pcount[0]):
                late_q.pop(0)()
                return True
            return False

        def pop_own():
            if pop_late():
                return True
            if own_q:
                own_q.pop(0)()
                return True
            return False

        def pop_bg():
            if bg_q:
                bg_q.pop(0)()
                return True
            return False

        def flush_own():
            while own_q or late_q:
                if own_q:
                    own_q.pop(0)()
                else:
                    pop_late(force=True)

        def need(tag):
            while any(tag in getattr(f, "res", ()) for f in own_q):
                pop_own()

        def flush_bg():
            while bg_q:
                flush_own()
                pop_bg()
            flush_own()

        eAB = ExitStack()
        hTh = eAB.enter_context(nc.sbuf_tensor("s_hTh", [128, KC, THALO], BF16))

        def project(pp, B_pp, ctrd, wt, Bw, mov, n):
            i = ctrd["pp"] % len(pp)
            ctrd["pp"] += 1
            pcount[0] += 1
            need(("pp", i))
            while len(own_q) > 4:
                pop_own()
            for kc in range(KC):
                S.op("pe", lambda t, i=i, kc=kc, mov=mov, wt=wt, n=n: t.matmul(
                    pp[i][:, 0:n], lhsT=wt[:, kc, :], rhs=mov(kc), start=(kc == 0), stop=(kc == KC - 1)),
                    reads=[Bw, B_hTh, B_hTo], writes=[B_pp[i]])
                if kc in (3, 7, 11, 15):
                    pop_own()
                elif kc in (1, 5, 9, 13):
                    pop_bg()
            return i

        with ExitStack() as ea:
            def sba(name, shape, dt):
                return ea.enter_context(nc.sbuf_tensor("a_" + name, shape, dt))

            xt = [sba(f"xt{i}", [128, D], F32) for i in range(3)]
            xn = [sba(f"xn{i}", [128, D], BF16) for i in range(2)]
            junk = sba("junk", [128, D], BF16)
            gb = sba("gb", [128, D], F32)
            mskf = sba("mskf", [128, NMSK], F32)
            stat = sba("stat", [128, 3 * 24], F32)
            tpa = [ea.enter_context(nc.psum_tensor(f"tpa{i}", [128, 8, 128], BF16)) for i in range(2)]
            B_xt = [Buf("xt0"), Buf("xt1"), Buf("xt2")]
            B_xn = [Buf("xn0"), Buf("xn1")]
            B_junk, B_gb, B_mskf, B_stat = Buf("junk"), Buf("gb"), Buf("mskf"), Buf("stat")
            B_st = [Buf(f"st{j}") for j in range(NTOK // 128)]
            B_tpa = [Buf("tpa0"), Buf("tpa1")]

            S.op("sp", lambda s: s.dma_start(out=cst[:], in_=cst_d), writes=[B_cst], dsem="dc0")
            S.op("sp", lambda s: s.dma_start(out=mskf[:], in_=msk_d), writes=[B_mskf], dsem="dc1")
            S.op("sp", lambda s: s.dma_start(out=gb[:], in_=gain), writes=[B_gb], dsem="dc2")
            S.op("dve", lambda v: v.tensor_copy(out=mskb[:], in_=mskf[:]), reads=[B_mskf], writes=[B_mskb])
            S.op("dve", lambda v: v.memset(ones[:], 1.0), writes=[B_ones])
            S.op("dve", lambda v: v.memset(stat[:], 0.0), writes=[B_stat])
            S.op("dve", lambda v: v.tensor_scalar_mul(out=pshalf[:], in0=cst[:, C_PS:C_PS + 8], scalar1=0.5),
                 reads=[B_cst], writes=[B_misc])
            S.op("dve", lambda v: v.tensor_scalar_mul(out=halfb[:], in0=cst[:, C_BG:C_BG + 32], scalar1=0.5),
                 reads=[B_cst], writes=[B_misc])
            S.op("dve", lambda v: v.tensor_scalar_mul(out=gb[:], in0=gb[:], scalar1=float(np.sqrt(D))),
                 reads=[B_gb], writes=[B_gb])

            NTT = NTOK // 128

            def a_front(j):
                b, b3, c0 = j % 2, j % 3, 3 * j
                S.op("sp", lambda s: s.dma_start(out=xt[b3][:], in_=xh[j * 128:(j + 1) * 128, :]),
                     writes=[B_xt[b3]], dsem=f"dx{b3}")
                S.op("act", lambda a: a.activation(out=junk[:], in_=xt[b3][:], func=AF.Square,
                                                   accum_out=stat[:, c0:c0 + 1]),
                     reads=[B_xt[b3], B_stat], writes=[B_junk, B_st[j]])
                S.op("act", lambda a: a.activation(out=stat[:, c0 + 1:c0 + 2], in_=stat[:, c0:c0 + 1],
                                                   func=AF.Ln, bias=float(D * EPS)),
                     reads=[B_st[j]], writes=[B_st[j]])
                S.op("act", lambda a: a.activation(out=stat[:, c0 + 2:c0 + 3], in_=stat[:, c0 + 1:c0 + 2],
                                                   func=AF.Exp, scale=-0.5),
                     reads=[B_st[j]], writes=[B_st[j]])
                S.op("dve", lambda v: v.scalar_tensor_tensor(
                    out=xn[b][:], in0=xt[b3][:], scalar=stat[:, c0 + 2:c0 + 3], in1=gb[:],
                    op0=ALU.mult, op1=ALU.mult),
                    reads=[B_xt[b3], B_st[j], B_gb], writes=[B_xn[b]])

            def a_back(j):
                b = j % 2
                for half in range(2):
                    for k8 in range(8):
                        kc = half * 8 + k8
                        S.op("pe", lambda t, half=half, k8=k8, kc=kc: t.transpose(
                            tpa[half][:, k8, :], xn[b][:, kc * 128:(kc + 1) * 128], ident),
                            reads=[B_xn[b], B_mskb], writes=[B_tpa[half]])
                    if j < 16:
                        dst = hTh[:, half * 8:(half + 1) * 8, j * 128:(j + 1) * 128]
                        bd = B_hTh
                    else:
                        dst = hTo[:, half * 8:(half + 1) * 8, (j - 16) * 128:(j - 15) * 128]
                        bd = B_hTo
                    if half == 0:
                        S.op("act", lambda a, dst=dst, half=half: a.copy(out=dst, in_=tpa[half][:]),
                             reads=[B_tpa[half]], writes=[bd])
                    else:
                        S.op("dve", lambda v, dst=dst, half=half: v.tensor_copy(out=dst, in_=tpa[half][:]),
                             reads=[B_tpa[half]], writes=[bd])

            for j in range(NTT + 1):
                if j < NTT:
                    a_front(j)
                if j >= 1:
                    a_back(j - 1)
            S.emit_phase()

        with ExitStack() as eb:
            def sbb(name, shape, dt):
                return eb.enter_context(nc.sbuf_tensor("b_" + name, shape, dt))

            def psb(name, shape, dt):
                return eb.enter_context(nc.psum_tensor("bp_" + name, shape, dt))

            tab = sbb("tab", [32, 2, NTOK], BF16)
            B_tab = Buf("tab")
            NU = 2
            kS = [sbb(f"kS{i}", [128, 3072], BF16) for i in range(NU)]
            qS = [sbb(f"qS{i}", [128, 1024], BF16) for i in range(NU)]
            vS = [sbb(f"vS{i}", [128, 24, 128], BF16) for i in range(NU)]
            B_kS = [Buf(f"kS{i}") for i in range(NU)]
            B_qS = [Buf(f"qS{i}") for i in range(NU)]
            B_vS = [Buf(f"vS{i}") for i in range(NU)]
            vT = [sbb(f"vT{i}", [128, 512], BF16) for i in range(2)]
            B_vT = [Buf("vT0"), Buf("vT1")]
            vT3 = sbb("vT3", [128, 3072], BF16)
            B_vT3 = Buf("vT3")
            sq = [sbb(f"sq{i}", [128, 512], BF16) for i in range(2)]
            B_sq = [Buf("sq0"), Buf("sq1")]
            rr = [sbb(f"rr{i}", [128, 512], F32) for i in range(2)]
            B_rr = [Buf("rr0"), Buf("rr1")]
            rotbs = [sbb(f"rotb{i}", [32, 1536], BF16) for i in range(2)]
            B_rots = [[Buf(f"rot{i}a"), Buf(f"rot{i}b")] for i in range(2)]
            NE = 3
            Eb = [sbb(f"E{i}", [128, 512], BF16) for i in range(NE)]
            B_E = [Buf(f"E{i}") for i in range(NE)]
            NUMs = sbb("NUM", [128, TOWN], F32)
            DENs = sbb("DEN", [128, TOWN], F32)
            szs = sbb("sz", [128, TOWN], F32)
            ez = [sbb(f"ez{i}", [128, 512], F32) for i in range(2)]
            B_NUM, B_DEN, B_sz = Buf("NUM"), Buf("DEN"), Buf("sz")
            B_ez = [Buf("ez0"), Buf("ez1")]

            pp = [psb(f"pp{i}", [128, 512], F32) for i in range(3)]
            B_pp = [Buf("pp0"), Buf("pp1"), Buf("pp2")]
            prot = psb("prot", [128, 512], F32)
            B_prot = Buf("prot")
            pS = prot
            B_pS = B_prot
            paux = prot[:, :].bitcast(BF16)
            B_paux = B_prot
            psc = [psb(f"psc{i}", [128, 512], F32) for i in range(2)]
            B_psc = [Buf("psc0"), Buf("psc1")]
            pnds = [psb(f"pnd{i}", [128, 512], F32) for i in range(2)]
            B_pnds = [Buf("pnd0"), Buf("pnd1")]

            S.op("pool", lambda g: g.dma_start(out=tab[:], in_=tab_d), writes=[B_tab], dsem="dc3")

            ctr = {"pp": 0, "vT": 0, "sq": 0, "rr": 0, "rtmp": 0, "E": 0, "psc": 0, "ez": 0, "pnd": 0, "rot": 0}

            def nxt(name, n):
                i = ctr[name] % n
                ctr[name] += 1
                return i

            hTh_v16 = lambda kc: hTh[:, kc, :].rearrange("p (l r) -> p r l", r=16)
            hTo_v16 = lambda kc: hTo[:, kc, :].rearrange("p (l r) -> p r l", r=16)

            def tabv(cs, lo, hi, r):
                a = tab[:, cs, lo:hi]
                if r > 1:
                    a = a.rearrange("p (l r) -> p r l", r=r)
                return a

            def groups_for(g, kind, u):
                res = []
                ks, qs, vs = kS[u], qS[u], vS[u]
                if g == 0:
                    lst = []
                    if kind != "q":
                        lst.append(("h", 1920, 2048))
                    lst += [("o", 0, 512), ("o", 512, 1024)]
                    for (wh, lo, hi) in lst:
                        n = hi - lo
                        if wh == "h":
                            mov = lambda kc, lo=lo, hi=hi: hTh[:, kc, lo:hi]
                            tlo = lo
                            kd = ks[:, 0:128]
                            vt0 = 0
                        else:
                            mov = lambda kc, lo=lo, hi=hi: hTo[:, kc, lo:hi]
                            tlo = THALO + lo
                            kd = ks[:, 128 + lo:128 + hi]
                            vt0 = 1 + lo // 128
                        dst = kd if kind == "k" else (qs[:, lo:hi] if kind == "q" else None)
                        res.append(dict(mov=mov, n=n, r=1, dst=dst, cos=tabv(0, tlo, tlo + n, 1),
                                        sin=tabv(1, tlo, tlo + n, 1), vdst=vs[:, vt0:vt0 + n // 128, :]))
                elif g == 1:
                    ks3 = ks[:, 0:1536].rearrange("p (r l) -> p r l", r=4)
                    qs3 = qs[:, 0:1024].rearrange("p (r l) -> p r l", r=4)
                    vs4 = vs[:, 0:12, :].rearrange("p (r t) c -> p r t c", r=4)
                    lst = []
                    if kind != "q":
                        lst.append(("h", 1536, 2048, 0))
                    lst += [("o", 0, 512, 1), ("o", 512, 1024, 2)]
                    for (wh, lo, hi, t) in lst:
                        if wh == "h":
                            mov = lambda kc, lo=lo, hi=hi: hTh[:, kc, lo:hi]
                            tlo = lo
                        else:
                            mov = lambda kc, lo=lo, hi=hi: hTo[:, kc, lo:hi]
                            tlo = THALO + lo
                        dst = ks3[:, :, t * 128:(t + 1) * 128] if kind == "k" else (
                            qs3[:, :, (t - 1) * 128:t * 128] if kind == "q" else None)
                        res.append(dict(mov=mov, n=512, r=4, dst=dst, cos=tabv(0, tlo, tlo + 512, 4),
                                        sin=tabv(1, tlo, tlo + 512, 4), vdst=vs4[:, :, t, :]))
                else:
                    ksh = ks[:, 0:2048].rearrange("p (r l) -> p r l", r=16)
                    kso = ks[:, 2048:3072].rearrange("p (r l) -> p r l", r=16)
                    qs3 = qs[:, 0:1024].rearrange("p (r l) -> p r l", r=16)
                    v3h = vT3[:, 0:2048].rearrange("p (r l) -> p r l", r=16)
                    v3o = vT3[:, 2048:3072].rearrange("p (r l) -> p r l", r=16)
                    if kind != "q":
                        for m in range(4):
                            mov = lambda kc, m=m: hTh[:, kc, 512 * m:512 * m + 512]
                            res.append(dict(mov=mov, n=512, r=16, dst=ksh[:, :, 32 * m:32 * m + 32],
                                            cos=tabv(0, 512 * m, 512 * m + 512, 16),
                                            sin=tabv(1, 512 * m, 512 * m + 512, 16),
                                            v3dst=v3h[:, :, 32 * m:32 * m + 32], last=False))
                    for m in range(2):
                        mov = lambda kc, m=m: hTo[:, kc, 512 * m:512 * m + 512]
                        dst = kso[:, :, 32 * m:32 * m + 32] if kind == "k" else qs3[:, :, 32 * m:32 * m + 32]
                        tlo = THALO + 512 * m
                        res.append(dict(mov=mov, n=512, r=16, dst=dst,
                                        cos=tabv(0, tlo, tlo + 512, 16), sin=tabv(1, tlo, tlo + 512, 16),
                                        v3dst=v3o[:, :, 32 * m:32 * m + 32], last=(m == 1)))
                return res

            def viewd(ap2, r):
                return ap2 if r == 1 else ap2.rearrange("p (l r) -> p r l", r=r)

            def view3(ap2, r):
                return ap2 if r == 1 else ap2.rearrange("p (r l) -> p r l", r=r)

            def post_qk(i, gd, gcol):
                n, r = gd["n"], gd["r"]
                P = pp[i][:, 0:n]
                BD = gd["B"]
                dst = gd["dst"]
                dst32 = dst[0:32]
                si = nxt("sq", 2)
                need(("sq", si))
                S.op("act", lambda a: a.activation(out=sq[si][:, 0:n], in_=P, func=AF.Square),
                     reads=[B_pp[i]], writes=[B_sq[si]])

                def stage_b():
                    S.op("pe", lambda t: t.matmul(pS[:, 0:n], lhsT=ones[:], rhs=sq[si][:, 0:n], start=True, stop=True),
                         reads=[B_sq[si], B_ones], writes=[B_pS])
                    ri = nxt("rr", 2)
                    S.op("act", lambda a: a.activation(out=rr[ri][:, 0:n], in_=pS[:, 0:n], func=AF.Ln,
                                                       bias=float(128 * EPS)),
                         reads=[B_pS], writes=[B_rr[ri]])
                    S.op("act", lambda a: a.activation(out=rr[ri][:, 0:n], in_=rr[ri][:, 0:n], func=AF.Exp, scale=-0.5),
                         reads=[B_rr[ri]], writes=[B_rr[ri]])
                    S.op("dve", lambda v: v.scalar_tensor_tensor(
                        out=dst, in0=viewd(P, r), scalar=cst[:, gcol:gcol + 1], in1=viewd(rr[ri][:, 0:n], r),
                        op0=ALU.mult, op1=ALU.mult),
                        reads=[B_pp[i], B_rr[ri], B_cst], writes=[BD])

                stage_b.res = (("pp", i), ("sq", si))
                own_q.append(stage_b)

            def post_v(i, gd, g):
                n, r = gd["n"], gd["r"]
                BD = gd["B"]
                if g < 2:
                    vi = nxt("vT", 2)
                    need(("vT", vi))
                    S.op("act", lambda a: a.copy(out=view3(vT[vi][:, 0:n], r), in_=viewd(pp[i][:, 0:n], r)),
                         reads=[B_pp[i]], writes=[B_vT[vi]])
                    nch = n // 128
                    vdst = gd["vdst"]

                    def stage_b():
                        for c in range(nch):
                            S.op("pe", lambda t, c=c: t.transpose(paux[:, c * 128:(c + 1) * 128],
                                                                  vT[vi][:, c * 128:(c + 1) * 128], ident),
                                 reads=[B_vT[vi], B_mskb], writes=[B_paux])
                        src = paux[:, 0:n].rearrange("p (t c) -> p t c", c=128)
                        S.op("act", lambda a: a.copy(out=vdst, in_=src), reads=[B_paux], writes=[BD])

                    stage_b.res = (("vT", vi),)
                    own_q.append(stage_b)
                else:
                    v3dst = gd["v3dst"]
                    need(("vT3",))
                    S.op("act", lambda a: a.copy(out=v3dst, in_=viewd(pp[i][:, 0:n], r)),
                         reads=[B_pp[i]], writes=[B_vT3])
                    if gd["last"]:
                        vs = vS[gd["u"]]
                        for j in range(3):
                            def stage_t(j=j):
                                for c in range(8):
                                    S.op("pe", lambda t, c=c: t.transpose(
                                        paux[:, c * 128:(c + 1) * 128],
                                        vT3[:, (8 * j + c) * 128:(8 * j + c + 1) * 128], ident),
                                        reads=[B_vT3, B_mskb], writes=[B_paux])
                                src = paux[:, 0:1024].rearrange("p (t c) -> p t c", c=128)
                                S.op("act", lambda a: a.copy(out=vs[:, 8 * j:8 * j + 8, :], in_=src),
                                     reads=[B_paux], writes=[BD])
                            stage_t.res = (("vT3",),)
                            own_q.append(stage_t)

            def queue_rope(kind, g, u):
                st = kS[u] if kind == "k" else qS[u]
                BD = B_kS[u] if kind == "k" else B_qS[u]
                if kind == "q":
                    chunks = [(0, 1024, 2048, 3072, (1, 4, 16)[g])]
                elif g == 0:
                    chunks = [(0, 1152, 1920, 3072, 1)]
                elif g == 1:
                    chunks = [(0, 1536, 1536, 3072, 4)]
                else:
                    chunks = [(0, 1024, 0, 2048, 16, 0), (1024, 2048, 0, 2048, 16, 1), (2048, 3072, 2048, 3072, 16)]
                for ch in chunks:
                    def stage_r(ch=ch):
                        c0, c1, t0, t1, r = ch[:5]
                        n = c1 - c0
                        ri = ctr["rot"] % 2
                        while any(getattr(f, "rot", None) == ri for f in late_q):
                            pop_late(force=True)
                        ctr["rot"] += 1
                        rotb_ = rotbs[ri]
                        Br = B_rots[ri]
                        flat = st[0:32, c0:c1]
                        dv = flat if r == 1 else flat.rearrange("p (r l) -> p r l", r=(8 if len(ch) == 6 else r))
                        rb = rotb_[:, 0:n]
                        rv = rb if r == 1 else rb.rearrange("p (r l) -> p r l", r=(8 if len(ch) == 6 else r))

                        def tv(cs):
                            a_ = tab[:, cs, t0:t1]
                            if r > 1:
                                a_ = a_.rearrange("p (l r) -> p r l", r=r)
                            if len(ch) == 6:
                                a_ = a_[:, 8 * ch[5]:8 * ch[5] + 8, :]
                            return a_
                        S.op("sp", lambda s_: s_.dma_start(out=rotb_[0:16, 0:n], in_=st[16:32, c0:c1]),
                             reads=[BD], writes=[Br[0]], dsem=f"drot{2 * ri}")
                        S.op("sp", lambda s_: s_.dma_start(out=rotb_[16:32, 0:n], in_=st[0:16, c0:c1]),
                             reads=[BD], writes=[Br[1]], dsem=f"drot{2 * ri + 1}")

                        def stage_r2():
                            S.op("dve", lambda v: v.tensor_tensor(out=dv, in0=dv, in1=tv(0), op=ALU.mult),
                                 reads=[BD, B_tab, Br[0], Br[1]], writes=[BD])
                            S.op("dve", lambda v: v.tensor_tensor(out=rv, in0=rv, in1=tv(1), op=ALU.mult),
                                 reads=[Br[0], Br[1], B_tab], writes=[Br[0], Br[1]])
                            S.op("dve", lambda v: v.tensor_tensor(out=dv, in0=dv, in1=rv, op=ALU.add),
                                 reads=[BD, Br[0], Br[1]], writes=[BD])
                        stage_r2.due = pcount[0] + 3
                        stage_r2.rot = ri
                        late_q.append(stage_r2)
                    own_q.append(stage_r)

            MA = mskb[:, M_A:M_A + 512]
            MB = mskb[:, M_B:M_B + 512]
            MC = mskb[:, M_C:M_C + 512]

            def attention(g, u, first, unit_id):
                ks, qs, vs = kS[u], qS[u], vS[u]
                Bk, Bq, Bv = B_kS[u], B_qS[u], B_vS[u]
                batches = []
                if g == 0:
                    for bi in range(4):
                        sc, pv = [], []
                        for qq in range(2):
                            i = 2 * bi + qq + 1
                            qap = qs[:, 128 * (i - 1):128 * i]
                            e0 = qq * 256
                            sc.append((ks[:, 128 * (i - 1):128 * i], qap, e0, 128))
                            sc.append((ks[:, 128 * i:128 * (i + 1)], qap, e0 + 128, 128))
                            pv.append((qq * 128,
                                       [(vs[:, i - 1, :], e0, 0, 128), (vs[:, i, :], e0 + 128, 0, 128)]))
                        batches.append((sc, MA if bi == 0 else MB, pv))
                    numview = lambda t, p: t[:, 256 * p:256 * p + 256]
                    pview = lambda t: t
                elif g == 1:
                    ks3 = ks[:, 0:1536].rearrange("p (r l) -> p r l", r=4)
                    qs3 = qs[:, 0:1024].rearrange("p (r l) -> p r l", r=4)
                    vs4 = vs[:, 0:12, :].rearrange("p (r t) c -> p r t c", r=4)
                    for r in range(4):
                        sc, pv = [], []
                        for qq in range(2):
                            i = qq + 1
                            qap = qs3[:, r, 128 * (i - 1):128 * i]
                            e0 = qq * 256
                            sc.append((ks3[:, r, 128 * (i - 1):128 * i], qap, e0, 128))
                            sc.append((ks3[:, r, 128 * i:128 * (i + 1)], qap, e0 + 128, 128))
                            pv.append((qq * 128,
                                       [(vs4[:, r, i - 1, :], e0, 0, 128), (vs4[:, r, i, :], e0 + 128, 0, 128)]))
                        batches.append((sc, MA, pv))
                    numview = lambda t, p: t[:].rearrange("p (l r) -> p r l", r=4)[:, p, :]
                    pview = lambda t: t
                else:
                    ksh = ks[:, 0:2048].rearrange("p (r l) -> p r l", r=16)
                    kso = ks[:, 2048:3072]
                    qso = qs[:, 0:1024]
                    for bi in range(4):
                        sc, pv = [], []
                        for pq in range(2):
                            pi = 2 * bi + pq
                            e0 = pq * 256
                            r0, r1 = 2 * pi, 2 * pi + 1
                            sc.append((ksh[:, r0, :], qso[:, 64 * r0:64 * r0 + 64], e0, 64))
                            sc.append((ksh[:, r1, :], qso[:, 64 * r1:64 * r1 + 64], e0 + 64, 64))
                            sc.append((kso[:, 128 * pi:128 * pi + 128], qso[:, 128 * pi:128 * pi + 128], e0 + 128, 128))
                            pv.append((pq * 128,
                                       [(vs[:, 16 + pi, :], e0 + 128, 0, 128),
                                        (vs[:, r0, :], e0, 0, 64),
                                        (vs[:, r1, :], e0 + 64, 64, 64)]))
                        batches.append((sc, MC, pv))
                    numview = lambda t, p: t[:].rearrange("p (l r) -> p r l", r=16)[:, 4 * p:4 * p + 4, :]
                    pview = lambda t: t.rearrange("p (r l) -> p r l", r=4)

                state = {}

                def mk_s(bi):
                    sc, mask, pv = batches[bi]

                    def st():
                        si = nxt("psc", 2)
                        for (kap, qap, e0, nq) in sc:
                            S.op("pe", lambda t, kap=kap, qap=qap, e0=e0, nq=nq: t.matmul(
                                psc[si][:, e0:e0 + nq], lhsT=kap, rhs=qap, start=True, stop=True),
                                reads=[Bk, Bq], writes=[B_psc[si]])
                        ei = nxt("E", NE)
                        state[bi] = ei
                        S.op("act", lambda a: a.activation(out=Eb[ei][:], in_=psc[si][:], func=AF.Exp, scale=SQ128),
                             reads=[B_psc[si]], writes=[B_E[ei]])
                        S.op("dve", lambda v: v.tensor_tensor(out=Eb[ei][:], in0=Eb[ei][:], in1=mask, op=ALU.mult),
                             reads=[B_E[ei], B_mskb], writes=[B_E[ei]])
                    return st

                def mk_p(bi):
                    sc, mask, pv = batches[bi]

                    def st():
                        ei = state[bi]
                        pi_ = nxt("pnd", 2)
                        pnd, B_pnd = pnds[pi_], B_pnds[pi_]
                        for (c0, jobs) in pv:
                            for which in range(2):
                                nj = len(jobs)
                                for ji, (vap, e0, oc, on) in enumerate(jobs):
                                    lhs = vap if which == 0 else ones[:]
                                    cb = 256 * which + c0 + oc
                                    S.op("pe", lambda t, cb=cb, on=on, lhs=lhs, e0=e0, ji=ji, nj=nj:
                                         t.matmul(pnd[:, cb:cb + on], lhsT=lhs, rhs=Eb[ei][:, e0:e0 + on],
                                                  start=(ji == 0), stop=(ji == nj - 1)),
                                         reads=[B_E[ei], Bv, B_ones], writes=[B_pnd])
                        p = bi
                        if first:
                            S.op("act", lambda a: a.copy(out=numview(NUMs, p), in_=pview(pnd[:, 0:256])),
                                 reads=[B_pnd], writes=[B_NUM])
                            S.op("act", lambda a: a.copy(out=numview(DENs, p), in_=pview(pnd[:, 256:512])),
                                 reads=[B_pnd], writes=[B_DEN])
                        else:
                            S.op("dve", lambda v: v.tensor_tensor(out=numview(NUMs, p), in0=pview(pnd[:, 0:256]),
                                                                  in1=numview(NUMs, p), op=ALU.add),
                                 reads=[B_pnd, B_NUM], writes=[B_NUM])
                            S.op("dve", lambda v: v.tensor_tensor(out=numview(DENs, p), in0=pview(pnd[:, 256:512]),
                                                                  in1=numview(DENs, p), op=ALU.add),
                                 reads=[B_pnd, B_DEN], writes=[B_DEN])
                    return st

                def pre():
                    flush_own()
                def mk_nop():
                    def nop():
                        pass
                    return nop
                order = [mk_nop() for _ in range(10 if g == 2 else 7)] + [pre, mk_s(0), mk_s(1), mk_p(0), mk_s(2), mk_p(1), mk_s(3),
                                                        mk_p(2), mk_p(3)]
                for f in order:
                    f.unit = unit_id
                bg_q.extend(order)

            uctr = 0
            for h in range(NHEAD):
                for g in range(3):
                    u = uctr % NU
                    uctr += 1
                    first_proj = True
                    while bg_q and getattr(bg_q[0], "unit", 0) <= uctr - NU:
                        flush_own()
                        pop_bg()
                    for kind, cbase, gcol in (("k", CK, C_KG), ("v", CV, None), ("q", CQ, C_QG)):
                        col0 = cbase + g * 1024 + h * 128
                        wt, Bw = load_wblock(col0)
                        for gd in groups_for(g, kind, u):
                            gd["B"] = {"k": B_kS[u], "v": B_vS[u], "q": B_qS[u]}[kind]
                            if first_proj and uctr > NU:
                                pass
                            i = project(pp, B_pp, ctr, wt, Bw, gd["mov"], gd["n"])
                            if first_proj:
                                first_proj = False
                            gd["u"] = u
                            if kind == "v":
                                post_v(i, gd, g)
                            else:
                                post_qk(i, gd, gcol)
                        if kind != "v":
                            queue_rope(kind, g, u)
                    attention(g, u, first=(g == 0), unit_id=uctr)
                wt, Bw = load_wblock(CZA + h * 128)
                for m in range(2):
                    i = project(pp, B_pp, ctr, wt, Bw, lambda kc, m=m: hTo[:, kc, 512 * m:512 * m + 512], 512)
                    zi = nxt("ez", 2)
                    need(("ez", zi))
                    S.op("act", lambda a, i=i, zi=zi: a.activation(out=ez[zi][:], in_=pp[i][:], func=AF.Exp, scale=-1.0),
                         reads=[B_pp[i]], writes=[B_ez[zi]])

                    def stage_z(i=i, zi=zi, m=m):
                        S.op("act", lambda a: a.activation(out=ez[zi][:], in_=ez[zi][:], func=AF.Ln, bias=1.0),
                             reads=[B_ez[zi]], writes=[B_ez[zi]])
                        S.op("act", lambda a: a.activation(out=ez[zi][:], in_=ez[zi][:], func=AF.Exp, scale=-1.0),
                             reads=[B_ez[zi]], writes=[B_ez[zi]])
                        S.op("dve", lambda v: v.tensor_tensor(out=szs[:, 512 * m:512 * m + 512], in0=pp[i][:],
                                                              in1=ez[zi][:], op=ALU.mult),
                             reads=[B_ez[zi], B_pp[i]], writes=[B_sz])
                    stage_z.res = (("pp", i), ("ez", zi))
                    own_q.append(stage_z)

                def fin(h=h):
                    flush_own()
                    S.op("act", lambda a: a.activation(out=DENs[:], in_=DENs[:], func=AF.Ln),
                         reads=[B_DEN], writes=[B_DEN])
                    S.op("act", lambda a: a.activation(out=DENs[:], in_=DENs[:], func=AF.Exp, scale=-1.0),
                         reads=[B_DEN], writes=[B_DEN])
                    S.op("dve", lambda v: v.tensor_tensor(out=NUMs[:], in0=NUMs[:], in1=DENs[:], op=ALU.mult),
                         reads=[B_NUM, B_DEN], writes=[B_NUM])
                    S.op("dve", lambda v: v.tensor_tensor(out=aT[:, h, :], in0=NUMs[:], in1=szs[:], op=ALU.mult),
                         reads=[B_NUM, B_sz], writes=[B_aT[h]])
                fin.unit = uctr
                bg_q.append(fin)
            for blk in range(8):
                wt, Bw = load_wblock(CU + blk * 128)
                i = project(pp, B_pp, ctr, wt, Bw, lambda kc: hTh[:, kc, THALO - 32:THALO], 32)
                S.op("act", lambda a, i=i, blk=blk: a.copy(out=uh[:, blk, :], in_=pp[i][:, 16:32]),
                     reads=[B_pp[i]], writes=[B_uh])
            flush_bg()
            S.emit_phase()
        eAB.close()

        with ExitStack() as ec:
            def sbc(name, shape, dt):
                return ec.enter_context(nc.sbuf_tensor("c_" + name, shape, dt))

            def psc_(name, shape, dt):
                return ec.enter_context(nc.psum_tensor("cp_" + name, shape, dt))

            UW = 16 + TOWN
            uext = [sbc(f"uext{i}", [128, UW], F32) for i in range(3)]
            B_ue = [Buf(f"ue{i}") for i in range(3)]
            dT = sbc("dT", [128, 8, TOWN], BF16)
            B_dT = [Buf(f"dT{i}") for i in range(8)]
            dfix = sbc("dfix", [128, 16], F32)
            B_dfix = Buf("dfix")
            pT = sbc("pT", [128, 8, TOWN], BF16)
            B_pT = [Buf(f"pT{i}") for i in range(8)]
            pm = sbc("pm", [128, 8, 256], BF16)
            B_pm = Buf("pm")
            mT = sbc("mT", [128, KC, TOWN], BF16)
            B_mT = [Buf(f"mT{i}") for i in range(KC)]
            tg = [sbc(f"tg{i}", [128, TOWN], F32) for i in range(3)]
            B_tg = [Buf(f"tg{i}") for i in range(3)]
            wb = [sbc(f"wb{i}", [128, 8, 128], BF16) for i in range(4)]
            B_wb = [Buf(f"wb{i}") for i in range(4)]
            wo = [sbc(f"wo{i}", [128, KC, 512], BF16) for i in range(2)]
            B_wo = [Buf("wo0"), Buf("wo1")]
            xr = [sbc(f"xr{i}", [128, 512], F32) for i in range(2)]
            B_xr = [Buf(f"xr{i}") for i in range(2)]
            ob = [sbc(f"ob{i}", [128, 512], F32) for i in range(2)]
            B_ob = [Buf(f"ob{i}") for i in range(2)]

            cpp = [psc_(f"cpp{i}", [128, 512], F32) for i in range(2)]
            B_cpp = [Buf("cpp0"), Buf("cpp1")]
            pya = [psc_(f"pya{i}", [128, 512], F32) for i in range(2)]
            B_pya = [Buf("pya0"), Buf("pya1")]
            pyp = [psc_(f"pyp{i}", [128, 512], F32) for i in range(2)]
            B_pyp = [Buf("pyp0"), Buf("pyp1")]
            pfo = [psc_(f"pfo{i}", [128, 512], F32) for i in range(2)]
            B_pfo = [Buf("pfo0"), Buf("pfo1")]

            cc = {"pp": 0, "wb": 0, "wo": 0, "xr": 0, "ob": 0, "pfo": 0, "pya": 0, "pyp": 0}

            def nx(name, n):
                i = cc[name] % n
                cc[name] += 1
                return i

            own_mov = lambda m: (lambda kc: hTo[:, kc, 512 * m:512 * m + 512])

            S.op("pool", lambda g: g.dma_start(out=pm[:], in_=pmaps.rearrange("g (k p) e -> p (g k) e", p=128)),
                 writes=[B_pm], dsem="dc4")

            for blk in range(8):
                gq = blk // 2
                ksz = 2 ** (gq + 1)
                wt, Bw = load_wblock(CU + blk * 128)
                S.op("act", lambda a, blk=blk: a.copy(out=uext[0][:, 0:16], in_=uh[:, blk, :]),
                     reads=[B_uh], writes=[B_ue[0]])
                for m in range(2):
                    i = project(cpp, B_cpp, cc, wt, Bw, own_mov(m), 512)
                    S.op("act", lambda a, i=i, m=m: a.copy(out=uext[0][:, 16 + 512 * m:16 + 512 * m + 512],
                                                           in_=cpp[i][:]),
                         reads=[B_cpp[i]], writes=[B_ue[0]])
                cur = 0
                sh = 1
                for step in range(gq + 1):
                    nxtb = 1 if cur != 1 else 2
                    lo = 2 * sh - 1
                    S.op("dve", lambda v, cur=cur, nxtb=nxtb, lo=lo, sh=sh: v.tensor_tensor(
                        out=uext[nxtb][:, lo:UW], in0=uext[cur][:, lo:UW], in1=uext[cur][:, lo - sh:UW - sh],
                        op=ALU.add),
                        reads=[B_ue[cur]], writes=[B_ue[nxtb]])
                    cur = nxtb
                    sh *= 2
                S.op("dve", lambda v, cur=cur, blk=blk, ksz=ksz: v.scalar_tensor_tensor(
                    out=dT[:, blk, :], in0=uext[cur][:, 16:UW], scalar=1.0 / ksz, in1=uext[0][:, 16:UW],
                    op0=ALU.mult, op1=ALU.subtract),
                    reads=[B_ue[cur], B_ue[0]], writes=[B_dT[blk]])
                S.op("dve", lambda v, cur=cur, gq=gq: v.tensor_tensor(
                    out=dfix[:], in0=uext[cur][:, 16:32], in1=cst[:, C_IC + 16 * gq:C_IC + 16 * gq + 16],
                    op=ALU.mult),
                    reads=[B_ue[cur], B_cst], writes=[B_dfix])
                S.op("dve", lambda v, blk=blk: v.tensor_tensor(
                    out=dT[:, blk, 0:16], in0=dfix[:], in1=uext[0][:, 16:32], op=ALU.subtract),
                    reads=[B_dfix, B_ue[0]], writes=[B_dT[blk]])
            for eb2 in range(8):
                gq, half = eb2 // 2, eb2 % 2
                wt, Bw = load_wblock(CZP + eb2 * 128)
                for m in range(2):
                    i = project(cpp, B_cpp, cc, wt, Bw, own_mov(m), 512)
                    fi = nx("pfo", 2)
                    for k2 in range(2):
                        S.op("pe", lambda t, fi=fi, gq=gq, k2=k2, half=half, m=m: t.matmul(
                            pfo[fi][:], lhsT=pm[:, 2 * gq + k2, 128 * half:128 * half + 128],
                            rhs=dT[:, 2 * gq + k2, 512 * m:512 * m + 512], start=(k2 == 0), stop=(k2 == 1)),
                            reads=[B_pm, B_dT[2 * gq + k2]], writes=[B_pfo[fi]])
                    S.op("act", lambda a, i=i, m=m: a.activation(out=tg[0][:, 512 * m:512 * m + 512], in_=cpp[i][:],
                                                                 func=AF.Tanh, scale=0.5),
                         reads=[B_cpp[i]], writes=[B_tg[0]])
                    S.op("dve", lambda v, i=i, m=m: v.scalar_tensor_tensor(
                        out=tg[0][:, 512 * m:512 * m + 512], in0=tg[0][:, 512 * m:512 * m + 512], scalar=1.0,
                        in1=cpp[i][:], op0=ALU.add, op1=ALU.mult),
                        reads=[B_tg[0], B_cpp[i]], writes=[B_tg[0]])
                    S.op("dve", lambda v, fi=fi, eb2=eb2, m=m: v.scalar_tensor_tensor(
                        out=pT[:, eb2, 512 * m:512 * m + 512], in0=pfo[fi][:], scalar=pshalf[:, eb2:eb2 + 1],
                        in1=tg[0][:, 512 * m:512 * m + 512], op0=ALU.mult, op1=ALU.mult),
                        reads=[B_pfo[fi], B_tg[0], B_misc], writes=[B_pT[eb2]])

            w_ba_v = w_ba.rearrange("(h p) n -> p h n", p=128)
            w_bp_v = w_bp.rearrange("(h p) n -> p h n", p=128)
            for db in range(KC):
                wa_i = nx("wb", 4)
                S.op("pool", lambda g, wa_i=wa_i, db=db: g.dma_start(out=wb[wa_i][:],
                                                                      in_=w_ba_v[:, :, 128 * db:128 * db + 128]),
                     writes=[B_wb[wa_i]], dsem=f"db{wa_i}")
                wp_i = nx("wb", 4)
                S.op("pool", lambda g, wp_i=wp_i, db=db: g.dma_start(out=wb[wp_i][:],
                                                                      in_=w_bp_v[:, :, 128 * db:128 * db + 128]),
                     writes=[B_wb[wp_i]], dsem=f"db{wp_i}")
                for which, cbase in ((0, CG), (1, CG + D)):
                    wt, Bw = load_wblock(cbase + db * 128)
                    tgi = 1 + which
                    for m in range(2):
                        i = project(cpp, B_cpp, cc, wt, Bw, own_mov(m), 512)
                        bcol = which * KC + db
                        S.op("act", lambda a, i=i, m=m, tgi=tgi, bcol=bcol: a.activation(
                            out=tg[tgi][:, 512 * m:512 * m + 512], in_=cpp[i][:], func=AF.Tanh,
                            bias=halfb[:, bcol:bcol + 1], scale=0.5),
                            reads=[B_cpp[i], B_misc], writes=[B_tg[tgi]])
                for m in range(2):
                    ya = nx("pya", 2)
                    for h in range(NHEAD):
                        S.op("pe", lambda t, ya=ya, h=h, m=m, wa_i=wa_i: t.matmul(
                            pya[ya][:], lhsT=wb[wa_i][:, h, :], rhs=aT[:, h, 512 * m:512 * m + 512],
                            start=(h == 0), stop=(h == NHEAD - 1)),
                            reads=[B_wb[wa_i], B_aT[h]], writes=[B_pya[ya]])
                    yp = nx("pyp", 2)
                    for e in range(8):
                        S.op("pe", lambda t, yp=yp, e=e, m=m, wp_i=wp_i: t.matmul(
                            pyp[yp][:], lhsT=wb[wp_i][:, e, :], rhs=pT[:, e, 512 * m:512 * m + 512],
                            start=(e == 0), stop=(e == 7)),
                            reads=[B_wb[wp_i], B_pT[e]], writes=[B_pyp[yp]])
                    S.op("dve", lambda v, ya=ya, m=m: v.scalar_tensor_tensor(
                        out=tg[1][:, 512 * m:512 * m + 512], in0=tg[1][:, 512 * m:512 * m + 512], scalar=1.0,
                        in1=pya[ya][:], op0=ALU.add, op1=ALU.mult),
                        reads=[B_tg[1], B_pya[ya]], writes=[B_tg[1]])
                    S.op("dve", lambda v, yp=yp, m=m: v.scalar_tensor_tensor(
                        out=tg[2][:, 512 * m:512 * m + 512], in0=tg[2][:, 512 * m:512 * m + 512], scalar=1.0,
                        in1=pyp[yp][:], op0=ALU.add, op1=ALU.mult),
                        reads=[B_tg[2], B_pyp[yp]], writes=[B_tg[2]])
                S.op("dve", lambda v, db=db: v.tensor_tensor(out=mT[:, db, :], in0=tg[1][:], in1=tg[2][:], op=ALU.add),
                     reads=[B_tg[1], B_tg[2]], writes=[B_mT[db]])

            w_out_v = w_out.rearrange("(k p) n -> p k n", p=128)
            for cg in range(4):
                wi = nx("wo", 2)
                S.op("pool", lambda g, wi=wi, cg=cg: g.dma_start(out=wo[wi][:],
                                                                  in_=w_out_v[:, :, 512 * cg:512 * cg + 512]),
                     writes=[B_wo[wi]], dsem=f"dwo{wi}")
                for tt in range(8):
                    xi = nx("xr", 2)
                    S.op("act", lambda a, xi=xi, tt=tt, cg=cg: a.dma_start(
                        out=xr[xi][:], in_=xh[THALO + 128 * tt:THALO + 128 * tt + 128, 512 * cg:512 * cg + 512]),
                        writes=[B_xr[xi]], dsem=f"dr{xi}")
                    fi = nx("pfo", 2)
                    for db in range(KC):
                        S.op("pe", lambda t, fi=fi, db=db, tt=tt, wi=wi: t.matmul(
                            pfo[fi][:], lhsT=mT[:, db, 128 * tt:128 * tt + 128], rhs=wo[wi][:, db, :],
                            start=(db == 0), stop=(db == KC - 1)),
                            reads=[B_mT[db], B_wo[wi]], writes=[B_pfo[fi]])
                    oi = nx("ob", 2)
                    S.op("dve", lambda v, fi=fi, xi=xi, oi=oi: v.scalar_tensor_tensor(
                        out=ob[oi][:], in0=pfo[fi][:], scalar=0.5, in1=xr[xi][:], op0=ALU.mult, op1=ALU.add),
                        reads=[B_pfo[fi], B_xr[xi]], writes=[B_ob[oi]])
                    S.op("sp", lambda s, oi=oi, tt=tt, cg=cg: s.dma_start(
                        out=out_d[128 * tt:128 * tt + 128, 512 * cg:512 * cg + 512], in_=ob[oi][:]),
                        reads=[B_ob[oi]], dsem=f"do{oi}")
            S.emit_phase()
    return nc


def _host_consts(c):
    i = np.arange(128)[:, None]
    m = np.arange(128)[None, :]
    cur = (i <= m).astype(np.float32)
    prev = (i >= m).astype(np.float32)
    halo = prev if c > 0 else np.zeros_like(prev)
    msk = np.zeros((128, NMSK), np.float32)
    msk[:, M_A:M_A + 512] = np.concatenate([halo, cur, prev, cur], axis=1)
    msk[:, M_B:M_B + 512] = np.concatenate([prev, cur, prev, cur], axis=1)
    i64 = np.arange(128)[:, None]
    m64 = np.arange(64)[None, :]
    if c == 0:
        halo3 = np.zeros((128, 64), np.float32)
    elif c == 1:
        halo3 = ((i64 >= m64) & (i64 >= 64)).astype(np.float32)
    else:
        halo3 = (i64 >= m64).astype(np.float32)
    bd = np.zeros((128, 128), np.float32)
    c64 = (np.arange(64)[:, None] <= np.arange(64)[None, :]).astype(np.float32)
    bd[0:64, 0:64] = c64
    bd[64:128, 64:128] = c64
    one = np.concatenate([halo3, halo3, bd], axis=1)
    msk[:, M_C:M_C + 512] = np.concatenate([one, one], axis=1)
    msk[:, M_ID:M_ID + 128] = np.eye(128, dtype=np.float32)
    rt = np.zeros((32, 32), np.float32)
    for cc in range(16):
        rt[cc + 16, cc] = -1.0
        rt[cc, cc + 16] = 1.0
    msk[0:32, M_RT:M_RT + 32] = rt
    start = 1024 * c
    pos = np.arange(start - THALO, start + TOWN).astype(np.float64)
    inv = 500000.0 ** (-np.arange(0, 32, 2, dtype=np.float64) / 32.0)
    ang = (pos[None, :].astype(np.float32) * inv[:, None].astype(np.float32)).astype(np.float64)
    tab = np.zeros((32, 2, NTOK), np.float32)
    tab[0:16, 0] = np.cos(ang)
    tab[16:32, 0] = np.cos(ang)
    tab[0:16, 1] = -np.sin(ang)
    tab[16:32, 1] = np.sin(ang)
    ic = np.zeros((4, 16), np.float32)
    for gq in range(4):
        k = 2 ** (gq + 1)
        if c == 0:
            ic[gq] = 1.0 / np.minimum(np.arange(16) + 1, k)
        else:
            ic[gq] = 1.0 / k
    return msk, tab, ic


_NC_CACHE = {}


def kernel(x, norm_gain, w_in, b_gates, q_norm_gain, k_norm_gain, pool_maps, pool_scale,
           w_branch_attn, w_branch_pool, w_out):
    x = np.asarray(x, np.float32)
    f = lambda a: np.ascontiguousarray(np.asarray(a, np.float32))
    w_in, w_ba, w_bp, w_o, pmaps = f(w_in), f(w_branch_attn), f(w_branch_pool), f(w_out), f(pool_maps)
    gain = np.ascontiguousarray(np.broadcast_to(f(norm_gain)[None, :], (128, D)))
    if "nc" not in _NC_CACHE:
        _NC_CACHE["nc"] = build_program()
    nc = _NC_CACHE["nc"]
    in_maps = []
    for core in range(8):
        b, c = core // 4, core % 4
        start = 1024 * c
        xh = np.zeros((NTOK, D), np.float32)
        lo = start - THALO
        src_lo = max(lo, 0)
        xh[src_lo - lo:] = x[b, src_lo:start + TOWN]
        msk, tab, ic = _host_consts(c)
        cst = np.zeros((128, NCST), np.float32)
        cst[:, C_QG] = f(q_norm_gain)
        cst[:, C_KG] = f(k_norm_gain)
        cst[:, C_PS:C_PS + 8] = f(pool_scale).reshape(8, 128).T
        cst[:, C_BG:C_BG + 32] = f(b_gates).reshape(32, 128).T
        cst[:, C_IC:C_IC + 64] = ic.reshape(1, 64)
        in_maps.append({"xh": xh, "w_in": w_in, "w_ba": w_ba, "w_bp": w_bp, "w_out": w_o, "pmaps": pmaps,
                        "gain": gain, "cst": cst, "msk": msk, "tab": tab})
    res = run_bass_kernel_spmd(nc, in_maps, core_ids=list(range(8)))
    out = np.zeros((2, SEQ, D), np.float32)
    for core in range(8):
        b, c = core // 4, core % 4
        out[b, 1024 * c:1024 * c + 1024] = res.results[core]["out"]
    return out
```

```python
import numpy as np
import concourse.bass as bass
import concourse.mybir as mybir
from concourse.bass_utils import run_bass_kernel_spmd

F32 = mybir.dt.float32
BF16 = mybir.dt.bfloat16
AF = mybir.ActivationFunctionType
ALU = mybir.AluOpType

D = 2048
KC = 16
TOWN = 1024
THALO = 2048
NTOK = TOWN + THALO
SEQ = 4096
EPS = 1e-6
NHEAD = 8
SQ128 = float(np.sqrt(128.0))

CQ, CK, CV, CZA, CU, CZP, CG = 0, 3072, 6144, 9216, 10240, 11264, 12288

C_QG, C_KG, C_PS, C_BG, C_IC = 0, 1, 2, 10, 42
NCST = 42 + 64
M_A, M_B, M_C, M_ID, M_RT = 0, 512, 1024, 1536, 1664
NMSK = 1696


class Buf:
    __slots__ = ("name", "w", "r")

    def __init__(self, name):
        self.name = name
        self.w = None
        self.r = {}


class Op:
    __slots__ = ("eng", "fn", "deps", "signal", "semval", "key", "seq", "isdma", "phase")


class Sched:
    ENGS = ("pe", "act", "dve", "pool", "sp")

    def __init__(self, nc, sems):
        self.nc = nc
        self.sems = sems
        self.streams = {e: [] for e in self.ENGS}
        self.seqctr = {k: 0 for k in sems}
        self.semcnt = {k: 0 for k in sems}
        self.waited = {e: {k: 0 for k in sems} for e in self.ENGS}
        self.lastop = {k: None for k in sems}
        self.phase = 0

    def op(self, eng, fn, reads=(), writes=(), dsem=None):
        o = Op()
        o.eng = eng
        o.fn = fn
        o.signal = False
        o.semval = None
        o.phase = self.phase
        o.isdma = dsem is not None
        o.key = dsem if dsem is not None else eng
        o.seq = self.seqctr[o.key]
        self.seqctr[o.key] += 1
        deps = {}

        def add(d):
            if d is None or d is o:
                return
            cur = deps.get(d.key)
            if cur is None or cur.seq < d.seq:
                deps[d.key] = d

        for b in reads:
            add(b.w)
        for b in writes:
            add(b.w)
            for d in b.r.values():
                add(d)
        for b in reads:
            b.r[o.key] = o
        for b in writes:
            b.w = o
            b.r = {}
        o.deps = deps
        self.streams[eng].append(o)
        self.lastop[o.key] = o
        return o

    def emit_phase(self):
        nc = self.nc
        streams = self.streams
        for e in self.ENGS:
            for o in streams[e]:
                if o.isdma:
                    o.signal = True
                for k, d in o.deps.items():
                    if d.phase != self.phase or (d.eng == "pe" and o.eng == "pe" and not d.isdma):
                        continue
                    d.signal = True
        for k, o in self.lastop.items():
            if o is not None:
                o.signal = True
        for e in self.ENGS:
            for o in streams[e]:
                if o.signal:
                    self.semcnt[o.key] += 16 if o.isdma else 1
                    o.semval = self.semcnt[o.key]
        final = dict(self.semcnt)
        sems = self.sems
        waited = self.waited

        def run(ename, eng):
            w = waited[ename]
            for o in streams[ename]:
                for k, d in o.deps.items():
                    if d.phase != self.phase or (d.eng == "pe" and ename == "pe" and not d.isdma):
                        continue
                    if w[k] < d.semval:
                        eng.wait_ge(sems[k], d.semval)
                        w[k] = d.semval
                ins = o.fn(eng)
                if o.signal:
                    ins.then_inc(sems[o.key], 16 if o.isdma else 1)
            for k, v in final.items():
                if w[k] < v:
                    eng.wait_ge(sems[k], v)
                    w[k] = v

        with nc.Block() as block:
            @block.tensor
            def _(t):
                run("pe", t)

            @block.scalar
            def _(a):
                run("act", a)

            @block.vector
            def _(v):
                run("dve", v)

            @block.gpsimd
            def _(g):
                run("pool", g)

            @block.sync
            def _(s):
                run("sp", s)
        self.streams = {e: [] for e in self.ENGS}
        self.lastop = {k: None for k in self.sems}
        self.phase += 1


def build_program(debug=False):
    nc = bass.Bass("TRN2", target_bir_lowering=False)
    xh = nc.dram_tensor("xh", [NTOK, D], F32, kind="ExternalInput").ap()
    w_in = nc.dram_tensor("w_in", [D, 16384], F32, kind="ExternalInput").ap()
    w_ba = nc.dram_tensor("w_ba", [1024, D], F32, kind="ExternalInput").ap()
    w_bp = nc.dram_tensor("w_bp", [1024, D], F32, kind="ExternalInput").ap()
    w_out = nc.dram_tensor("w_out", [D, D], F32, kind="ExternalInput").ap()
    pmaps = nc.dram_tensor("pmaps", [4, 256, 256], F32, kind="ExternalInput").ap()
    gain = nc.dram_tensor("gain", [128, D], F32, kind="ExternalInput").ap()
    cst_d = nc.dram_tensor("cst", [128, NCST], F32, kind="ExternalInput").ap()
    msk_d = nc.dram_tensor("msk", [128, NMSK], F32, kind="ExternalInput").ap()
    tab_d = nc.dram_tensor("tab", [32, 2, NTOK], F32, kind="ExternalInput").ap()
    out_d = nc.dram_tensor("out", [TOWN, D], F32, kind="ExternalOutput").ap()

    w_in_v = w_in.rearrange("(kc p) n -> p kc n", p=128)

    from contextlib import ExitStack
    es = ExitStack()
    with es:
        def sb(name, shape, dt):
            return es.enter_context(nc.sbuf_tensor("s_" + name, shape, dt))

        semnames = (["pe", "act", "dve", "pool", "sp"] + [f"dw{i}" for i in range(4)] + [f"dx{i}" for i in range(3)]
                    + [f"do{i}" for i in range(3)] + [f"dc{i}" for i in range(5)] + [f"db{i}" for i in range(4)]
                    + [f"dwo{i}" for i in range(2)] + [f"dr{i}" for i in range(3)] + ["drot0", "drot1", "drot2", "drot3"])
        sems = {k: es.enter_context(nc.semaphore("s_" + k)) for k in semnames}
        S = Sched(nc, sems)

        hTo = sb("hTo", [128, KC, TOWN], BF16)
        aT = sb("aT", [128, NHEAD, TOWN], BF16)
        cst = sb("cst", [128, NCST], F32)
        mskb = sb("mskb", [128, NMSK], BF16)
        NW = 3
        wring = [sb(f"wr{i}", [128, KC, 128], BF16) for i in range(NW)]
        ones = sb("ones", [128, 128], BF16)
        pshalf = sb("pshalf", [128, 8], F32)
        halfb = sb("halfb", [128, 32], F32)
        uh = sb("uh", [128, 8, 16], F32)
        B_hTh, B_hTo, B_cst, B_mskb, B_ones = Buf("hTh"), Buf("hTo"), Buf("cst"), Buf("mskb"), Buf("ones")
        B_aT = [Buf(f"aT{h}") for h in range(NHEAD)]
        B_wr = [Buf(f"wr{i}") for i in range(NW)]
        B_misc = Buf("misc")
        B_uh = Buf("uh")
        wctr = [0]

        def load_wblock(col0):
            i = wctr[0] % NW
            wctr[0] += 1
            t = wring[i]
            S.op("pool", lambda g, t=t, col0=col0: g.dma_start(out=t[:], in_=w_in_v[:, :, col0:col0 + 128]),
                 writes=[B_wr[i]], dsem=f"dw{i}")
            return t, B_wr[i]

        ident = mskb[:, M_ID:M_ID + 128]
        rt = mskb[0:32, M_RT:M_RT + 32]

        own_q, bg_q, late_q = [], [], []
        pcount = [0]

        def pop_late(force=False):
            if late_q and (force or late_q[0].due <= pcount[0]):
                late_q.pop(0)()
                return True
            return False

        def pop_own():
            if pop_late():
                return True
            if own_q:
                own_q.pop(0)()
                return True
            return False

        def pop_bg():
            if bg_q:
                bg_q.pop(0)()
                return True
            return False

        def flush_own():
            while own_q or late_q:
                if own_q:
                    own_q.pop(0)()
                else:
                    pop_late(force=True)

        def need(tag):
            while any(tag in getattr(f, "res", ()) for f in own_q):
                pop_own()

        def flush_bg():
            while bg_q:
                flush_own()
                pop_bg()
            flush_own()

        eAB = ExitStack()
        hTh = eAB.enter_context(nc.sbuf_tensor("s_hTh", [128, KC, THALO], BF16))

        def project(pp, B_pp, ctrd, wt, Bw, mov, n):
            i = ctrd["pp"] % len(pp)
            ctrd["pp"] += 1
            pcount[0] += 1
            need(("pp", i))
            while len(own_q) > 4:
                pop_own()
            for kc in range(KC):
                S.op("pe", lambda t, i=i, kc=kc, mov=mov, wt=wt, n=n: t.matmul(
                    pp[i][:, 0:n], lhsT=wt[:, kc, :], rhs=mov(kc), start=(kc == 0), stop=(kc == KC - 1)),
                    reads=[Bw, B_hTh, B_hTo], writes=[B_pp[i]])
                if kc in (3, 7, 11, 15):
                    pop_own()
                elif kc in (1, 5, 9, 13):
                    pop_bg()
            return i

        with ExitStack() as ea:
            def sba(name, shape, dt):
                return ea.enter_context(nc.sbuf_tensor("a_" + name, shape, dt))

            xt = [sba(f"xt{i}", [128, D], F32) for i in range(3)]
            xn = [sba(f"xn{i}", [128, D], BF16) for i in range(2)]
            junk = sba("junk", [128, D], BF16)
            gb = sba("gb", [128, D], F32)
            mskf = sba("mskf", [128, NMSK], F32)
            stat = sba("stat", [128, 3 * 24], F32)
            tpa = [ea.enter_context(nc.psum_tensor(f"tpa{i}", [128, 8, 128], BF16)) for i in range(2)]
            B_xt = [Buf("xt0"), Buf("xt1"), Buf("xt2")]
            B_xn = [Buf("xn0"), Buf("xn1")]
            B_junk, B_gb, B_mskf, B_stat = Buf("junk"), Buf("gb"), Buf("mskf"), Buf("stat")
            B_st = [Buf(f"st{j}") for j in range(NTOK // 128)]
            B_tpa = [Buf("tpa0"), Buf("tpa1")]

            S.op("sp", lambda s: s.dma_start(out=cst[:], in_=cst_d), writes=[B_cst], dsem="dc0")
            S.op("sp", lambda s: s.dma_start(out=mskf[:], in_=msk_d), writes=[B_mskf], dsem="dc1")
            S.op("sp", lambda s: s.dma_start(out=gb[:], in_=gain), writes=[B_gb], dsem="dc2")
            S.op("dve", lambda v: v.tensor_copy(out=mskb[:], in_=mskf[:]), reads=[B_mskf], writes=[B_mskb])
            S.op("dve", lambda v: v.memset(ones[:], 1.0), writes=[B_ones])
            S.op("dve", lambda v: v.memset(stat[:], 0.0), writes=[B_stat])
            S.op("dve", lambda v: v.tensor_scalar_mul(out=pshalf[:], in0=cst[:, C_PS:C_PS + 8], scalar1=0.5),
                 reads=[B_cst], writes=[B_misc])
            S.op("dve", lambda v: v.tensor_scalar_mul(out=halfb[:], in0=cst[:, C_BG:C_BG + 32], scalar1=0.5),
                 reads=[B_cst], writes=[B_misc])
            S.op("dve", lambda v: v.tensor_scalar_mul(out=gb[:], in0=gb[:], scalar1=float(np.sqrt(D))),
                 reads=[B_gb], writes=[B_gb])

            NTT = NTOK // 128

            def a_front(j):
                b, b3, c0 = j % 2, j % 3, 3 * j
                S.op("sp", lambda s: s.dma_start(out=xt[b3][:], in_=xh[j * 128:(j + 1) * 128, :]),
                     writes=[B_xt[b3]], dsem=f"dx{b3}")
                S.op("act", lambda a: a.activation(out=junk[:], in_=xt[b3][:], func=AF.Square,
                                                   accum_out=stat[:, c0:c0 + 1]),
                     reads=[B_xt[b3], B_stat], writes=[B_junk, B_st[j]])
                S.op("act", lambda a: a.activation(out=stat[:, c0 + 1:c0 + 2], in_=stat[:, c0:c0 + 1],
                                                   func=AF.Ln, bias=float(D * EPS)),
                     reads=[B_st[j]], writes=[B_st[j]])
                S.op("act", lambda a: a.activation(out=stat[:, c0 + 2:c0 + 3], in_=stat[:, c0 + 1:c0 + 2],
                                                   func=AF.Exp, scale=-0.5),
                     reads=[B_st[j]], writes=[B_st[j]])
                S.op("dve", lambda v: v.scalar_tensor_tensor(
                    out=xn[b][:], in0=xt[b3][:], scalar=stat[:, c0 + 2:c0 + 3], in1=gb[:],
                    op0=ALU.mult, op1=ALU.mult),
                    reads=[B_xt[b3], B_st[j], B_gb], writes=[B_xn[b]])

            def a_back(j):
                b = j % 2
                for half in range(2):
                    for k8 in range(8):
                        kc = half * 8 + k8
                        S.op("pe", lambda t, half=half, k8=k8, kc=kc: t.transpose(
                            tpa[half][:, k8, :], xn[b][:, kc * 128:(kc + 1) * 128], ident),
                            reads=[B_xn[b], B_mskb], writes=[B_tpa[half]])
                    if j < 16:
                        dst = hTh[:, half * 8:(half + 1) * 8, j * 128:(j + 1) * 128]
                        bd = B_hTh
                    else:
                        dst = hTo[:, half * 8:(half + 1) * 8, (j - 16) * 128:(j - 15) * 128]
                        bd = B_hTo
                    if half == 0:
                        S.op("act", lambda a, dst=dst, half=half: a.copy(out=dst, in_=tpa[half][:]),
                             reads=[B_tpa[half]], writes=[bd])
                    else:
                        S.op("dve", lambda v, dst=dst, half=half: v.tensor_copy(out=dst, in_=tpa[half][:]),
                             reads=[B_tpa[half]], writes=[bd])

            for j in range(NTT + 1):
                if j < NTT:
                    a_front(j)
                if j >= 1:
                    a_back(j - 1)
            S.emit_phase()

        with ExitStack() as eb:
            def sbb(name, shape, dt):
                return eb.enter_context(nc.sbuf_tensor("b_" + name, shape, dt))

            def psb(name, shape, dt):
                return eb.enter_context(nc.psum_tensor("bp_" + name, shape, dt))

            tab = sbb("tab", [32, 2, NTOK], BF16)
            B_tab = Buf("tab")
            NU = 2
            kS = [sbb(f"kS{i}", [128, 3072], BF16) for i in range(NU)]
            qS = [sbb(f"qS{i}", [128, 1024], BF16) for i in range(NU)]
            vS = [sbb(f"vS{i}", [128, 24, 128], BF16) for i in range(NU)]
            B_kS = [Buf(f"kS{i}") for i in range(NU)]
            B_qS = [Buf(f"qS{i}") for i in range(NU)]
            B_vS = [Buf(f"vS{i}") for i in range(NU)]
            vT = [sbb(f"vT{i}", [128, 512], BF16) for i in range(2)]
            B_vT = [Buf("vT0"), Buf("vT1")]
            vT3 = sbb("vT3", [128, 3072], BF16)
            B_vT3 = Buf("vT3")
            sq = [sbb(f"sq{i}", [128, 512], BF16) for i in range(2)]
            B_sq = [Buf("sq0"), Buf("sq1")]
            rr = [sbb(f"rr{i}", [128, 512], F32) for i in range(2)]
            B_rr = [Buf("rr0"), Buf("rr1")]
            rotbs = [sbb(f"rotb{i}", [32, 1536], BF16) for i in range(2)]
            B_rots = [[Buf(f"rot{i}a"), Buf(f"rot{i}b")] for i in range(2)]
            NE = 3
            Eb = [sbb(f"E{i}", [128, 512], BF16) for i in range(NE)]
            B_E = [Buf(f"E{i}") for i in range(NE)]
            NUMs = sbb("NUM", [128, TOWN], F32)
            DENs = sbb("DEN", [128, TOWN], F32)
            szs = sbb("sz", [128, TOWN], F32)
            ez = [sbb(f"ez{i}", [128, 512], F32) for i in range(2)]
            B_NUM, B_DEN, B_sz = Buf("NUM"), Buf("DEN"), Buf("sz")
            B_ez = [Buf("ez0"), Buf("ez1")]

            pp = [psb(f"pp{i}", [128, 512], F32) for i in range(3)]
            B_pp = [Buf("pp0"), Buf("pp1"), Buf("pp2")]
            prot = psb("prot", [128, 512], F32)
            B_prot = Buf("prot")
            pS = prot
            B_pS = B_prot
            paux = prot[:, :].bitcast(BF16)
            B_paux = B_prot
            psc = [psb(f"psc{i}", [128, 512], F32) for i in range(2)]
            B_psc = [Buf("psc0"), Buf("psc1")]
            pnds = [psb(f"pnd{i}", [128, 512], F32) for i in range(2)]
            B_pnds = [Buf("pnd0"), Buf("pnd1")]

            S.op("pool", lambda g: g.dma_start(out=tab[:], in_=tab_d), writes=[B_tab], dsem="dc3")

            ctr = {"pp": 0, "vT": 0, "sq": 0, "rr": 0, "rtmp": 0, "E": 0, "psc": 0, "ez": 0, "pnd": 0, "rot": 0}

            def nxt(name, n):
                i = ctr[name] % n
                ctr[name] += 1
                return i

            hTh_v16 = lambda kc: hTh[:, kc, :].rearrange("p (l r) -> p r l", r=16)
            hTo_v16 = lambda kc: hTo[:, kc, :].rearrange("p (l r) -> p r l", r=16)

            def tabv(cs, lo, hi, r):
                a = tab[:, cs, lo:hi]
                if r > 1:
                    a = a.rearrange("p (l r) -> p r l", r=r)
                return a

            def groups_for(g, kind, u):
                res = []
                ks, qs, vs = kS[u], qS[u], vS[u]
                if g == 0:
                    lst = []
                    if kind != "q":
                        lst.append(("h", 1920, 2048))
                    lst += [("o", 0, 512), ("o", 512, 1024)]
                    for (wh, lo, hi) in lst:
                        n = hi - lo
                        if wh == "h":
                            mov = lambda kc, lo=lo, hi=hi: hTh[:, kc, lo:hi]
                            tlo = lo
                            kd = ks[:, 0:128]
                            vt0 = 0
                        else:
                            mov = lambda kc, lo=lo, hi=hi: hTo[:, kc, lo:hi]
                            tlo = THALO + lo
                            kd = ks[:, 128 + lo:128 + hi]
                            vt0 = 1 + lo // 128
                        dst = kd if kind == "k" else (qs[:, lo:hi] if kind == "q" else None)
                        res.append(dict(mov=mov, n=n, r=1, dst=dst, cos=tabv(0, tlo, tlo + n, 1),
                                        sin=tabv(1, tlo, tlo + n, 1), vdst=vs[:, vt0:vt0 + n // 128, :]))
                elif g == 1:
                    ks3 = ks[:, 0:1536].rearrange("p (r l) -> p r l", r=4)
                    qs3 = qs[:, 0:1024].rearrange("p (r l) -> p r l", r=4)
                    vs4 = vs[:, 0:12, :].rearrange("p (r t) c -> p r t c", r=4)
                    lst = []
                    if kind != "q":
                        lst.append(("h", 1536, 2048, 0))
                    lst += [("o", 0, 512, 1), ("o", 512, 1024, 2)]
                    for (wh, lo, hi, t) in lst:
                        if wh == "h":
                            mov = lambda kc, lo=lo, hi=hi: hTh[:, kc, lo:hi]
                            tlo = lo
                        else:
                            mov = lambda kc, lo=lo, hi=hi: hTo[:, kc, lo:hi]
                            tlo = THALO + lo
                        dst = ks3[:, :, t * 128:(t + 1) * 128] if kind == "k" else (
                            qs3[:, :, (t - 1) * 128:t * 128] if kind == "q" else None)
                        res.append(dict(mov=mov, n=512, r=4, dst=dst, cos=tabv(0, tlo, tlo + 512, 4),
                                        sin=tabv(1, tlo, tlo + 512, 4), vdst=vs4[:, :, t, :]))
                else:
                    ksh = ks[:, 0:2048].rearrange("p (r l) -> p r l", r=16)
                    kso = ks[:, 2048:3072].rearrange("p (r l) -> p r l", r=16)
                    qs3 = qs[:, 0:1024].rearrange("p (r l) -> p r l", r=16)
                    v3h = vT3[:, 0:2048].rearrange("p (r l) -> p r l", r=16)
                    v3o = vT3[:, 2048:3072].rearrange("p (r l) -> p r l", r=16)
                    if kind != "q":
                        for m in range(4):
                            mov = lambda kc, m=m: hTh[:, kc, 512 * m:512 * m + 512]
                            res.append(dict(mov=mov, n=512, r=16, dst=ksh[:, :, 32 * m:32 * m + 32],
                                            cos=tabv(0, 512 * m, 512 * m + 512, 16),
                                            sin=tabv(1, 512 * m, 512 * m + 512, 16),
                                            v3dst=v3h[:, :, 32 * m:32 * m + 32], last=False))
                    for m in range(2):
                        mov = lambda kc, m=m: hTo[:, kc, 512 * m:512 * m + 512]
                        dst = kso[:, :, 32 * m:32 * m + 32] if kind == "k" else qs3[:, :, 32 * m:32 * m + 32]
                        tlo = THALO + 512 * m
                        res.append(dict(mov=mov, n=512, r=16, dst=dst,
                                        cos=tabv(0, tlo, tlo + 512, 16), sin=tabv(1, tlo, tlo + 512, 16),
                                        v3dst=v3o[:, :, 32 * m:32 * m + 32], last=(m == 1)))
                return res

            def viewd(ap2, r):
                return ap2 if r == 1 else ap2.rearrange("p (l r) -> p r l", r=r)

            def view3(ap2, r):
                return ap2 if r == 1 else ap2.rearrange("p (r l) -> p r l", r=r)

            def post_qk(i, gd, gcol):
                n, r = gd["n"], gd["r"]
                P = pp[i][:, 0:n]
                BD = gd["B"]
                dst = gd["dst"]
                dst32 = dst[0:32]
                si = nxt("sq", 2)
                need(("sq", si))
                S.op("act", lambda a: a.activation(out=sq[si][:, 0:n], in_=P, func=AF.Square),
                     reads=[B_pp[i]], writes=[B_sq[si]])

                def stage_b():
                    S.op("pe", lambda t: t.matmul(pS[:, 0:n], lhsT=ones[:], rhs=sq[si][:, 0:n], start=True, stop=True),
                         reads=[B_sq[si], B_ones], writes=[B_pS])
                    ri = nxt("rr", 2)
                    S.op("act", lambda a: a.activation(out=rr[ri][:, 0:n], in_=pS[:, 0:n], func=AF.Ln,
                                                       bias=float(128 * EPS)),
                         reads=[B_pS], writes=[B_rr[ri]])
                    S.op("act", lambda a: a.activation(out=rr[ri][:, 0:n], in_=rr[ri][:, 0:n], func=AF.Exp, scale=-0.5),
                         reads=[B_rr[ri]], writes=[B_rr[ri]])
                    S.op("dve", lambda v: v.scalar_tensor_tensor(
                        out=dst, in0=viewd(P, r), scalar=cst[:, gcol:gcol + 1], in1=viewd(rr[ri][:, 0:n], r),
                        op0=ALU.mult, op1=ALU.mult),
                        reads=[B_pp[i], B_rr[ri], B_cst], writes=[BD])

                stage_b.res = (("pp", i), ("sq", si))
                own_q.append(stage_b)

            def post_v(i, gd, g):
                n, r = gd["n"], gd["r"]
                BD = gd["B"]
                if g < 2:
                    vi = nxt("vT", 2)
                    need(("vT", vi))
                    S.op("act", lambda a: a.copy(out=view3(vT[vi][:, 0:n], r), in_=viewd(pp[i][:, 0:n], r)),
                         reads=[B_pp[i]], writes=[B_vT[vi]])
                    nch = n // 128
                    vdst = gd["vdst"]

                    def stage_b():
                        for c in range(nch):
                            S.op("pe", lambda t, c=c: t.transpose(paux[:, c * 128:(c + 1) * 128],
                                                                  vT[vi][:, c * 128:(c + 1) * 128], ident),
                                 reads=[B_vT[vi], B_mskb], writes=[B_paux])
                        src = paux[:, 0:n].rearrange("p (t c) -> p t c", c=128)
                        S.op("act", lambda a: a.copy(out=vdst, in_=src), reads=[B_paux], writes=[BD])

                    stage_b.res = (("vT", vi),)
                    own_q.append(stage_b)
                else:
                    v3dst = gd["v3dst"]
                    need(("vT3",))
                    S.op("act", lambda a: a.copy(out=v3dst, in_=viewd(pp[i][:, 0:n], r)),
                         reads=[B_pp[i]], writes=[B_vT3])
                    if gd["last"]:
                        vs = vS[gd["u"]]
                        for j in range(3):
                            def stage_t(j=j):
                                for c in range(8):
                                    S.op("pe", lambda t, c=c: t.transpose(
                                        paux[:, c * 128:(c + 1) * 128],
                                        vT3[:, (8 * j + c) * 128:(8 * j + c + 1) * 128], ident),
                                        reads=[B_vT3, B_mskb], writes=[B_paux])
                                src = paux[:, 0:1024].rearrange("p (t c) -> p t c", c=128)
                                S.op("act", lambda a: a.copy(out=vs[:, 8 * j:8 * j + 8, :], in_=src),
                                     reads=[B_paux], writes=[BD])
                            stage_t.res = (("vT3",),)
                            own_q.append(stage_t)

            def queue_rope(kind, g, u):
                st = kS[u] if kind == "k" else qS[u]
                BD = B_kS[u] if kind == "k" else B_qS[u]
                if kind == "q":
                    chunks = [(0, 1024, 2048, 3072, (1, 4, 16)[g])]
                elif g == 0:
                    chunks = [(0, 1152, 1920, 3072, 1)]
                elif g == 1:
                    chunks = [(0, 1536, 1536, 3072, 4)]
                else:
                    chunks = [(0, 1024, 0, 2048, 16, 0), (1024, 2048, 0, 2048, 16, 1), (2048, 3072, 2048, 3072, 16)]
                for ch in chunks:
                    def stage_r(ch=ch):
                        c0, c1, t0, t1, r = ch[:5]
                        n = c1 - c0
                        ri = ctr["rot"] % 2
                        while any(getattr(f, "rot", None) == ri for f in late_q):
                            pop_late(force=True)
                        ctr["rot"] += 1
                        rotb_ = rotbs[ri]
                        Br = B_rots[ri]
                        flat = st[0:32, c0:c1]
                        dv = flat if r == 1 else flat.rearrange("p (r l) -> p r l", r=(8 if len(ch) == 6 else r))
                        rb = rotb_[:, 0:n]
                        rv = rb if r == 1 else rb.rearrange("p (r l) -> p r l", r=(8 if len(ch) == 6 else r))

                        def tv(cs):
                            a_ = tab[:, cs, t0:t1]
                            if r > 1:
                                a_ = a_.rearrange("p (l r) -> p r l", r=r)
                            if len(ch) == 6:
                                a_ = a_[:, 8 * ch[5]:8 * ch[5] + 8, :]
                            return a_
                        S.op("sp", lambda s_: s_.dma_start(out=rotb_[0:16, 0:n], in_=st[16:32, c0:c1]),
                             reads=[BD], writes=[Br[0]], dsem=f"drot{2 * ri}")
                        S.op("sp", lambda s_: s_.dma_start(out=rotb_[16:32, 0:n], in_=st[0:16, c0:c1]),
                             reads=[BD], writes=[Br[1]], dsem=f"drot{2 * ri + 1}")

                        def stage_r2():
                            S.op("dve", lambda v: v.tensor_tensor(out=dv, in0=dv, in1=tv(0), op=ALU.mult),
                                 reads=[BD, B_tab, Br[0], Br[1]], writes=[BD])
                            S.op("dve", lambda v: v.tensor_tensor(out=rv, in0=rv, in1=tv(1), op=ALU.mult),
                                 reads=[Br[0], Br[1], B_tab], writes=[Br[0], Br[1]])
                            S.op("dve", lambda v: v.tensor_tensor(out=dv, in0=dv, in1=rv, op=ALU.add),
                                 reads=[BD, Br[0], Br[1]], writes=[BD])
                        stage_r2.due = pcount[0] + 2
                        stage_r2.rot = ri
                        late_q.append(stage_r2)
                    own_q.append(stage_r)

            MA = mskb[:, M_A:M_A + 512]
            MB = mskb[:, M_B:M_B + 512]
            MC = mskb[:, M_C:M_C + 512]

            def attention(g, u, first, unit_id):
                ks, qs, vs = kS[u], qS[u], vS[u]
                Bk, Bq, Bv = B_kS[u], B_qS[u], B_vS[u]
                batches = []
                if g == 0:
                    for bi in range(4):
                        sc, pv = [], []
                        for qq in range(2):
                            i = 2 * bi + qq + 1
                            qap = qs[:, 128 * (i - 1):128 * i]
                            e0 = qq * 256
                            sc.append((ks[:, 128 * (i - 1):128 * i], qap, e0, 128))
                            sc.append((ks[:, 128 * i:128 * (i + 1)], qap, e0 + 128, 128))
                            pv.append((qq * 128,
                                       [(vs[:, i - 1, :], e0, 0, 128), (vs[:, i, :], e0 + 128, 0, 128)]))
                        batches.append((sc, MA if bi == 0 else MB, pv))
                    numview = lambda t, p: t[:, 256 * p:256 * p + 256]
                    pview = lambda t: t
                elif g == 1:
                    ks3 = ks[:, 0:1536].rearrange("p (r l) -> p r l", r=4)
                    qs3 = qs[:, 0:1024].rearrange("p (r l) -> p r l", r=4)
                    vs4 = vs[:, 0:12, :].rearrange("p (r t) c -> p r t c", r=4)
                    for r in range(4):
                        sc, pv = [], []
                        for qq in range(2):
                            i = qq + 1
                            qap = qs3[:, r, 128 * (i - 1):128 * i]
                            e0 = qq * 256
                            sc.append((ks3[:, r, 128 * (i - 1):128 * i], qap, e0, 128))
                            sc.append((ks3[:, r, 128 * i:128 * (i + 1)], qap, e0 + 128, 128))
                            pv.append((qq * 128,
                                       [(vs4[:, r, i - 1, :], e0, 0, 128), (vs4[:, r, i, :], e0 + 128, 0, 128)]))
                        batches.append((sc, MA, pv))
                    numview = lambda t, p: t[:].rearrange("p (l r) -> p r l", r=4)[:, p, :]
                    pview = lambda t: t
                else:
                    ksh = ks[:, 0:2048].rearrange("p (r l) -> p r l", r=16)
                    kso = ks[:, 2048:3072]
                    qso = qs[:, 0:1024]
                    for bi in range(4):
                        sc, pv = [], []
                        for pq in range(2):
                            pi = 2 * bi + pq
                            e0 = pq * 256
                            r0, r1 = 2 * pi, 2 * pi + 1
                            sc.append((ksh[:, r0, :], qso[:, 64 * r0:64 * r0 + 64], e0, 64))
                            sc.append((ksh[:, r1, :], qso[:, 64 * r1:64 * r1 + 64], e0 + 64, 64))
                            sc.append((kso[:, 128 * pi:128 * pi + 128], qso[:, 128 * pi:128 * pi + 128], e0 + 128, 128))
                            pv.append((pq * 128,
                                       [(vs[:, 16 + pi, :], e0 + 128, 0, 128),
                                        (vs[:, r0, :], e0, 0, 64),
                                        (vs[:, r1, :], e0 + 64, 64, 64)]))
                        batches.append((sc, MC, pv))
                    numview = lambda t, p: t[:].rearrange("p (l r) -> p r l", r=16)[:, 4 * p:4 * p + 4, :]
                    pview = lambda t: t.rearrange("p (r l) -> p r l", r=4)

                state = {}

                def mk_s(bi):
                    sc, mask, pv = batches[bi]

                    def st():
                        si = nxt("psc", 2)
                        for (kap, qap, e0, nq) in sc:
                            S.op("pe", lambda t, kap=kap, qap=qap, e0=e0, nq=nq: t.matmul(
                                psc[si][:, e0:e0 + nq], lhsT=kap, rhs=qap, start=True, stop=True),
                                reads=[Bk, Bq], writes=[B_psc[si]])
                        ei = nxt("E", NE)
                        state[bi] = ei
                        S.op("act", lambda a: a.activation(out=Eb[ei][:], in_=psc[si][:], func=AF.Exp, scale=SQ128),
                             reads=[B_psc[si]], writes=[B_E[ei]])
                        S.op("dve", lambda v: v.tensor_tensor(out=Eb[ei][:], in0=Eb[ei][:], in1=mask, op=ALU.mult),
                             reads=[B_E[ei], B_mskb], writes=[B_E[ei]])
                    return st

                def mk_p(bi):
                    sc, mask, pv = batches[bi]

                    def st():
                        ei = state[bi]
                        pi_ = nxt("pnd", 2)
                        pnd, B_pnd = pnds[pi_], B_pnds[pi_]
                        for (c0, jobs) in pv:
                            for which in range(2):
                                nj = len(jobs)
                                for ji, (vap, e0, oc, on) in enumerate(jobs):
                                    lhs = vap if which == 0 else ones[:]
                                    cb = 256 * which + c0 + oc
                                    S.op("pe", lambda t, cb=cb, on=on, lhs=lhs, e0=e0, ji=ji, nj=nj:
                                         t.matmul(pnd[:, cb:cb + on], lhsT=lhs, rhs=Eb[ei][:, e0:e0 + on],
                                                  start=(ji == 0), stop=(ji == nj - 1)),
                                         reads=[B_E[ei], Bv, B_ones], writes=[B_pnd])
                        p = bi
                        if first:
                            S.op("act", lambda a: a.copy(out=numview(NUMs, p), in_=pview(pnd[:, 0:256])),
                                 reads=[B_pnd], writes=[B_NUM])
                            S.op("act", lambda a: a.copy(out=numview(DENs, p), in_=pview(pnd[:, 256:512])),
                                 reads=[B_pnd], writes=[B_DEN])
                        else:
                            S.op("dve", lambda v: v.tensor_tensor(out=numview(NUMs, p), in0=pview(pnd[:, 0:256]),
                                                                  in1=numview(NUMs, p), op=ALU.add),
                                 reads=[B_pnd, B_NUM], writes=[B_NUM])
                            S.op("dve", lambda v: v.tensor_tensor(out=numview(DENs, p), in0=pview(pnd[:, 256:512]),
                                                                  in1=numview(DENs, p), op=ALU.add),
                                 reads=[B_pnd, B_DEN], writes=[B_DEN])
                    return st

                def pre():
                    flush_own()
                def mk_nop():
                    def nop():
                        pass
                    return nop
                order = [mk_nop() for _ in range(16 if g == 2 else 8)] + [pre, mk_s(0), mk_s(1), mk_p(0), mk_s(2), mk_p(1), mk_s(3),
                                                        mk_p(2), mk_p(3)]
                for f in order:
                    f.unit = unit_id
                bg_q.extend(order)

            uctr = 0
            for h in range(NHEAD):
                for g in range(3):
                    u = uctr % NU
                    uctr += 1
                    first_proj = True
                    while bg_q and getattr(bg_q[0], "unit", 0) <= uctr - NU:
                        flush_own()
                        pop_bg()
                    for kind, cbase, gcol in (("k", CK, C_KG), ("v", CV, None), ("q", CQ, C_QG)):
                        col0 = cbase + g * 1024 + h * 128
                        wt, Bw = load_wblock(col0)
                        for gd in groups_for(g, kind, u):
                            gd["B"] = {"k": B_kS[u], "v": B_vS[u], "q": B_qS[u]}[kind]
                            if first_proj and uctr > NU:
                                pass
                            i = project(pp, B_pp, ctr, wt, Bw, gd["mov"], gd["n"])
                            if first_proj:
                                first_proj = False
                            gd["u"] = u
                            if kind == "v":
                                post_v(i, gd, g)
                            else:
                                post_qk(i, gd, gcol)
                        if kind != "v":
                            queue_rope(kind, g, u)
                    attention(g, u, first=(g == 0), unit_id=uctr)
                wt, Bw = load_wblock(CZA + h * 128)
                for m in range(2):
                    i = project(pp, B_pp, ctr, wt, Bw, lambda kc, m=m: hTo[:, kc, 512 * m:512 * m + 512], 512)
                    zi = nxt("ez", 2)
                    need(("ez", zi))
                    S.op("act", lambda a, i=i, zi=zi: a.activation(out=ez[zi][:], in_=pp[i][:], func=AF.Exp, scale=-1.0),
                         reads=[B_pp[i]], writes=[B_ez[zi]])

                    def stage_z(i=i, zi=zi, m=m):
                        S.op("act", lambda a: a.activation(out=ez[zi][:], in_=ez[zi][:], func=AF.Ln, bias=1.0),
                             reads=[B_ez[zi]], writes=[B_ez[zi]])
                        S.op("act", lambda a: a.activation(out=ez[zi][:], in_=ez[zi][:], func=AF.Exp, scale=-1.0),
                             reads=[B_ez[zi]], writes=[B_ez[zi]])
                        S.op("dve", lambda v: v.tensor_tensor(out=szs[:, 512 * m:512 * m + 512], in0=pp[i][:],
                                                              in1=ez[zi][:], op=ALU.mult),
                             reads=[B_ez[zi], B_pp[i]], writes=[B_sz])
                    stage_z.res = (("pp", i), ("ez", zi))
                    own_q.append(stage_z)

                def fin(h=h):
                    flush_own()
                    S.op("act", lambda a: a.activation(out=DENs[:], in_=DENs[:], func=AF.Ln),
                         reads=[B_DEN], writes=[B_DEN])
                    S.op("act", lambda a: a.activation(out=DENs[:], in_=DENs[:], func=AF.Exp, scale=-1.0),
                         reads=[B_DEN], writes=[B_DEN])
                    S.op("dve", lambda v: v.tensor_tensor(out=NUMs[:], in0=NUMs[:], in1=DENs[:], op=ALU.mult),
                         reads=[B_NUM, B_DEN], writes=[B_NUM])
                    S.op("dve", lambda v: v.tensor_tensor(out=aT[:, h, :], in0=NUMs[:], in1=szs[:], op=ALU.mult),
                         reads=[B_NUM, B_sz], writes=[B_aT[h]])
                fin.unit = uctr
                bg_q.append(fin)
            for blk in range(8):
                wt, Bw = load_wblock(CU + blk * 128)
                i = project(pp, B_pp, ctr, wt, Bw, lambda kc: hTh[:, kc, THALO - 32:THALO], 32)
                S.op("act", lambda a, i=i, blk=blk: a.copy(out=uh[:, blk, :], in_=pp[i][:, 16:32]),
                     reads=[B_pp[i]], writes=[B_uh])
            flush_bg()
            S.emit_phase()
        eAB.close()

        with ExitStack() as ec:
            def sbc(name, shape, dt):
                return ec.enter_context(nc.sbuf_tensor("c_" + name, shape, dt))

            def psc_(name, shape, dt):
                return ec.enter_context(nc.psum_tensor("cp_" + name, shape, dt))

            UW = 16 + TOWN
            uext = [sbc(f"uext{i}", [128, UW], F32) for i in range(3)]
            B_ue = [Buf(f"ue{i}") for i in range(3)]
            dT = sbc("dT", [128, 8, TOWN], BF16)
            B_dT = [Buf(f"dT{i}") for i in range(8)]
            dfix = sbc("dfix", [128, 16], F32)
            B_dfix = Buf("dfix")
            pT = sbc("pT", [128, 8, TOWN], BF16)
            B_pT = [Buf(f"pT{i}") for i in range(8)]
            pm = sbc("pm", [128, 8, 256], BF16)
            B_pm = Buf("pm")
            mT = sbc("mT", [128, KC, TOWN], BF16)
            B_mT = [Buf(f"mT{i}") for i in range(KC)]
            tg = [sbc(f"tg{i}", [128, TOWN], F32) for i in range(3)]
            B_tg = [Buf(f"tg{i}") for i in range(3)]
            wb = [sbc(f"wb{i}", [128, 8, 128], BF16) for i in range(4)]
            B_wb = [Buf(f"wb{i}") for i in range(4)]
            wo = [sbc(f"wo{i}", [128, KC, 512], BF16) for i in range(2)]
            B_wo = [Buf("wo0"), Buf("wo1")]
            xr = [sbc(f"xr{i}", [128, 512], F32) for i in range(2)]
            B_xr = [Buf(f"xr{i}") for i in range(2)]
            ob = [sbc(f"ob{i}", [128, 512], F32) for i in range(2)]
            B_ob = [Buf(f"ob{i}") for i in range(2)]

            cpp = [psc_(f"cpp{i}", [128, 512], F32) for i in range(2)]
            B_cpp = [Buf("cpp0"), Buf("cpp1")]
            pya = [psc_(f"pya{i}", [128, 512], F32) for i in range(2)]
            B_pya = [Buf("pya0"), Buf("pya1")]
            pyp = [psc_(f"pyp{i}", [128, 512], F32) for i in range(2)]
            B_pyp = [Buf("pyp0"), Buf("pyp1")]
            pfo = [psc_(f"pfo{i}", [128, 512], F32) for i in range(2)]
            B_pfo = [Buf("pfo0"), Buf("pfo1")]

            cc = {"pp": 0, "wb": 0, "wo": 0, "xr": 0, "ob": 0, "pfo": 0, "pya": 0, "pyp": 0}

            def nx(name, n):
                i = cc[name] % n
                cc[name] += 1
                return i

            own_mov = lambda m: (lambda kc: hTo[:, kc, 512 * m:512 * m + 512])

            S.op("pool", lambda g: g.dma_start(out=pm[:], in_=pmaps.rearrange("g (k p) e -> p (g k) e", p=128)),
                 writes=[B_pm], dsem="dc4")

            for blk in range(8):
                gq = blk // 2
                ksz = 2 ** (gq + 1)
                wt, Bw = load_wblock(CU + blk * 128)
                S.op("act", lambda a, blk=blk: a.copy(out=uext[0][:, 0:16], in_=uh[:, blk, :]),
                     reads=[B_uh], writes=[B_ue[0]])
                for m in range(2):
                    i = project(cpp, B_cpp, cc, wt, Bw, own_mov(m), 512)
                    S.op("act", lambda a, i=i, m=m: a.copy(out=uext[0][:, 16 + 512 * m:16 + 512 * m + 512],
                                                           in_=cpp[i][:]),
                         reads=[B_cpp[i]], writes=[B_ue[0]])
                cur = 0
                sh = 1
                for step in range(gq + 1):
                    nxtb = 1 if cur != 1 else 2
                    lo = 2 * sh - 1
                    S.op("dve", lambda v, cur=cur, nxtb=nxtb, lo=lo, sh=sh: v.tensor_tensor(
                        out=uext[nxtb][:, lo:UW], in0=uext[cur][:, lo:UW], in1=uext[cur][:, lo - sh:UW - sh],
                        op=ALU.add),
                        reads=[B_ue[cur]], writes=[B_ue[nxtb]])
                    cur = nxtb
                    sh *= 2
                S.op("dve", lambda v, cur=cur, blk=blk, ksz=ksz: v.scalar_tensor_tensor(
                    out=dT[:, blk, :], in0=uext[cur][:, 16:UW], scalar=1.0 / ksz, in1=uext[0][:, 16:UW],
                    op0=ALU.mult, op1=ALU.subtract),
                    reads=[B_ue[cur], B_ue[0]], writes=[B_dT[blk]])
                S.op("dve", lambda v, cur=cur, gq=gq: v.tensor_tensor(
                    out=dfix[:], in0=uext[cur][:, 16:32], in1=cst[:, C_IC + 16 * gq:C_IC + 16 * gq + 16],
                    op=ALU.mult),
                    reads=[B_ue[cur], B_cst], writes=[B_dfix])
                S.op("dve", lambda v, blk=blk: v.tensor_tensor(
                    out=dT[:, blk, 0:16], in0=dfix[:], in1=uext[0][:, 16:32], op=ALU.subtract),
                    reads=[B_dfix, B_ue[0]], writes=[B_dT[blk]])
            for eb2 in range(8):
                gq, half = eb2 // 2, eb2 % 2
                wt, Bw = load_wblock(CZP + eb2 * 128)
                for m in range(2):
                    i = project(cpp, B_cpp, cc, wt, Bw, own_mov(m), 512)
                    fi = nx("pfo", 2)
                    for k2 in range(2):
                        S.op("pe", lambda t, fi=fi, gq=gq, k2=k2, half=half, m=m: t.matmul(
                            pfo[fi][:], lhsT=pm[:, 2 * gq + k2, 128 * half:128 * half + 128],
                            rhs=dT[:, 2 * gq + k2, 512 * m:512 * m + 512], start=(k2 == 0), stop=(k2 == 1)),
                            reads=[B_pm, B_dT[2 * gq + k2]], writes=[B_pfo[fi]])
                    S.op("act", lambda a, i=i, m=m: a.activation(out=tg[0][:, 512 * m:512 * m + 512], in_=cpp[i][:],
                                                                 func=AF.Tanh, scale=0.5),
                         reads=[B_cpp[i]], writes=[B_tg[0]])
                    S.op("dve", lambda v, i=i, m=m: v.scalar_tensor_tensor(
                        out=tg[0][:, 512 * m:512 * m + 512], in0=tg[0][:, 512 * m:512 * m + 512], scalar=1.0,
                        in1=cpp[i][:], op0=ALU.add, op1=ALU.mult),
                        reads=[B_tg[0], B_cpp[i]], writes=[B_tg[0]])
                    S.op("dve", lambda v, fi=fi, eb2=eb2, m=m: v.scalar_tensor_tensor(
                        out=pT[:, eb2, 512 * m:512 * m + 512], in0=pfo[fi][:], scalar=pshalf[:, eb2:eb2 + 1],
                        in1=tg[0][:, 512 * m:512 * m + 512], op0=ALU.mult, op1=ALU.mult),
                        reads=[B_pfo[fi], B_tg[0], B_misc], writes=[B_pT[eb2]])

            w_ba_v = w_ba.rearrange("(h p) n -> p h n", p=128)
            w_bp_v = w_bp.rearrange("(h p) n -> p h n", p=128)
            for db in range(KC):
                wa_i = nx("wb", 4)
                S.op("pool", lambda g, wa_i=wa_i, db=db: g.dma_start(out=wb[wa_i][:],
                                                                      in_=w_ba_v[:, :, 128 * db:128 * db + 128]),
                     writes=[B_wb[wa_i]], dsem=f"db{wa_i}")
                wp_i = nx("wb", 4)
                S.op("pool", lambda g, wp_i=wp_i, db=db: g.dma_start(out=wb[wp_i][:],
                                                                      in_=w_bp_v[:, :, 128 * db:128 * db + 128]),
                     writes=[B_wb[wp_i]], dsem=f"db{wp_i}")
                for which, cbase in ((0, CG), (1, CG + D)):
                    wt, Bw = load_wblock(cbase + db * 128)
                    tgi = 1 + which
                    for m in range(2):
                        i = project(cpp, B_cpp, cc, wt, Bw, own_mov(m), 512)
                        bcol = which * KC + db
                        S.op("act", lambda a, i=i, m=m, tgi=tgi, bcol=bcol: a.activation(
                            out=tg[tgi][:, 512 * m:512 * m + 512], in_=cpp[i][:], func=AF.Tanh,
                            bias=halfb[:, bcol:bcol + 1], scale=0.5),
                            reads=[B_cpp[i], B_misc], writes=[B_tg[tgi]])
                for m in range(2):
                    ya = nx("pya", 2)
                    for h in range(NHEAD):
                        S.op("pe", lambda t, ya=ya, h=h, m=m, wa_i=wa_i: t.matmul(
                            pya[ya][:], lhsT=wb[wa_i][:, h, :], rhs=aT[:, h, 512 * m:512 * m + 512],
                            start=(h == 0), stop=(h == NHEAD - 1)),
                            reads=[B_wb[wa_i], B_aT[h]], writes=[B_pya[ya]])
                    yp = nx("pyp", 2)
                    for e in range(8):
                        S.op("pe", lambda t, yp=yp, e=e, m=m, wp_i=wp_i: t.matmul(
                            pyp[yp][:], lhsT=wb[wp_i][:, e, :], rhs=pT[:, e, 512 * m:512 * m + 512],
                            start=(e == 0), stop=(e == 7)),
                            reads=[B_wb[wp_i], B_pT[e]], writes=[B_pyp[yp]])
                    S.op("dve", lambda v, ya=ya, m=m: v.scalar_tensor_tensor(
                        out=tg[1][:, 512 * m:512 * m + 512], in0=tg[1][:, 512 * m:512 * m + 512], scalar=1.0,
                        in1=pya[ya][:], op0=ALU.add, op1=ALU.mult),
                        reads=[B_tg[1], B_pya[ya]], writes=[B_tg[1]])
                    S.op("dve", lambda v, yp=yp, m=m: v.scalar_tensor_tensor(
                        out=tg[2][:, 512 * m:512 * m + 512], in0=tg[2][:, 512 * m:512 * m + 512], scalar=1.0,
                        in1=pyp[yp][:], op0=ALU.add, op1=ALU.mult),
                        reads=[B_tg[2], B_pyp[yp]], writes=[B_tg[2]])
                S.op("dve", lambda v, db=db: v.tensor_tensor(out=mT[:, db, :], in0=tg[1][:], in1=tg[2][:], op=ALU.add),
                     reads=[B_tg[1], B_tg[2]], writes=[B_mT[db]])

            w_out_v = w_out.rearrange("(k p) n -> p k n", p=128)
            for cg in range(4):
                wi = nx("wo", 2)
                S.op("pool", lambda g, wi=wi, cg=cg: g.dma_start(out=wo[wi][:],
                                                                  in_=w_out_v[:, :, 512 * cg:512 * cg + 512]),
                     writes=[B_wo[wi]], dsem=f"dwo{wi}")
                for tt in range(8):
                    xi = nx("xr", 2)
                    S.op("act", lambda a, xi=xi, tt=tt, cg=cg: a.dma_start(
                        out=xr[xi][:], in_=xh[THALO + 128 * tt:THALO + 128 * tt + 128, 512 * cg:512 * cg + 512]),
                        writes=[B_xr[xi]], dsem=f"dr{xi}")
                    fi = nx("pfo", 2)
                    for db in range(KC):
                        S.op("pe", lambda t, fi=fi, db=db, tt=tt, wi=wi: t.matmul(
                            pfo[fi][:], lhsT=mT[:, db, 128 * tt:128 * tt + 128], rhs=wo[wi][:, db, :],
                            start=(db == 0), stop=(db == KC - 1)),
                            reads=[B_mT[db], B_wo[wi]], writes=[B_pfo[fi]])
                    oi = nx("ob", 2)
                    S.op("dve", lambda v, fi=fi, xi=xi, oi=oi: v.scalar_tensor_tensor(
                        out=ob[oi][:], in0=pfo[fi][:], scalar=0.5, in1=xr[xi][:], op0=ALU.mult, op1=ALU.add),
                        reads=[B_pfo[fi], B_xr[xi]], writes=[B_ob[oi]])
                    S.op("sp", lambda s, oi=oi, tt=tt, cg=cg: s.dma_start(
                        out=out_d[128 * tt:128 * tt + 128, 512 * cg:512 * cg + 512], in_=ob[oi][:]),
                        reads=[B_ob[oi]], dsem=f"do{oi}")
            S.emit_phase()
    return nc


def _host_consts(c):
    i = np.arange(128)[:, None]
    m = np.arange(128)[None, :]
    cur = (i <= m).astype(np.float32)
    prev = (i >= m).astype(np.float32)
    halo = prev if c > 0 else np.zeros_like(prev)
    msk = np.zeros((128, NMSK), np.float32)
    msk[:, M_A:M_A + 512] = np.concatenate([halo, cur, prev, cur], axis=1)
    msk[:, M_B:M_B + 512] = np.concatenate([prev, cur, prev, cur], axis=1)
    i64 = np.arange(128)[:, None]
    m64 = np.arange(64)[None, :]
    if c == 0:
        halo3 = np.zeros((128, 64), np.float32)
    elif c == 1:
        halo3 = ((i64 >= m64) & (i64 >= 64)).astype(np.float32)
    else:
        halo3 = (i64 >= m64).astype(np.float32)
    bd = np.zeros((128, 128), np.float32)
    c64 = (np.arange(64)[:, None] <= np.arange(64)[None, :]).astype(np.float32)
    bd[0:64, 0:64] = c64
    bd[64:128, 64:128] = c64
    one = np.concatenate([halo3, halo3, bd], axis=1)
    msk[:, M_C:M_C + 512] = np.concatenate([one, one], axis=1)
    msk[:, M_ID:M_ID + 128] = np.eye(128, dtype=np.float32)
    rt = np.zeros((32, 32), np.float32)
    for cc in range(16):
        rt[cc + 16, cc] = -1.0
        rt[cc, cc + 16] = 1.0
    msk[0:32, M_RT:M_RT + 32] = rt
    start = 1024 * c
    pos = np.arange(start - THALO, start + TOWN).astype(np.float64)
    inv = 500000.0 ** (-np.arange(0, 32, 2, dtype=np.float64) / 32.0)
    ang = (pos[None, :].astype(np.float32) * inv[:, None].astype(np.float32)).astype(np.float64)
    tab = np.zeros((32, 2, NTOK), np.float32)
    tab[0:16, 0] = np.cos(ang)
    tab[16:32, 0] = np.cos(ang)
    tab[0:16, 1] = -np.sin(ang)
    tab[16:32, 1] = np.sin(ang)
    ic = np.zeros((4, 16), np.float32)
    for gq in range(4):
        k = 2 ** (gq + 1)
        if c == 0:
            ic[gq] = 1.0 / np.minimum(np.arange(16) + 1, k)
        else:
            ic[gq] = 1.0 / k
    return msk, tab, ic


_NC_CACHE = {}


def kernel(x, norm_gain, w_in, b_gates, q_norm_gain, k_norm_gain, pool_maps, pool_scale,
           w_branch_attn, w_branch_pool, w_out):
    x = np.asarray(x, np.float32)
    f = lambda a: np.ascontiguousarray(np.asarray(a, np.float32))
    w_in, w_ba, w_bp, w_o, pmaps = f(w_in), f(w_branch_attn), f(w_branch_pool), f(w_out), f(pool_maps)
    gain = np.ascontiguousarray(np.broadcast_to(f(norm_gain)[None, :], (128, D)))
    if "nc" not in _NC_CACHE:
        _NC_CACHE["nc"] = build_program()
    nc = _NC_CACHE["nc"]
    in_maps = []
    for core in range(8):
        b, c = core // 4, core % 4
        start = 1024 * c
        xh = np.zeros((NTOK, D), np.float32)
        lo = start - THALO
        src_lo = max(lo, 0)
        xh[src_lo - lo:] = x[b, src_lo:start + TOWN]
        msk, tab, ic = _host_consts(c)
        cst = np.zeros((128, NCST), np.float32)
        cst[:, C_QG] = f(q_norm_gain)
        cst[:, C_KG] = f(k_norm_gain)
        cst[:, C_PS:C_PS + 8] = f(pool_scale).reshape(8, 128).T
        cst[:, C_BG:C_BG + 32] = f(b_gates).reshape(32, 128).T
        cst[:, C_IC:C_IC + 64] = ic.reshape(1, 64)
        in_maps.append({"xh": xh, "w_in": w_in, "w_ba": w_ba, "w_bp": w_bp, "w_out": w_o, "pmaps": pmaps,
                        "gain": gain, "cst": cst, "msk": msk, "tab": tab})
    res = run_bass_kernel_spmd(nc, in_maps, core_ids=list(range(8)))
    out = np.zeros((2, SEQ, D), np.float32)
    for core in range(8):
        b, c = core // 4, core % 4
        out[b, 1024 * c:1024 * c + 1024] = res.results[core]["out"]
    return out
```

```python
import numpy as np
import concourse.bass as bass
import concourse.mybir as mybir
from concourse.bass_utils import run_bass_kernel_spmd

F32 = mybir.dt.float32
BF16 = mybir.dt.bfloat16
AF = mybir.ActivationFunctionType
ALU = mybir.AluOpType

D = 2048
KC = 16
TOWN = 1024
THALO = 2048
NTOK = TOWN + THALO
SEQ = 4096
EPS = 1e-6
NHEAD = 8
SQ128 = float(np.sqrt(128.0))

CQ, CK, CV, CZA, CU, CZP, CG = 0, 3072, 6144, 9216, 10240, 11264, 12288

C_QG, C_KG, C_PS, C_BG, C_IC = 0, 1, 2, 10, 42
NCST = 42 + 64
M_A, M_B, M_C, M_ID, M_RT = 0, 512, 1024, 1536, 1664
NMSK = 1696


class Buf:
    __slots__ = ("name", "w", "r")

    def __init__(self, name):
        self.name = name
        self.w = None
        self.r = {}


class Op:
    __slots__ = ("eng", "fn", "deps", "signal", "semval", "key", "seq", "isdma", "phase")


class Sched:
    ENGS = ("pe", "act", "dve", "pool", "sp")

    def __init__(self, nc, sems):
        self.nc = nc
        self.sems = sems
        self.streams = {e: [] for e in self.ENGS}
        self.seqctr = {k: 0 for k in sems}
        self.semcnt = {k: 0 for k in sems}
        self.waited = {e: {k: 0 for k in sems} for e in self.ENGS}
        self.lastop = {k: None for k in sems}
        self.phase = 0

    def op(self, eng, fn, reads=(), writes=(), dsem=None):
        o = Op()
        o.eng = eng
        o.fn = fn
        o.signal = False
        o.semval = None
        o.phase = self.phase
        o.isdma = dsem is not None
        o.key = dsem if dsem is not None else eng
        o.seq = self.seqctr[o.key]
        self.seqctr[o.key] += 1
        deps = {}

        def add(d):
            if d is None or d is o:
                return
            cur = deps.get(d.key)
            if cur is None or cur.seq < d.seq:
                deps[d.key] = d

        for b in reads:
            add(b.w)
        for b in writes:
            add(b.w)
            for d in b.r.values():
                add(d)
        for b in reads:
            b.r[o.key] = o
        for b in writes:
            b.w = o
            b.r = {}
        o.deps = deps
        self.streams[eng].append(o)
        self.lastop[o.key] = o
        return o

    def emit_phase(self):
        nc = self.nc
        streams = self.streams
        for e in self.ENGS:
            for o in streams[e]:
                if o.isdma:
                    o.signal = True
                for k, d in o.deps.items():
                    if d.phase != self.phase or (d.eng == "pe" and o.eng == "pe" and not d.isdma):
                        continue
                    d.signal = True
        for k, o in self.lastop.items():
            if o is not None:
                o.signal = True
        for e in self.ENGS:
            for o in streams[e]:
                if o.signal:
                    self.semcnt[o.key] += 16 if o.isdma else 1
                    o.semval = self.semcnt[o.key]
        final = dict(self.semcnt)
        sems = self.sems
        waited = self.waited

        def run(ename, eng):
            w = waited[ename]
            for o in streams[ename]:
                for k, d in o.deps.items():
                    if d.phase != self.phase or (d.eng == "pe" and ename == "pe" and not d.isdma):
                        continue
                    if w[k] < d.semval:
                        eng.wait_ge(sems[k], d.semval)
                        w[k] = d.semval
                ins = o.fn(eng)
                if o.signal:
                    ins.then_inc(sems[o.key], 16 if o.isdma else 1)
            for k, v in final.items():
                if w[k] < v:
                    eng.wait_ge(sems[k], v)
                    w[k] = v

        with nc.Block() as block:
            @block.tensor
            def _(t):
                run("pe", t)

            @block.scalar
            def _(a):
                run("act", a)

            @block.vector
            def _(v):
                run("dve", v)

            @block.gpsimd
            def _(g):
                run("pool", g)

            @block.sync
            def _(s):
                run("sp", s)
        self.streams = {e: [] for e in self.ENGS}
        self.lastop = {k: None for k in self.sems}
        self.phase += 1


def build_program(debug=False):
    nc = bass.Bass("TRN2", target_bir_lowering=False)
    xh = nc.dram_tensor("xh", [NTOK, D], F32, kind="ExternalInput").ap()
    w_in = nc.dram_tensor("w_in", [D, 16384], F32, kind="ExternalInput").ap()
    w_ba = nc.dram_tensor("w_ba", [1024, D], F32, kind="ExternalInput").ap()
    w_bp = nc.dram_tensor("w_bp", [1024, D], F32, kind="ExternalInput").ap()
    w_out = nc.dram_tensor("w_out", [D, D], F32, kind="ExternalInput").ap()
    pmaps = nc.dram_tensor("pmaps", [4, 256, 256], F32, kind="ExternalInput").ap()
    gain = nc.dram_tensor("gain", [128, D], F32, kind="ExternalInput").ap()
    cst_d = nc.dram_tensor("cst", [128, NCST], F32, kind="ExternalInput").ap()
    msk_d = nc.dram_tensor("msk", [128, NMSK], F32, kind="ExternalInput").ap()
    tab_d = nc.dram_tensor("tab", [32, 2, NTOK], F32, kind="ExternalInput").ap()
    out_d = nc.dram_tensor("out", [TOWN, D], F32, kind="ExternalOutput").ap()

    w_in_v = w_in.rearrange("(kc p) n -> p kc n", p=128)

    from contextlib import ExitStack
    es = ExitStack()
    with es:
        def sb(name, shape, dt):
            return es.enter_context(nc.sbuf_tensor("s_" + name, shape, dt))

        semnames = (["pe", "act", "dve", "pool", "sp"] + [f"dw{i}" for i in range(4)] + [f"dx{i}" for i in range(3)]
                    + [f"do{i}" for i in range(3)] + [f"dc{i}" for i in range(5)] + [f"db{i}" for i in range(4)]
                    + [f"dwo{i}" for i in range(2)] + [f"dr{i}" for i in range(3)] + ["drot0", "drot1", "drot2", "drot3"])
        sems = {k: es.enter_context(nc.semaphore("s_" + k)) for k in semnames}
        S = Sched(nc, sems)

        hTo = sb("hTo", [128, KC, TOWN], BF16)
        aT = sb("aT", [128, NHEAD, TOWN], BF16)
        cst = sb("cst", [128, NCST], F32)
        mskb = sb("mskb", [128, NMSK], BF16)
        NW = 3
        wring = [sb(f"wr{i}", [128, KC, 128], BF16) for i in range(NW)]
        ones = sb("ones", [128, 128], BF16)
        pshalf = sb("pshalf", [128, 8], F32)
        halfb = sb("halfb", [128, 32], F32)
        uh = sb("uh", [128, 8, 16], F32)
        B_hTh, B_hTo, B_cst, B_mskb, B_ones = Buf("hTh"), Buf("hTo"), Buf("cst"), Buf("mskb"), Buf("ones")
        B_aT = [Buf(f"aT{h}") for h in range(NHEAD)]
        B_wr = [Buf(f"wr{i}") for i in range(NW)]
        B_misc = Buf("misc")
        B_uh = Buf("uh")
        wctr = [0]

        def load_wblock(col0):
            i = wctr[0] % NW
            wctr[0] += 1
            t = wring[i]
            S.op("pool", lambda g, t=t, col0=col0: g.dma_start(out=t[:], in_=w_in_v[:, :, col0:col0 + 128]),
                 writes=[B_wr[i]], dsem=f"dw{i}")
            return t, B_wr[i]

        ident = mskb[:, M_ID:M_ID + 128]
        rt = mskb[0:32, M_RT:M_RT + 32]

        own_q, bg_q, late_q = [], [], []
        pcount = [0]

        def pop_late(force=False):
            if late_q and (force or late_q[0].due <= pcount[0]):
                late_q.pop(0)()
                return True
            return False

        def pop_own():
            if pop_late():
                return True
            if own_q:
                own_q.pop(0)()
                return True
            return False

        def pop_bg():
            if bg_q:
                bg_q.pop(0)()
                return True
            return False

        def flush_own():
            while own_q or late_q:
                if own_q:
                    own_q.pop(0)()
                else:
                    pop_late(force=True)

        def need(tag):
            while any(tag in getattr(f, "res", ()) for f in own_q):
                pop_own()

        def flush_bg():
            while bg_q:
                flush_own()
                pop_bg()
            flush_own()

        eAB = ExitStack()
        hTh = eAB.enter_context(nc.sbuf_tensor("s_hTh", [128, KC, THALO], BF16))

        def project(pp, B_pp, ctrd, wt, Bw, mov, n):
            i = ctrd["pp"] % len(pp)
            ctrd["pp"] += 1
            pcount[0] += 1
            need(("pp", i))
            while len(own_q) > 4:
                pop_own()
            for kc in range(KC):
                S.op("pe", lambda t, i=i, kc=kc, mov=mov, wt=wt, n=n: t.matmul(
                    pp[i][:, 0:n], lhsT=wt[:, kc, :], rhs=mov(kc), start=(kc == 0), stop=(kc == KC - 1)),
                    reads=[Bw, B_hTh, B_hTo], writes=[B_pp[i]])
                if kc in (3, 7, 11, 15):
                    pop_own()
                elif kc in (1, 5, 9, 13):
                    pop_bg()
            return i

        with ExitStack() as ea:
            def sba(name, shape, dt):
                return ea.enter_context(nc.sbuf_tensor("a_" + name, shape, dt))

            xt = [sba(f"xt{i}", [128, D], F32) for i in range(3)]
            xn = [sba(f"xn{i}", [128, D], BF16) for i in range(2)]
            junk = sba("junk", [128, D], BF16)
            gb = sba("gb", [128, D], F32)
            mskf = sba("mskf", [128, NMSK], F32)
            stat = sba("stat", [128, 3 * 24], F32)
            tpa = [ea.enter_context(nc.psum_tensor(f"tpa{i}", [128, 8, 128], BF16)) for i in range(2)]
            B_xt = [Buf("xt0"), Buf("xt1"), Buf("xt2")]
            B_xn = [Buf("xn0"), Buf("xn1")]
            B_junk, B_gb, B_mskf, B_stat = Buf("junk"), Buf("gb"), Buf("mskf"), Buf("stat")
            B_st = [Buf(f"st{j}") for j in range(NTOK // 128)]
            B_tpa = [Buf("tpa0"), Buf("tpa1")]

            S.op("sp", lambda s: s.dma_start(out=cst[:], in_=cst_d), writes=[B_cst], dsem="dc0")
            S.op("sp", lambda s: s.dma_start(out=mskf[:], in_=msk_d), writes=[B_mskf], dsem="dc1")
            S.op("sp", lambda s: s.dma_start(out=gb[:], in_=gain), writes=[B_gb], dsem="dc2")
            S.op("dve", lambda v: v.tensor_copy(out=mskb[:], in_=mskf[:]), reads=[B_mskf], writes=[B_mskb])
            S.op("dve", lambda v: v.memset(ones[:], 1.0), writes=[B_ones])
            S.op("dve", lambda v: v.memset(stat[:], 0.0), writes=[B_stat])
            S.op("dve", lambda v: v.tensor_scalar_mul(out=pshalf[:], in0=cst[:, C_PS:C_PS + 8], scalar1=0.5),
                 reads=[B_cst], writes=[B_misc])
            S.op("dve", lambda v: v.tensor_scalar_mul(out=halfb[:], in0=cst[:, C_BG:C_BG + 32], scalar1=0.5),
                 reads=[B_cst], writes=[B_misc])
            S.op("dve", lambda v: v.tensor_scalar_mul(out=gb[:], in0=gb[:], scalar1=float(np.sqrt(D))),
                 reads=[B_gb], writes=[B_gb])

            NTT = NTOK // 128

            def a_front(j):
                b, b3, c0 = j % 2, j % 3, 3 * j
                S.op("sp", lambda s: s.dma_start(out=xt[b3][:], in_=xh[j * 128:(j + 1) * 128, :]),
                     writes=[B_xt[b3]], dsem=f"dx{b3}")
                S.op("act", lambda a: a.activation(out=junk[:], in_=xt[b3][:], func=AF.Square,
                                                   accum_out=stat[:, c0:c0 + 1]),
                     reads=[B_xt[b3], B_stat], writes=[B_junk, B_st[j]])
                S.op("act", lambda a: a.activation(out=stat[:, c0 + 1:c0 + 2], in_=stat[:, c0:c0 + 1],
                                                   func=AF.Ln, bias=float(D * EPS)),
                     reads=[B_st[j]], writes=[B_st[j]])
                S.op("act", lambda a: a.activation(out=stat[:, c0 + 2:c0 + 3], in_=stat[:, c0 + 1:c0 + 2],
                                                   func=AF.Exp, scale=-0.5),
                     reads=[B_st[j]], writes=[B_st[j]])
                S.op("dve", lambda v: v.scalar_tensor_tensor(
                    out=xn[b][:], in0=xt[b3][:], scalar=stat[:, c0 + 2:c0 + 3], in1=gb[:],
                    op0=ALU.mult, op1=ALU.mult),
                    reads=[B_xt[b3], B_st[j], B_gb], writes=[B_xn[b]])

            def a_back(j):
                b = j % 2
                for half in range(2):
                    for k8 in range(8):
                        kc = half * 8 + k8
                        S.op("pe", lambda t, half=half, k8=k8, kc=kc: t.transpose(
                            tpa[half][:, k8, :], xn[b][:, kc * 128:(kc + 1) * 128], ident),
                            reads=[B_xn[b], B_mskb], writes=[B_tpa[half]])
                    if j < 16:
                        dst = hTh[:, half * 8:(half + 1) * 8, j * 128:(j + 1) * 128]
                        bd = B_hTh
                    else:
                        dst = hTo[:, half * 8:(half + 1) * 8, (j - 16) * 128:(j - 15) * 128]
                        bd = B_hTo
                    if half == 0:
                        S.op("act", lambda a, dst=dst, half=half: a.copy(out=dst, in_=tpa[half][:]),
                             reads=[B_tpa[half]], writes=[bd])
                    else:
                        S.op("dve", lambda v, dst=dst, half=half: v.tensor_copy(out=dst, in_=tpa[half][:]),
                             reads=[B_tpa[half]], writes=[bd])

            for j in range(NTT + 1):
                if j < NTT:
                    a_front(j)
                if j >= 1:
                    a_back(j - 1)
            S.emit_phase()

        with ExitStack() as eb:
            def sbb(name, shape, dt):
                return eb.enter_context(nc.sbuf_tensor("b_" + name, shape, dt))

            def psb(name, shape, dt):
                return eb.enter_context(nc.psum_tensor("bp_" + name, shape, dt))

            tab = sbb("tab", [32, 2, NTOK], BF16)
            B_tab = Buf("tab")
            NU = 2
            kS = [sbb(f"kS{i}", [128, 3072], BF16) for i in range(NU)]
            qS = [sbb(f"qS{i}", [128, 1024], BF16) for i in range(NU)]
            vS = [sbb(f"vS{i}", [128, 24, 128], BF16) for i in range(NU)]
            B_kS = [Buf(f"kS{i}") for i in range(NU)]
            B_qS = [Buf(f"qS{i}") for i in range(NU)]
            B_vS = [Buf(f"vS{i}") for i in range(NU)]
            vT = [sbb(f"vT{i}", [128, 512], BF16) for i in range(2)]
            B_vT = [Buf("vT0"), Buf("vT1")]
            vT3 = sbb("vT3", [128, 3072], BF16)
            B_vT3 = Buf("vT3")
            sq = [sbb(f"sq{i}", [128, 512], BF16) for i in range(2)]
            B_sq = [Buf("sq0"), Buf("sq1")]
            rr = [sbb(f"rr{i}", [128, 512], F32) for i in range(2)]
            B_rr = [Buf("rr0"), Buf("rr1")]
            rotbs = [sbb(f"rotb{i}", [32, 1536], BF16) for i in range(2)]
            B_rots = [[Buf(f"rot{i}a"), Buf(f"rot{i}b")] for i in range(2)]
            NE = 3
            Eb = [sbb(f"E{i}", [128, 512], BF16) for i in range(NE)]
            B_E = [Buf(f"E{i}") for i in range(NE)]
            NUMs = sbb("NUM", [128, TOWN], F32)
            DENs = sbb("DEN", [128, TOWN], F32)
            szs = sbb("sz", [128, TOWN], F32)
            ez = [sbb(f"ez{i}", [128, 512], F32) for i in range(2)]
            B_NUM, B_DEN, B_sz = Buf("NUM"), Buf("DEN"), Buf("sz")
            B_ez = [Buf("ez0"), Buf("ez1")]

            pp = [psb(f"pp{i}", [128, 512], F32) for i in range(3)]
            B_pp = [Buf("pp0"), Buf("pp1"), Buf("pp2")]
            prot = psb("prot", [128, 512], F32)
            B_prot = Buf("prot")
            pS = prot
            B_pS = B_prot
            paux = prot[:, :].bitcast(BF16)
            B_paux = B_prot
            psc = [psb(f"psc{i}", [128, 512], F32) for i in range(2)]
            B_psc = [Buf("psc0"), Buf("psc1")]
            pnds = [psb(f"pnd{i}", [128, 512], F32) for i in range(2)]
            B_pnds = [Buf("pnd0"), Buf("pnd1")]

            S.op("pool", lambda g: g.dma_start(out=tab[:], in_=tab_d), writes=[B_tab], dsem="dc3")

            ctr = {"pp": 0, "vT": 0, "sq": 0, "rr": 0, "rtmp": 0, "E": 0, "psc": 0, "ez": 0, "pnd": 0, "rot": 0}

            def nxt(name, n):
                i = ctr[name] % n
                ctr[name] += 1
                return i

            hTh_v16 = lambda kc: hTh[:, kc, :].rearrange("p (l r) -> p r l", r=16)
            hTo_v16 = lambda kc: hTo[:, kc, :].rearrange("p (l r) -> p r l", r=16)

            def tabv(cs, lo, hi, r):
                a = tab[:, cs, lo:hi]
                if r > 1:
                    a = a.rearrange("p (l r) -> p r l", r=r)
                return a

            def groups_for(g, kind, u):
                res = []
                ks, qs, vs = kS[u], qS[u], vS[u]
                if g == 0:
                    lst = []
                    if kind != "q":
                        lst.append(("h", 1920, 2048))
                    lst += [("o", 0, 512), ("o", 512, 1024)]
                    for (wh, lo, hi) in lst:
                        n = hi - lo
                        if wh == "h":
                            mov = lambda kc, lo=lo, hi=hi: hTh[:, kc, lo:hi]
                            tlo = lo
                            kd = ks[:, 0:128]
                            vt0 = 0
                        else:
                            mov = lambda kc, lo=lo, hi=hi: hTo[:, kc, lo:hi]
                            tlo = THALO + lo
                            kd = ks[:, 128 + lo:128 + hi]
                            vt0 = 1 + lo // 128
                        dst = kd if kind == "k" else (qs[:, lo:hi] if kind == "q" else None)
                        res.append(dict(mov=mov, n=n, r=1, dst=dst, cos=tabv(0, tlo, tlo + n, 1),
                                        sin=tabv(1, tlo, tlo + n, 1), vdst=vs[:, vt0:vt0 + n // 128, :]))
                elif g == 1:
                    ks3 = ks[:, 0:1536].rearrange("p (r l) -> p r l", r=4)
                    qs3 = qs[:, 0:1024].rearrange("p (r l) -> p r l", r=4)
                    vs4 = vs[:, 0:12, :].rearrange("p (r t) c -> p r t c", r=4)
                    lst = []
                    if kind != "q":
                        lst.append(("h", 1536, 2048, 0))
                    lst += [("o", 0, 512, 1), ("o", 512, 1024, 2)]
                    for (wh, lo, hi, t) in lst:
                        if wh == "h":
                            mov = lambda kc, lo=lo, hi=hi: hTh[:, kc, lo:hi]
                            tlo = lo
                        else:
                            mov = lambda kc, lo=lo, hi=hi: hTo[:, kc, lo:hi]
                            tlo = THALO + lo
                        dst = ks3[:, :, t * 128:(t + 1) * 128] if kind == "k" else (
                            qs3[:, :, (t - 1) * 128:t * 128] if kind == "q" else None)
                        res.append(dict(mov=mov, n=512, r=4, dst=dst, cos=tabv(0, tlo, tlo + 512, 4),
                                        sin=tabv(1, tlo, tlo + 512, 4), vdst=vs4[:, :, t, :]))
                else:
                    ksh = ks[:, 0:2048].rearrange("p (r l) -> p r l", r=16)
                    kso = ks[:, 2048:3072].rearrange("p (r l) -> p r l", r=16)
                    qs3 = qs[:, 0:1024].rearrange("p (r l) -> p r l", r=16)
                    v3h = vT3[:, 0:2048].rearrange("p (r l) -> p r l", r=16)
                    v3o = vT3[:, 2048:3072].rearrange("p (r l) -> p r l", r=16)
                    if kind != "q":
                        for m in range(4):
                            mov = lambda kc, m=m: hTh[:, kc, 512 * m:512 * m + 512]
                            res.append(dict(mov=mov, n=512, r=16, dst=ksh[:, :, 32 * m:32 * m + 32],
                                            cos=tabv(0, 512 * m, 512 * m + 512, 16),
                                            sin=tabv(1, 512 * m, 512 * m + 512, 16),
                                            v3dst=v3h[:, :, 32 * m:32 * m + 32], last=False))
                    for m in range(2):
                        mov = lambda kc, m=m: hTo[:, kc, 512 * m:512 * m + 512]
                        dst = kso[:, :, 32 * m:32 * m + 32] if kind == "k" else qs3[:, :, 32 * m:32 * m + 32]
                        tlo = THALO + 512 * m
                        res.append(dict(mov=mov, n=512, r=16, dst=dst,
                                        cos=tabv(0, tlo, tlo + 512, 16), sin=tabv(1, tlo, tlo + 512, 16),
                                        v3dst=v3o[:, :, 32 * m:32 * m + 32], last=(m == 1)))
                return res

            def viewd(ap2, r):
                return ap2 if r == 1 else ap2.rearrange("p (l r) -> p r l", r=r)

            def view3(ap2, r):
                return ap2 if r == 1 else ap2.rearrange("p (r l) -> p r l", r=r)

            def post_qk(i, gd, gcol):
                n, r = gd["n"], gd["r"]
                P = pp[i][:, 0:n]
                BD = gd["B"]
                dst = gd["dst"]
                dst32 = dst[0:32]
                si = nxt("sq", 2)
                need(("sq", si))
                S.op("act", lambda a: a.activation(out=sq[si][:, 0:n], in_=P, func=AF.Square),
                     reads=[B_pp[i]], writes=[B_sq[si]])

                def stage_b():
                    S.op("pe", lambda t: t.matmul(pS[:, 0:n], lhsT=ones[:], rhs=sq[si][:, 0:n], start=True, stop=True),
                         reads=[B_sq[si], B_ones], writes=[B_pS])
                    ri = nxt("rr", 2)
                    S.op("act", lambda a: a.activation(out=rr[ri][:, 0:n], in_=pS[:, 0:n], func=AF.Ln,
                                                       bias=float(128 * EPS)),
                         reads=[B_pS], writes=[B_rr[ri]])
                    S.op("act", lambda a: a.activation(out=rr[ri][:, 0:n], in_=rr[ri][:, 0:n], func=AF.Exp, scale=-0.5),
                         reads=[B_rr[ri]], writes=[B_rr[ri]])
                    S.op("dve", lambda v: v.scalar_tensor_tensor(
                        out=dst, in0=viewd(P, r), scalar=cst[:, gcol:gcol + 1], in1=viewd(rr[ri][:, 0:n], r),
                        op0=ALU.mult, op1=ALU.mult),
                        reads=[B_pp[i], B_rr[ri], B_cst], writes=[BD])

                stage_b.res = (("pp", i), ("sq", si))
                own_q.append(stage_b)

            def post_v(i, gd, g):
                n, r = gd["n"], gd["r"]
                BD = gd["B"]
                if g < 2:
                    vi = nxt("vT", 2)
                    need(("vT", vi))
                    S.op("act", lambda a: a.copy(out=view3(vT[vi][:, 0:n], r), in_=viewd(pp[i][:, 0:n], r)),
                         reads=[B_pp[i]], writes=[B_vT[vi]])
                    nch = n // 128
                    vdst = gd["vdst"]

                    def stage_b():
                        for c in range(nch):
                            S.op("pe", lambda t, c=c: t.transpose(paux[:, c * 128:(c + 1) * 128],
                                                                  vT[vi][:, c * 128:(c + 1) * 128], ident),
                                 reads=[B_vT[vi], B_mskb], writes=[B_paux])
                        src = paux[:, 0:n].rearrange("p (t c) -> p t c", c=128)
                        S.op("act", lambda a: a.copy(out=vdst, in_=src), reads=[B_paux], writes=[BD])

                    stage_b.res = (("vT", vi),)
                    own_q.append(stage_b)
                else:
                    v3dst = gd["v3dst"]
                    need(("vT3",))
                    S.op("act", lambda a: a.copy(out=v3dst, in_=viewd(pp[i][:, 0:n], r)),
                         reads=[B_pp[i]], writes=[B_vT3])
                    if gd["last"]:
                        vs = vS[gd["u"]]
                        for j in range(3):
                            def stage_t(j=j):
                                for c in range(8):
                                    S.op("pe", lambda t, c=c: t.transpose(
                                        paux[:, c * 128:(c + 1) * 128],
                                        vT3[:, (8 * j + c) * 128:(8 * j + c + 1) * 128], ident),
                                        reads=[B_vT3, B_mskb], writes=[B_paux])
                                src = paux[:, 0:1024].rearrange("p (t c) -> p t c", c=128)
                                S.op("act", lambda a: a.copy(out=vs[:, 8 * j:8 * j + 8, :], in_=src),
                                     reads=[B_paux], writes=[BD])
                            stage_t.res = (("vT3",),)
                            own_q.append(stage_t)

            def queue_rope(kind, g, u):
                st = kS[u] if kind == "k" else qS[u]
                BD = B_kS[u] if kind == "k" else B_qS[u]
                if kind == "q":
                    chunks = [(0, 1024, 2048, 3072, (1, 4, 16)[g])]
                elif g == 0:
                    chunks = [(0, 1152, 1920, 3072, 1)]
                elif g == 1:
                    chunks = [(0, 1536, 1536, 3072, 4)]
                else:
                    chunks = [(0, 1024, 0, 2048, 16, 0), (1024, 2048, 0, 2048, 16, 1), (2048, 3072, 2048, 3072, 16)]
                for ch in chunks:
                    def stage_r(ch=ch):
                        c0, c1, t0, t1, r = ch[:5]
                        n = c1 - c0
                        ri = ctr["rot"] % 2
                        while any(getattr(f, "rot", None) == ri for f in late_q):
                            pop_late(force=True)
                        ctr["rot"] += 1
                        rotb_ = rotbs[ri]
                        Br = B_rots[ri]
                        flat = st[0:32, c0:c1]
                        dv = flat if r == 1 else flat.rearrange("p (r l) -> p r l", r=(8 if len(ch) == 6 else r))
                        rb = rotb_[:, 0:n]
                        rv = rb if r == 1 else rb.rearrange("p (r l) -> p r l", r=(8 if len(ch) == 6 else r))

                        def tv(cs):
                            a_ = tab[:, cs, t0:t1]
                            if r > 1:
                                a_ = a_.rearrange("p (l r) -> p r l", r=r)
                            if len(ch) == 6:
                                a_ = a_[:, 8 * ch[5]:8 * ch[5] + 8, :]
                            return a_
                        S.op("sp", lambda s_: s_.dma_start(out=rotb_[0:16, 0:n], in_=st[16:32, c0:c1]),
                             reads=[BD], writes=[Br[0]], dsem=f"drot{2 * ri}")
                        S.op("sp", lambda s_: s_.dma_start(out=rotb_[16:32, 0:n], in_=st[0:16, c0:c1]),
                             reads=[BD], writes=[Br[1]], dsem=f"drot{2 * ri + 1}")

                        def stage_r2():
                            S.op("dve", lambda v: v.tensor_tensor(out=dv, in0=dv, in1=tv(0), op=ALU.mult),
                                 reads=[BD, B_tab, Br[0], Br[1]], writes=[BD])
                            S.op("dve", lambda v: v.tensor_tensor(out=rv, in0=rv, in1=tv(1), op=ALU.mult),
                                 reads=[Br[0], Br[1], B_tab], writes=[Br[0], Br[1]])
                            S.op("dve", lambda v: v.tensor_tensor(out=dv, in0=dv, in1=rv, op=ALU.add),
                                 reads=[BD, Br[0], Br[1]], writes=[BD])
                        stage_r2.due = pcount[0] + 2
                        stage_r2.rot = ri
                        late_q.append(stage_r2)
                    own_q.append(stage_r)

            MA = mskb[:, M_A:M_A + 512]
            MB = mskb[:, M_B:M_B + 512]
            MC = mskb[:, M_C:M_C + 512]

            def attention(g, u, first, unit_id):
                ks, qs, vs = kS[u], qS[u], vS[u]
                Bk, Bq, Bv = B_kS[u], B_qS[u], B_vS[u]
                batches = []
                if g == 0:
                    for bi in range(4):
                        sc, pv = [], []
                        for qq in range(2):
                            i = 2 * bi + qq + 1
                            qap = qs[:, 128 * (i - 1):128 * i]
                            e0 = qq * 256
                            sc.append((ks[:, 128 * (i - 1):128 * i], qap, e0, 128))
                            sc.append((ks[:, 128 * i:128 * (i + 1)], qap, e0 + 128, 128))
                            pv.append((qq * 128,
                                       [(vs[:, i - 1, :], e0, 0, 128), (vs[:, i, :], e0 + 128, 0, 128)]))
                        batches.append((sc, MA if bi == 0 else MB, pv))
                    numview = lambda t, p: t[:, 256 * p:256 * p + 256]
                    pview = lambda t: t
                elif g == 1:
                    ks3 = ks[:, 0:1536].rearrange("p (r l) -> p r l", r=4)
                    qs3 = qs[:, 0:1024].rearrange("p (r l) -> p r l", r=4)
                    vs4 = vs[:, 0:12, :].rearrange("p (r t) c -> p r t c", r=4)
                    for r in range(4):
                        sc, pv = [], []
                        for qq in range(2):
                            i = qq + 1
                            qap = qs3[:, r, 128 * (i - 1):128 * i]
                            e0 = qq * 256
                            sc.append((ks3[:, r, 128 * (i - 1):128 * i], qap, e0, 128))
                            sc.append((ks3[:, r, 128 * i:128 * (i + 1)], qap, e0 + 128, 128))
                            pv.append((qq * 128,
                                       [(vs4[:, r, i - 1, :], e0, 0, 128), (vs4[:, r, i, :], e0 + 128, 0, 128)]))
                        batches.append((sc, MA, pv))
                    numview = lambda t, p: t[:].rearrange("p (l r) -> p r l", r=4)[:, p, :]
                    pview = lambda t: t
                else:
                    ksh = ks[:, 0:2048].rearrange("p (r l) -> p r l", r=16)
                    kso = ks[:, 2048:3072]
                    qso = qs[:, 0:1024]
                    for bi in range(4):
                        sc, pv = [], []
                        for pq in range(2):
                            pi = 2 * bi + pq
                            e0 = pq * 256
                            r0, r1 = 2 * pi, 2 * pi + 1
                            sc.append((ksh[:, r0, :], qso[:, 64 * r0:64 * r0 + 64], e0, 64))
                            sc.append((ksh[:, r1, :], qso[:, 64 * r1:64 * r1 + 64], e0 + 64, 64))
                            sc.append((kso[:, 128 * pi:128 * pi + 128], qso[:, 128 * pi:128 * pi + 128], e0 + 128, 128))
                            pv.append((pq * 128,
                                       [(vs[:, 16 + pi, :], e0 + 128, 0, 128),
                                        (vs[:, r0, :], e0, 0, 64),
                                        (vs[:, r1, :], e0 + 64, 64, 64)]))
                        batches.append((sc, MC, pv))
                    numview = lambda t, p: t[:].rearrange("p (l r) -> p r l", r=16)[:, 4 * p:4 * p + 4, :]
                    pview = lambda t: t.rearrange("p (r l) -> p r l", r=4)

                state = {}

                def mk_s(bi):
                    sc, mask, pv = batches[bi]

                    def st():
                        si = nxt("psc", 2)
                        for (kap, qap, e0, nq) in sc:
                            S.op("pe", lambda t, kap=kap, qap=qap, e0=e0, nq=nq: t.matmul(
                                psc[si][:, e0:e0 + nq], lhsT=kap, rhs=qap, start=True, stop=True),
                                reads=[Bk, Bq], writes=[B_psc[si]])
                        ei = nxt("E", NE)
                        state[bi] = ei
                        S.op("act", lambda a: a.activation(out=Eb[ei][:], in_=psc[si][:], func=AF.Exp, scale=SQ128),
                             reads=[B_psc[si]], writes=[B_E[ei]])
                        S.op("dve", lambda v: v.tensor_tensor(out=Eb[ei][:], in0=Eb[ei][:], in1=mask, op=ALU.mult),
                             reads=[B_E[ei], B_mskb], writes=[B_E[ei]])
                    return st

                def mk_p(bi):
                    sc, mask, pv = batches[bi]

                    def st():
                        ei = state[bi]
                        pi_ = nxt("pnd", 2)
                        pnd, B_pnd = pnds[pi_], B_pnds[pi_]
                        for (c0, jobs) in pv:
                            for which in range(2):
                                nj = len(jobs)
                                for ji, (vap, e0, oc, on) in enumerate(jobs):
                                    lhs = vap if which == 0 else ones[:]
                                    cb = 256 * which + c0 + oc
                                    S.op("pe", lambda t, cb=cb, on=on, lhs=lhs, e0=e0, ji=ji, nj=nj:
                                         t.matmul(pnd[:, cb:cb + on], lhsT=lhs, rhs=Eb[ei][:, e0:e0 + on],
                                                  start=(ji == 0), stop=(ji == nj - 1)),
                                         reads=[B_E[ei], Bv, B_ones], writes=[B_pnd])
                        p = bi
                        if first:
                            S.op("act", lambda a: a.copy(out=numview(NUMs, p), in_=pview(pnd[:, 0:256])),
                                 reads=[B_pnd], writes=[B_NUM])
                            S.op("act", lambda a: a.copy(out=numview(DENs, p), in_=pview(pnd[:, 256:512])),
                                 reads=[B_pnd], writes=[B_DEN])
                        else:
                            S.op("dve", lambda v: v.tensor_tensor(out=numview(NUMs, p), in0=pview(pnd[:, 0:256]),
                                                                  in1=numview(NUMs, p), op=ALU.add),
                                 reads=[B_pnd, B_NUM], writes=[B_NUM])
                            S.op("dve", lambda v: v.tensor_tensor(out=numview(DENs, p), in0=pview(pnd[:, 256:512]),
                                                                  in1=numview(DENs, p), op=ALU.add),
                                 reads=[B_pnd, B_DEN], writes=[B_DEN])
                    return st

                def pre():
                    flush_own()
                def mk_nop():
                    def nop():
                        pass
                    return nop
                order = [mk_nop() for _ in range(16 if g == 2 else 14)] + [pre, mk_s(0), mk_s(1), mk_p(0), mk_s(2), mk_p(1), mk_s(3),
                                                        mk_p(2), mk_p(3)]
                for f in order:
                    f.unit = unit_id
                bg_q.extend(order)

            uctr = 0
            for h in range(NHEAD):
                for g in range(3):
                    u = uctr % NU
                    uctr += 1
                    first_proj = True
                    while bg_q and getattr(bg_q[0], "unit", 0) <= uctr - NU:
                        flush_own()
                        pop_bg()
                    for kind, cbase, gcol in (("k", CK, C_KG), ("v", CV, None), ("q", CQ, C_QG)):
                        col0 = cbase + g * 1024 + h * 128
                        wt, Bw = load_wblock(col0)
                        for gd in groups_for(g, kind, u):
                            gd["B"] = {"k": B_kS[u], "v": B_vS[u], "q": B_qS[u]}[kind]
                            if first_proj and uctr > NU:
                                pass
                            i = project(pp, B_pp, ctr, wt, Bw, gd["mov"], gd["n"])
                            if first_proj:
                                first_proj = False
                            gd["u"] = u
                            if kind == "v":
                                post_v(i, gd, g)
                            else:
                                post_qk(i, gd, gcol)
                        if kind != "v":
                            queue_rope(kind, g, u)
                    attention(g, u, first=(g == 0), unit_id=uctr)
                wt, Bw = load_wblock(CZA + h * 128)
                for m in range(2):
                    i = project(pp, B_pp, ctr, wt, Bw, lambda kc, m=m: hTo[:, kc, 512 * m:512 * m + 512], 512)
                    zi = nxt("ez", 2)
                    need(("ez", zi))
                    S.op("act", lambda a, i=i, zi=zi: a.activation(out=ez[zi][:], in_=pp[i][:], func=AF.Exp, scale=-1.0),
                         reads=[B_pp[i]], writes=[B_ez[zi]])

                    def stage_z(i=i, zi=zi, m=m):
                        S.op("act", lambda a: a.activation(out=ez[zi][:], in_=ez[zi][:], func=AF.Ln, bias=1.0),
                             reads=[B_ez[zi]], writes=[B_ez[zi]])
                        S.op("act", lambda a: a.activation(out=ez[zi][:], in_=ez[zi][:], func=AF.Exp, scale=-1.0),
                             reads=[B_ez[zi]], writes=[B_ez[zi]])
                        S.op("dve", lambda v: v.tensor_tensor(out=szs[:, 512 * m:512 * m + 512], in0=pp[i][:],
                                                              in1=ez[zi][:], op=ALU.mult),
                             reads=[B_ez[zi], B_pp[i]], writes=[B_sz])
                    stage_z.res = (("pp", i), ("ez", zi))
                    own_q.append(stage_z)

                def fin(h=h):
                    flush_own()
                    S.op("act", lambda a: a.activation(out=DENs[:], in_=DENs[:], func=AF.Ln),
                         reads=[B_DEN], writes=[B_DEN])
                    S.op("act", lambda a: a.activation(out=DENs[:], in_=DENs[:], func=AF.Exp, scale=-1.0),
                         reads=[B_DEN], writes=[B_DEN])
                    S.op("dve", lambda v: v.tensor_tensor(out=NUMs[:], in0=NUMs[:], in1=DENs[:], op=ALU.mult),
                         reads=[B_NUM, B_DEN], writes=[B_NUM])
                    S.op("dve", lambda v: v.tensor_tensor(out=aT[:, h, :], in0=NUMs[:], in1=szs[:], op=ALU.mult),
                         reads=[B_NUM, B_sz], writes=[B_aT[h]])
                fin.unit = uctr
                bg_q.append(fin)
            for blk in range(8):
                wt, Bw = load_wblock(CU + blk * 128)
                i = project(pp, B_pp, ctr, wt, Bw, lambda kc: hTh[:, kc, THALO - 32:THALO], 32)
                S.op("act", lambda a, i=i, blk=blk: a.copy(out=uh[:, blk, :], in_=pp[i][:, 16:32]),
                     reads=[B_pp[i]], writes=[B_uh])
            flush_bg()
            S.emit_phase()
        eAB.close()

        with ExitStack() as ec:
            def sbc(name, shape, dt):
                return ec.enter_context(nc.sbuf_tensor("c_" + name, shape, dt))

            def psc_(name, shape, dt):
                return ec.enter_context(nc.psum_tensor("cp_" + name, shape, dt))

            UW = 16 + TOWN
            uext = [sbc(f"uext{i}", [128, UW], F32) for i in range(3)]
            B_ue = [Buf(f"ue{i}") for i in range(3)]
            dT = sbc("dT", [128, 8, TOWN], BF16)
            B_dT = [Buf(f"dT{i}") for i in range(8)]
            dfix = sbc("dfix", [128, 16], F32)
            B_dfix = Buf("dfix")
            pT = sbc("pT", [128, 8, TOWN], BF16)
            B_pT = [Buf(f"pT{i}") for i in range(8)]
            pm = sbc("pm", [128, 8, 256], BF16)
            B_pm = Buf("pm")
            mT = sbc("mT", [128, KC, TOWN], BF16)
            B_mT = [Buf(f"mT{i}") for i in range(KC)]
            tg = [sbc(f"tg{i}", [128, TOWN], F32) for i in range(3)]
            B_tg = [Buf(f"tg{i}") for i in range(3)]
            wb = [sbc(f"wb{i}", [128, 8, 128], BF16) for i in range(4)]
            B_wb = [Buf(f"wb{i}") for i in range(4)]
            wo = [sbc(f"wo{i}", [128, KC, 512], BF16) for i in range(2)]
            B_wo = [Buf("wo0"), Buf("wo1")]
            xr = [sbc(f"xr{i}", [128, 512], F32) for i in range(2)]
            B_xr = [Buf(f"xr{i}") for i in range(2)]
            ob = [sbc(f"ob{i}", [128, 512], F32) for i in range(2)]
            B_ob = [Buf(f"ob{i}") for i in range(2)]

            cpp = [psc_(f"cpp{i}", [128, 512], F32) for i in range(2)]
            B_cpp = [Buf("cpp0"), Buf("cpp1")]
            pya = [psc_(f"pya{i}", [128, 512], F32) for i in range(2)]
            B_pya = [Buf("pya0"), Buf("pya1")]
            pyp = [psc_(f"pyp{i}", [128, 512], F32) for i in range(2)]
            B_pyp = [Buf("pyp0"), Buf("pyp1")]
            pfo = [psc_(f"pfo{i}", [128, 512], F32) for i in range(2)]
            B_pfo = [Buf("pfo0"), Buf("pfo1")]

            cc = {"pp": 0, "wb": 0, "wo": 0, "xr": 0, "ob": 0, "pfo": 0, "pya": 0, "pyp": 0}

            def nx(name, n):
                i = cc[name] % n
                cc[name] += 1
                return i

            own_mov = lambda m: (lambda kc: hTo[:, kc, 512 * m:512 * m + 512])

            S.op("pool", lambda g: g.dma_start(out=pm[:], in_=pmaps.rearrange("g (k p) e -> p (g k) e", p=128)),
                 writes=[B_pm], dsem="dc4")

            for blk in range(8):
                gq = blk // 2
                ksz = 2 ** (gq + 1)
                wt, Bw = load_wblock(CU + blk * 128)
                S.op("act", lambda a, blk=blk: a.copy(out=uext[0][:, 0:16], in_=uh[:, blk, :]),
                     reads=[B_uh], writes=[B_ue[0]])
                for m in range(2):
                    i = project(cpp, B_cpp, cc, wt, Bw, own_mov(m), 512)
                    S.op("act", lambda a, i=i, m=m: a.copy(out=uext[0][:, 16 + 512 * m:16 + 512 * m + 512],
                                                           in_=cpp[i][:]),
                         reads=[B_cpp[i]], writes=[B_ue[0]])
                cur = 0
                sh = 1
                for step in range(gq + 1):
                    nxtb = 1 if cur != 1 else 2
                    lo = 2 * sh - 1
                    S.op("dve", lambda v, cur=cur, nxtb=nxtb, lo=lo, sh=sh: v.tensor_tensor(
                        out=uext[nxtb][:, lo:UW], in0=uext[cur][:, lo:UW], in1=uext[cur][:, lo - sh:UW - sh],
                        op=ALU.add),
                        reads=[B_ue[cur]], writes=[B_ue[nxtb]])
                    cur = nxtb
                    sh *= 2
                S.op("dve", lambda v, cur=cur, blk=blk, ksz=ksz: v.scalar_tensor_tensor(
                    out=dT[:, blk, :], in0=uext[cur][:, 16:UW], scalar=1.0 / ksz, in1=uext[0][:, 16:UW],
                    op0=ALU.mult, op1=ALU.subtract),
                    reads=[B_ue[cur], B_ue[0]], writes=[B_dT[blk]])
                S.op("dve", lambda v, cur=cur, gq=gq: v.tensor_tensor(
                    out=dfix[:], in0=uext[cur][:, 16:32], in1=cst[:, C_IC + 16 * gq:C_IC + 16 * gq + 16],
                    op=ALU.mult),
                    reads=[B_ue[cur], B_cst], writes=[B_dfix])
                S.op("dve", lambda v, blk=blk: v.tensor_tensor(
                    out=dT[:, blk, 0:16], in0=dfix[:], in1=uext[0][:, 16:32], op=ALU.subtract),
                    reads=[B_dfix, B_ue[0]], writes=[B_dT[blk]])
            for eb2 in range(8):
                gq, half = eb2 // 2, eb2 % 2
                wt, Bw = load_wblock(CZP + eb2 * 128)
                for m in range(2):
                    i = project(cpp, B_cpp, cc, wt, Bw, own_mov(m), 512)
                    fi = nx("pfo", 2)
                    for k2 in range(2):
                        S.op("pe", lambda t, fi=fi, gq=gq, k2=k2, half=half, m=m: t.matmul(
                            pfo[fi][:], lhsT=pm[:, 2 * gq + k2, 128 * half:128 * half + 128],
                            rhs=dT[:, 2 * gq + k2, 512 * m:512 * m + 512], start=(k2 == 0), stop=(k2 == 1)),
                            reads=[B_pm, B_dT[2 * gq + k2]], writes=[B_pfo[fi]])
                    S.op("act", lambda a, i=i, m=m: a.activation(out=tg[0][:, 512 * m:512 * m + 512], in_=cpp[i][:],
                                                                 func=AF.Tanh, scale=0.5),
                         reads=[B_cpp[i]], writes=[B_tg[0]])
                    S.op("dve", lambda v, i=i, m=m: v.scalar_tensor_tensor(
                        out=tg[0][:, 512 * m:512 * m + 512], in0=tg[0][:, 512 * m:512 * m + 512], scalar=1.0,
                        in1=cpp[i][:], op0=ALU.add, op1=ALU.mult),
                        reads=[B_tg[0], B_cpp[i]], writes=[B_tg[0]])
                    S.op("dve", lambda v, fi=fi, eb2=eb2, m=m: v.scalar_tensor_tensor(
                        out=pT[:, eb2, 512 * m:512 * m + 512], in0=pfo[fi][:], scalar=pshalf[:, eb2:eb2 + 1],
                        in1=tg[0][:, 512 * m:512 * m + 512], op0=ALU.mult, op1=ALU.mult),
                        reads=[B_pfo[fi], B_tg[0], B_misc], writes=[B_pT[eb2]])

            w_ba_v = w_ba.rearrange("(h p) n -> p h n", p=128)
            w_bp_v = w_bp.rearrange("(h p) n -> p h n", p=128)
            for db in range(KC):
                wa_i = nx("wb", 4)
                S.op("pool", lambda g, wa_i=wa_i, db=db: g.dma_start(out=wb[wa_i][:],
                                                                      in_=w_ba_v[:, :, 128 * db:128 * db + 128]),
                     writes=[B_wb[wa_i]], dsem=f"db{wa_i}")
                wp_i = nx("wb", 4)
                S.op("pool", lambda g, wp_i=wp_i, db=db: g.dma_start(out=wb[wp_i][:],
                                                                      in_=w_bp_v[:, :, 128 * db:128 * db + 128]),
                     writes=[B_wb[wp_i]], dsem=f"db{wp_i}")
                for which, cbase in ((0, CG), (1, CG + D)):
                    wt, Bw = load_wblock(cbase + db * 128)
                    tgi = 1 + which
                    for m in range(2):
                        i = project(cpp, B_cpp, cc, wt, Bw, own_mov(m), 512)
                        bcol = which * KC + db
                        S.op("act", lambda a, i=i, m=m, tgi=tgi, bcol=bcol: a.activation(
                            out=tg[tgi][:, 512 * m:512 * m + 512], in_=cpp[i][:], func=AF.Tanh,
                            bias=halfb[:, bcol:bcol + 1], scale=0.5),
                            reads=[B_cpp[i], B_misc], writes=[B_tg[tgi]])
                for m in range(2):
                    ya = nx("pya", 2)
                    for h in range(NHEAD):
                        S.op("pe", lambda t, ya=ya, h=h, m=m, wa_i=wa_i: t.matmul(
                            pya[ya][:], lhsT=wb[wa_i][:, h, :], rhs=aT[:, h, 512 * m:512 * m + 512],
                            start=(h == 0), stop=(h == NHEAD - 1)),
                            reads=[B_wb[wa_i], B_aT[h]], writes=[B_pya[ya]])
                    yp = nx("pyp", 2)
                    for e in range(8):
                        S.op("pe", lambda t, yp=yp, e=e, m=m, wp_i=wp_i: t.matmul(
                            pyp[yp][:], lhsT=wb[wp_i][:, e, :], rhs=pT[:, e, 512 * m:512 * m + 512],
                            start=(e == 0), stop=(e == 7)),
                            reads=[B_wb[wp_i], B_pT[e]], writes=[B_pyp[yp]])
                    S.op("dve", lambda v, ya=ya, m=m: v.scalar_tensor_tensor(
                        out=tg[1][:, 512 * m:512 * m + 512], in0=tg[1][:, 512 * m:512 * m + 512], scalar=1.0,
                        in1=pya[ya][:], op0=ALU.add, op1=ALU.mult),
                        reads=[B_tg[1], B_pya[ya]], writes=[B_tg[1]])
                    S.op("dve", lambda v, yp=yp, m=m: v.scalar_tensor_tensor(
                        out=tg[2][:, 512 * m:512 * m + 512], in0=tg[2][:, 512 * m:512 * m + 512], scalar=1.0,
                        in1=pyp[yp][:], op0=ALU.add, op1=ALU.mult),
                        reads=[B_tg[2], B_pyp[yp]], writes=[B_tg[2]])
                S.op("dve", lambda v, db=db: v.tensor_tensor(out=mT[:, db, :], in0=tg[1][:], in1=tg[2][:], op=ALU.add),
                     reads=[B_tg[1], B_tg[2]], writes=[B_mT[db]])

            w_out_v = w_out.rearrange("(k p) n -> p k n", p=128)
            for cg in range(4):
                wi = nx("wo", 2)
                S.op("pool", lambda g, wi=wi, cg=cg: g.dma_start(out=wo[wi][:],
                                                                  in_=w_out_v[:, :, 512 * cg:512 * cg + 512]),
                     writes=[B_wo[wi]], dsem=f"dwo{wi}")
                for tt in range(8):
                    xi = nx("xr", 2)
                    S.op("act", lambda a, xi=xi, tt=tt, cg=cg: a.dma_start(
                        out=xr[xi][:], in_=xh[THALO + 128 * tt:THALO + 128 * tt + 128, 512 * cg:512 * cg + 512]),
                        writes=[B_xr[xi]], dsem=f"dr{xi}")
                    fi = nx("pfo", 2)
                    for db in range(KC):
                        S.op("pe", lambda t, fi=fi, db=db, tt=tt, wi=wi: t.matmul(
                            pfo[fi][:], lhsT=mT[:, db, 128 * tt:128 * tt + 128], rhs=wo[wi][:, db, :],
                            start=(db == 0), stop=(db == KC - 1)),
                            reads=[B_mT[db], B_wo[wi]], writes=[B_pfo[fi]])
                    oi = nx("ob", 2)
                    S.op("dve", lambda v, fi=fi, xi=xi, oi=oi: v.scalar_tensor_tensor(
                        out=ob[oi][:], in0=pfo[fi][:], scalar=0.5, in1=xr[xi][:], op0=ALU.mult, op1=ALU.add),
                        reads=[B_pfo[fi], B_xr[xi]], writes=[B_ob[oi]])
                    S.op("sp", lambda s, oi=oi, tt=tt, cg=cg: s.dma_start(
                        out=out_d[128 * tt:128 * tt + 128, 512 * cg:512 * cg + 512], in_=ob[oi][:]),
                        reads=[B_ob[oi]], dsem=f"do{oi}")
            S.emit_phase()
    return nc


def _host_consts(c):
    i = np.arange(128)[:, None]
    m = np.arange(128)[None, :]
    cur = (i <= m).astype(np.float32)
    prev = (i >= m).astype(np.float32)
    halo = prev if c > 0 else np.zeros_like(prev)
    msk = np.zeros((128, NMSK), np.float32)
    msk[:, M_A:M_A + 512] = np.concatenate([halo, cur, prev, cur], axis=1)
    msk[:, M_B:M_B + 512] = np.concatenate([prev, cur, prev, cur], axis=1)
    i64 = np.arange(128)[:, None]
    m64 = np.arange(64)[None, :]
    if c == 0:
        halo3 = np.zeros((128, 64), np.float32)
    elif c == 1:
        halo3 = ((i64 >= m64) & (i64 >= 64)).astype(np.float32)
    else:
        halo3 = (i64 >= m64).astype(np.float32)
    bd = np.zeros((128, 128), np.float32)
    c64 = (np.arange(64)[:, None] <= np.arange(64)[None, :]).astype(np.float32)
    bd[0:64, 0:64] = c64
    bd[64:128, 64:128] = c64
    one = np.concatenate([halo3, halo3, bd], axis=1)
    msk[:, M_C:M_C + 512] = np.concatenate([one, one], axis=1)
    msk[:, M_ID:M_ID + 128] = np.eye(128, dtype=np.float32)
    rt = np.zeros((32, 32), np.float32)
    for cc in range(16):
        rt[cc + 16, cc] = -1.0
        rt[cc, cc + 16] = 1.0
    msk[0:32, M_RT:M_RT + 32] = rt
    start = 1024 * c
    pos = np.arange(start - THALO, start + TOWN).astype(np.float64)
    inv = 500000.0 ** (-np.arange(0, 32, 2, dtype=np.float64) / 32.0)
    ang = (pos[None, :].astype(np.float32) * inv[:, None].astype(np.float32)).astype(np.float64)
    tab = np.zeros((32, 2, NTOK), np.float32)
    tab[0:16, 0] = np.cos(ang)
    tab[16:32, 0] = np.cos(ang)
    tab[0:16, 1] = -np.sin(ang)
    tab[16:32, 1] = np.sin(ang)
    ic = np.zeros((4, 16), np.float32)
    for gq in range(4):
        k = 2 ** (gq + 1)
        if c == 0:
            ic[gq] = 1.0 / np.minimum(np.arange(16) + 1, k)
        else:
            ic[gq] = 1.0 / k
    return msk, tab, ic


_NC_CACHE = {}


def kernel(x, norm_gain, w_in, b_gates, q_norm_gain, k_norm_gain, pool_maps, pool_scale,
           w_branch_attn, w_branch_pool, w_out):
    x = np.asarray(x, np.float32)
    f = lambda a: np.ascontiguousarray(np.asarray(a, np.float32))
    w_in, w_ba, w_bp, w_o, pmaps = f(w_in), f(w_branch_attn), f(w_branch_pool), f(w_out), f(pool_maps)
    gain = np.ascontiguousarray(np.broadcast_to(f(norm_gain)[None, :], (128, D)))
    if "nc" not in _NC_CACHE:
        _NC_CACHE["nc"] = build_program()
    nc = _NC_CACHE["nc"]
    in_maps = []
    for core in range(8):
        b, c = core // 4, core % 4
        start = 1024 * c
        xh = np.zeros((NTOK, D), np.float32)
        lo = start - THALO
        src_lo = max(lo, 0)
        xh[src_lo - lo:] = x[b, src_lo:start + TOWN]
        msk, tab, ic = _host_consts(c)
        cst = np.zeros((128, NCST), np.float32)
        cst[:, C_QG] = f(q_norm_gain)
        cst[:, C_KG] = f(k_norm_gain)
        cst[:, C_PS:C_PS + 8] = f(pool_scale).reshape(8, 128).T
        cst[:, C_BG:C_BG + 32] = f(b_gates).reshape(32, 128).T
        cst[:, C_IC:C_IC + 64] = ic.reshape(1, 64)
        in_maps.append({"xh": xh, "w_in": w_in, "w_ba": w_ba, "w_bp": w_bp, "w_out": w_o, "pmaps": pmaps,
                        "gain": gain, "cst": cst, "msk": msk, "tab": tab})
    res = run_bass_kernel_spmd(nc, in_maps, core_ids=list(range(8)))
    out = np.zeros((2, SEQ, D), np.float32)
    for core in range(8):
        b, c = core // 4, core % 4
        out[b, 1024 * c:1024 * c + 1024] = res.results[core]["out"]
    return out
```

```python
import numpy as np
import concourse.bass as bass
import concourse.mybir as mybir
from concourse.bass_utils import run_bass_kernel_spmd

F32 = mybir.dt.float32
BF16 = mybir.dt.bfloat16
AF = mybir.ActivationFunctionType
ALU = mybir.AluOpType

D = 2048
KC = 16
TOWN = 1024
THALO = 2048
NTOK = TOWN + THALO
SEQ = 4096
EPS = 1e-6
NHEAD = 8
SQ128 = float(np.sqrt(128.0))

CQ, CK, CV, CZA, CU, CZP, CG = 0, 3072, 6144, 9216, 10240, 11264, 12288

C_QG, C_KG, C_PS, C_BG, C_IC = 0, 1, 2, 10, 42
NCST = 42 + 64
M_A, M_B, M_C, M_ID, M_RT = 0, 512, 1024, 1536, 1664
NMSK = 1696


class Buf:
    __slots__ = ("name", "w", "r")

    def __init__(self, name):
        self.name = name
        self.w = None
        self.r = {}


class Op:
    __slots__ = ("eng", "fn", "deps", "signal", "semval", "key", "seq", "isdma", "phase")


class Sched:
    ENGS = ("pe", "act", "dve", "pool", "sp")

    def __init__(self, nc, sems):
        self.nc = nc
        self.sems = sems
        self.streams = {e: [] for e in self.ENGS}
        self.seqctr = {k: 0 for k in sems}
        self.semcnt = {k: 0 for k in sems}
        self.waited = {e: {k: 0 for k in sems} for e in self.ENGS}
        self.lastop = {k: None for k in sems}
        self.phase = 0

    def op(self, eng, fn, reads=(), writes=(), dsem=None):
        o = Op()
        o.eng = eng
        o.fn = fn
        o.signal = False
        o.semval = None
        o.phase = self.phase
        o.isdma = dsem is not None
        o.key = dsem if dsem is not None else eng
        o.seq = self.seqctr[o.key]
        self.seqctr[o.key] += 1
        deps = {}

        def add(d):
            if d is None or d is o:
                return
            cur = deps.get(d.key)
            if cur is None or cur.seq < d.seq:
                deps[d.key] = d

        for b in reads:
            add(b.w)
        for b in writes:
            add(b.w)
            for d in b.r.values():
                add(d)
        for b in reads:
            b.r[o.key] = o
        for b in writes:
            b.w = o
            b.r = {}
        o.deps = deps
        self.streams[eng].append(o)
        self.lastop[o.key] = o
        return o

    def emit_phase(self):
        nc = self.nc
        streams = self.streams
        for e in self.ENGS:
            for o in streams[e]:
                if o.isdma:
                    o.signal = True
                for k, d in o.deps.items():
                    if d.phase != self.phase or (d.eng == "pe" and o.eng == "pe" and not d.isdma):
                        continue
                    d.signal = True
        for k, o in self.lastop.items():
            if o is not None:
                o.signal = True
        for e in self.ENGS:
            for o in streams[e]:
                if o.signal:
                    self.semcnt[o.key] += 16 if o.isdma else 1
                    o.semval = self.semcnt[o.key]
        final = dict(self.semcnt)
        sems = self.sems
        waited = self.waited

        def run(ename, eng):
            w = waited[ename]
            for o in streams[ename]:
                for k, d in o.deps.items():
                    if d.phase != self.phase or (d.eng == "pe" and ename == "pe" and not d.isdma):
                        continue
                    if w[k] < d.semval:
                        eng.wait_ge(sems[k], d.semval)
                        w[k] = d.semval
                ins = o.fn(eng)
                if o.signal:
                    ins.then_inc(sems[o.key], 16 if o.isdma else 1)
            for k, v in final.items():
                if w[k] < v:
                    eng.wait_ge(sems[k], v)
                    w[k] = v

        with nc.Block() as block:
            @block.tensor
            def _(t):
                run("pe", t)

            @block.scalar
            def _(a):
                run("act", a)

            @block.vector
            def _(v):
                run("dve", v)

            @block.gpsimd
            def _(g):
                run("pool", g)

            @block.sync
            def _(s):
                run("sp", s)
        self.streams = {e: [] for e in self.ENGS}
        self.lastop = {k: None for k in self.sems}
        self.phase += 1


def build_program(debug=False):
    nc = bass.Bass("TRN2", target_bir_lowering=False)
    xh = nc.dram_tensor("xh", [NTOK, D], F32, kind="ExternalInput").ap()
    w_in = nc.dram_tensor("w_in", [D, 16384], F32, kind="ExternalInput").ap()
    w_ba = nc.dram_tensor("w_ba", [1024, D], F32, kind="ExternalInput").ap()
    w_bp = nc.dram_tensor("w_bp", [1024, D], F32, kind="ExternalInput").ap()
    w_out = nc.dram_tensor("w_out", [D, D], F32, kind="ExternalInput").ap()
    pmaps = nc.dram_tensor("pmaps", [4, 256, 256], F32, kind="ExternalInput").ap()
    gain = nc.dram_tensor("gain", [128, D], F32, kind="ExternalInput").ap()
    cst_d = nc.dram_tensor("cst", [128, NCST], F32, kind="ExternalInput").ap()
    msk_d = nc.dram_tensor("msk", [128, NMSK], F32, kind="ExternalInput").ap()
    tab_d = nc.dram_tensor("tab", [32, 2, NTOK], F32, kind="ExternalInput").ap()
    out_d = nc.dram_tensor("out", [TOWN, D], F32, kind="ExternalOutput").ap()

    w_in_v = w_in.rearrange("(kc p) n -> p kc n", p=128)

    from contextlib import ExitStack
    es = ExitStack()
    with es:
        def sb(name, shape, dt):
            return es.enter_context(nc.sbuf_tensor("s_" + name, shape, dt))

        semnames = (["pe", "act", "dve", "pool", "sp"] + [f"dw{i}" for i in range(4)] + [f"dx{i}" for i in range(3)]
                    + [f"do{i}" for i in range(3)] + [f"dc{i}" for i in range(5)] + [f"db{i}" for i in range(4)]
                    + [f"dwo{i}" for i in range(2)] + [f"dr{i}" for i in range(3)] + ["drot0", "drot1", "drot2", "drot3"])
        sems = {k: es.enter_context(nc.semaphore("s_" + k)) for k in semnames}
        S = Sched(nc, sems)

        hTo = sb("hTo", [128, KC, TOWN], BF16)
        aT = sb("aT", [128, NHEAD, TOWN], BF16)
        cst = sb("cst", [128, NCST], F32)
        mskb = sb("mskb", [128, NMSK], BF16)
        NW = 3
        wring = [sb(f"wr{i}", [128, KC, 128], BF16) for i in range(NW)]
        ones = sb("ones", [128, 128], BF16)
        pshalf = sb("pshalf", [128, 8], F32)
        halfb = sb("halfb", [128, 32], F32)
        uh = sb("uh", [128, 8, 16], F32)
        B_hTh, B_hTo, B_cst, B_mskb, B_ones = Buf("hTh"), Buf("hTo"), Buf("cst"), Buf("mskb"), Buf("ones")
        B_aT = [Buf(f"aT{h}") for h in range(NHEAD)]
        B_wr = [Buf(f"wr{i}") for i in range(NW)]
        B_misc = Buf("misc")
        B_uh = Buf("uh")
        wctr = [0]

        def load_wblock(col0):
            i = wctr[0] % NW
            wctr[0] += 1
            t = wring[i]
            S.op("pool", lambda g, t=t, col0=col0: g.dma_start(out=t[:], in_=w_in_v[:, :, col0:col0 + 128]),
                 writes=[B_wr[i]], dsem=f"dw{i}")
            return t, B_wr[i]

        ident = mskb[:, M_ID:M_ID + 128]
        rt = mskb[0:32, M_RT:M_RT + 32]

        own_q, bg_q, late_q = [], [], []
        pcount = [0]

        def pop_late(force=False):
            if late_q and (force or late_q[0].due <= pcount[0]):
                late_q.pop(0)()
                return True
            return False

        def pop_own():
            if pop_late():
                return True
            if own_q:
                own_q.pop(0)()
                return True
            return False

        def pop_bg():
            if bg_q:
                bg_q.pop(0)()
                return True
            return False

        def flush_own():
            while own_q or late_q:
                if own_q:
                    own_q.pop(0)()
                else:
                    pop_late(force=True)

        def need(tag):
            while any(tag in getattr(f, "res", ()) for f in own_q):
                pop_own()

        def flush_bg():
            while bg_q:
                flush_own()
                pop_bg()
            flush_own()

        eAB = ExitStack()
        hTh = eAB.enter_context(nc.sbuf_tensor("s_hTh", [128, KC, THALO], BF16))

        def project(pp, B_pp, ctrd, wt, Bw, mov, n):
            i = ctrd["pp"] % len(pp)
            ctrd["pp"] += 1
            pcount[0] += 1
            need(("pp", i))
            while len(own_q) > 4:
                pop_own()
            for kc in range(KC):
                S.op("pe", lambda t, i=i, kc=kc, mov=mov, wt=wt, n=n: t.matmul(
                    pp[i][:, 0:n], lhsT=wt[:, kc, :], rhs=mov(kc), start=(kc == 0), stop=(kc == KC - 1)),
                    reads=[Bw, B_hTh, B_hTo], writes=[B_pp[i]])
                if kc in (3, 7, 11, 15):
                    pop_own()
                elif kc in (1, 5, 9, 13):
                    pop_bg()
            return i

        with ExitStack() as ea:
            def sba(name, shape, dt):
                return ea.enter_context(nc.sbuf_tensor("a_" + name, shape, dt))

            xt = [sba(f"xt{i}", [128, D], F32) for i in range(3)]
            xn = [sba(f"xn{i}", [128, D], BF16) for i in range(2)]
            junk = sba("junk", [128, D], BF16)
            gb = sba("gb", [128, D], F32)
            mskf = sba("mskf", [128, NMSK], F32)
            stat = sba("stat", [128, 3 * 24], F32)
            tpa = [ea.enter_context(nc.psum_tensor(f"tpa{i}", [128, 8, 128], BF16)) for i in range(2)]
            B_xt = [Buf("xt0"), Buf("xt1"), Buf("xt2")]
            B_xn = [Buf("xn0"), Buf("xn1")]
            B_junk, B_gb, B_mskf, B_stat = Buf("junk"), Buf("gb"), Buf("mskf"), Buf("stat")
            B_st = [Buf(f"st{j}") for j in range(NTOK // 128)]
            B_tpa = [Buf("tpa0"), Buf("tpa1")]

            S.op("sp", lambda s: s.dma_start(out=cst[:], in_=cst_d), writes=[B_cst], dsem="dc0")
            S.op("sp", lambda s: s.dma_start(out=mskf[:], in_=msk_d), writes=[B_mskf], dsem="dc1")
            S.op("sp", lambda s: s.dma_start(out=gb[:], in_=gain), writes=[B_gb], dsem="dc2")
            S.op("dve", lambda v: v.tensor_copy(out=mskb[:], in_=mskf[:]), reads=[B_mskf], writes=[B_mskb])
            S.op("dve", lambda v: v.memset(ones[:], 1.0), writes=[B_ones])
            S.op("dve", lambda v: v.memset(stat[:], 0.0), writes=[B_stat])
            S.op("dve", lambda v: v.tensor_scalar_mul(out=pshalf[:], in0=cst[:, C_PS:C_PS + 8], scalar1=0.5),
                 reads=[B_cst], writes=[B_misc])
            S.op("dve", lambda v: v.tensor_scalar_mul(out=halfb[:], in0=cst[:, C_BG:C_BG + 32], scalar1=0.5),
                 reads=[B_cst], writes=[B_misc])
            S.op("dve", lambda v: v.tensor_scalar_mul(out=gb[:], in0=gb[:], scalar1=float(np.sqrt(D))),
                 reads=[B_gb], writes=[B_gb])

            NTT = NTOK // 128

            def a_front(j):
                b, b3, c0 = j % 2, j % 3, 3 * j
                S.op("sp", lambda s: s.dma_start(out=xt[b3][:], in_=xh[j * 128:(j + 1) * 128, :]),
                     writes=[B_xt[b3]], dsem=f"dx{b3}")
                S.op("act", lambda a: a.activation(out=junk[:], in_=xt[b3][:], func=AF.Square,
                                                   accum_out=stat[:, c0:c0 + 1]),
                     reads=[B_xt[b3], B_stat], writes=[B_junk, B_st[j]])
                S.op("act", lambda a: a.activation(out=stat[:, c0 + 1:c0 + 2], in_=stat[:, c0:c0 + 1],
                                                   func=AF.Ln, bias=float(D * EPS)),
                     reads=[B_st[j]], writes=[B_st[j]])
                S.op("act", lambda a: a.activation(out=stat[:, c0 + 2:c0 + 3], in_=stat[:, c0 + 1:c0 + 2],
                                                   func=AF.Exp, scale=-0.5),
                     reads=[B_st[j]], writes=[B_st[j]])
                S.op("dve", lambda v: v.scalar_tensor_tensor(
                    out=xn[b][:], in0=xt[b3][:], scalar=stat[:, c0 + 2:c0 + 3], in1=gb[:],
                    op0=ALU.mult, op1=ALU.mult),
                    reads=[B_xt[b3], B_st[j], B_gb], writes=[B_xn[b]])

            def a_back(j):
                b = j % 2
                for half in range(2):
                    for k8 in range(8):
                        kc = half * 8 + k8
                        S.op("pe", lambda t, half=half, k8=k8, kc=kc: t.transpose(
                            tpa[half][:, k8, :], xn[b][:, kc * 128:(kc + 1) * 128], ident),
                            reads=[B_xn[b], B_mskb], writes=[B_tpa[half]])
                    if j < 16:
                        dst = hTh[:, half * 8:(half + 1) * 8, j * 128:(j + 1) * 128]
                        bd = B_hTh
                    else:
                        dst = hTo[:, half * 8:(half + 1) * 8, (j - 16) * 128:(j - 15) * 128]
                        bd = B_hTo
                    if half == 0:
                        S.op("act", lambda a, dst=dst, half=half: a.copy(out=dst, in_=tpa[half][:]),
                             reads=[B_tpa[half]], writes=[bd])
                    else:
                        S.op("dve", lambda v, dst=dst, half=half: v.tensor_copy(out=dst, in_=tpa[half][:]),
                             reads=[B_tpa[half]], writes=[bd])

            for j in range(NTT + 1):
                if j < NTT:
                    a_front(j)
                if j >= 1:
                    a_back(j - 1)
            S.emit_phase()

        with ExitStack() as eb:
            def sbb(name, shape, dt):
                return eb.enter_context(nc.sbuf_tensor("b_" + name, shape, dt))

            def psb(name, shape, dt):
                return eb.enter_context(nc.psum_tensor("bp_" + name, shape, dt))

            tab = sbb("tab", [32, 2, NTOK], BF16)
            B_tab = Buf("tab")
            NU = 2
            kS = [sbb(f"kS{i}", [128, 3072], BF16) for i in range(NU)]
            qS = [sbb(f"qS{i}", [128, 1024], BF16) for i in range(NU)]
            vS = [sbb(f"vS{i}", [128, 24, 128], BF16) for i in range(NU)]
            B_kS = [Buf(f"kS{i}") for i in range(NU)]
            B_qS = [Buf(f"qS{i}") for i in range(NU)]
            B_vS = [Buf(f"vS{i}") for i in range(NU)]
            vT = [sbb(f"vT{i}", [128, 512], BF16) for i in range(2)]
            B_vT = [Buf("vT0"), Buf("vT1")]
            vT3 = sbb("vT3", [128, 3072], BF16)
            B_vT3 = Buf("vT3")
            sq = [sbb(f"sq{i}", [128, 512], BF16) for i in range(2)]
            B_sq = [Buf("sq0"), Buf("sq1")]
            rr = [sbb(f"rr{i}", [128, 512], F32) for i in range(2)]
            B_rr = [Buf("rr0"), Buf("rr1")]
            rotbs = [sbb(f"rotb{i}", [32, 1536], BF16) for i in range(2)]
            B_rots = [[Buf(f"rot{i}a"), Buf(f"rot{i}b")] for i in range(2)]
            NE = 3
            Eb = [sbb(f"E{i}", [128, 512], BF16) for i in range(NE)]
            B_E = [Buf(f"E{i}") for i in range(NE)]
            NUMs = sbb("NUM", [128, TOWN], F32)
            DENs = sbb("DEN", [128, TOWN], F32)
            szs = sbb("sz", [128, TOWN], F32)
            ez = [sbb(f"ez{i}", [128, 512], F32) for i in range(2)]
            B_NUM, B_DEN, B_sz = Buf("NUM"), Buf("DEN"), Buf("sz")
            B_ez = [Buf("ez0"), Buf("ez1")]

            pp = [psb(f"pp{i}", [128, 512], F32) for i in range(3)]
            B_pp = [Buf("pp0"), Buf("pp1"), Buf("pp2")]
            prot = psb("prot", [128, 512], F32)
            B_prot = Buf("prot")
            pS = prot
            B_pS = B_prot
            paux = prot[:, :].bitcast(BF16)
            B_paux = B_prot
            psc = [psb(f"psc{i}", [128, 512], F32) for i in range(2)]
            B_psc = [Buf("psc0"), Buf("psc1")]
            pnds = [psb(f"pnd{i}", [128, 512], F32) for i in range(2)]
            B_pnds = [Buf("pnd0"), Buf("pnd1")]

            S.op("pool", lambda g: g.dma_start(out=tab[:], in_=tab_d), writes=[B_tab], dsem="dc3")

            ctr = {"pp": 0, "vT": 0, "sq": 0, "rr": 0, "rtmp": 0, "E": 0, "psc": 0, "ez": 0, "pnd": 0, "rot": 0}

            def nxt(name, n):
                i = ctr[name] % n
                ctr[name] += 1
                return i

            hTh_v16 = lambda kc: hTh[:, kc, :].rearrange("p (l r) -> p r l", r=16)
            hTo_v16 = lambda kc: hTo[:, kc, :].rearrange("p (l r) -> p r l", r=16)

            def tabv(cs, lo, hi, r):
                a = tab[:, cs, lo:hi]
                if r > 1:
                    a = a.rearrange("p (l r) -> p r l", r=r)
                return a

            def groups_for(g, kind, u):
                res = []
                ks, qs, vs = kS[u], qS[u], vS[u]
                if g == 0:
                    lst = []
                    if kind != "q":
                        lst.append(("h", 1920, 2048))
                    lst += [("o", 0, 512), ("o", 512, 1024)]
                    for (wh, lo, hi) in lst:
                        n = hi - lo
                        if wh == "h":
                            mov = lambda kc, lo=lo, hi=hi: hTh[:, kc, lo:hi]
                            tlo = lo
                            kd = ks[:, 0:128]
                            vt0 = 0
                        else:
                            mov = lambda kc, lo=lo, hi=hi: hTo[:, kc, lo:hi]
                            tlo = THALO + lo
                            kd = ks[:, 128 + lo:128 + hi]
                            vt0 = 1 + lo // 128
                        dst = kd if kind == "k" else (qs[:, lo:hi] if kind == "q" else None)
                        res.append(dict(mov=mov, n=n, r=1, dst=dst, cos=tabv(0, tlo, tlo + n, 1),
                                        sin=tabv(1, tlo, tlo + n, 1), vdst=vs[:, vt0:vt0 + n // 128, :]))
                elif g == 1:
                    ks3 = ks[:, 0:1536].rearrange("p (r l) -> p r l", r=4)
                    qs3 = qs[:, 0:1024].rearrange("p (r l) -> p r l", r=4)
                    vs4 = vs[:, 0:12, :].rearrange("p (r t) c -> p r t c", r=4)
                    lst = []
                    if kind != "q":
                        lst.append(("h", 1536, 2048, 0))
                    lst += [("o", 0, 512, 1), ("o", 512, 1024, 2)]
                    for (wh, lo, hi, t) in lst:
                        if wh == "h":
                            mov = lambda kc, lo=lo, hi=hi: hTh[:, kc, lo:hi]
                            tlo = lo
                        else:
                            mov = lambda kc, lo=lo, hi=hi: hTo[:, kc, lo:hi]
                            tlo = THALO + lo
                        dst = ks3[:, :, t * 128:(t + 1) * 128] if kind == "k" else (
                            qs3[:, :, (t - 1) * 128:t * 128] if kind == "q" else None)
                        res.append(dict(mov=mov, n=512, r=4, dst=dst, cos=tabv(0, tlo, tlo + 512, 4),
                                        sin=tabv(1, tlo, tlo + 512, 4), vdst=vs4[:, :, t, :]))
                else:
                    ksh = ks[:, 0:2048].rearrange("p (r l) -> p r l", r=16)
                    kso = ks[:, 2048:3072].rearrange("p (r l) -> p r l", r=16)
                    qs3 = qs[:, 0:1024].rearrange("p (r l) -> p r l", r=16)
                    v3h = vT3[:, 0:2048].rearrange("p (r l) -> p r l", r=16)
                    v3o = vT3[:, 2048:3072].rearrange("p (r l) -> p r l", r=16)
                    if kind != "q":
                        for m in range(4):
                            mov = lambda kc, m=m: hTh[:, kc, 512 * m:512 * m + 512]
                            res.append(dict(mov=mov, n=512, r=16, dst=ksh[:, :, 32 * m:32 * m + 32],
                                            cos=tabv(0, 512 * m, 512 * m + 512, 16),
                                            sin=tabv(1, 512 * m, 512 * m + 512, 16),
                                            v3dst=v3h[:, :, 32 * m:32 * m + 32], last=False))
                    for m in range(2):
                        mov = lambda kc, m=m: hTo[:, kc, 512 * m:512 * m + 512]
                        dst = kso[:, :, 32 * m:32 * m + 32] if kind == "k" else qs3[:, :, 32 * m:32 * m + 32]
                        tlo = THALO + 512 * m
                        res.append(dict(mov=mov, n=512, r=16, dst=dst,
                                        cos=tabv(0, tlo, tlo + 512, 16), sin=tabv(1, tlo, tlo + 512, 16),
                                        v3dst=v3o[:, :, 32 * m:32 * m + 32], last=(m == 1)))
                return res

            def viewd(ap2, r):
                return ap2 if r == 1 else ap2.rearrange("p (l r) -> p r l", r=r)

            def view3(ap2, r):
                return ap2 if r == 1 else ap2.rearrange("p (r l) -> p r l", r=r)

            def post_qk(i, gd, gcol):
                n, r = gd["n"], gd["r"]
                P = pp[i][:, 0:n]
                BD = gd["B"]
                dst = gd["dst"]
                dst32 = dst[0:32]
                si = nxt("sq", 2)
                need(("sq", si))
                S.op("act", lambda a: a.activation(out=sq[si][:, 0:n], in_=P, func=AF.Square),
                     reads=[B_pp[i]], writes=[B_sq[si]])

                def stage_b():
                    S.op("pe", lambda t: t.matmul(pS[:, 0:n], lhsT=ones[:], rhs=sq[si][:, 0:n], start=True, stop=True),
                         reads=[B_sq[si], B_ones], writes=[B_pS])
                    ri = nxt("rr", 2)
                    S.op("act", lambda a: a.activation(out=rr[ri][:, 0:n], in_=pS[:, 0:n], func=AF.Ln,
                                                       bias=float(128 * EPS)),
                         reads=[B_pS], writes=[B_rr[ri]])
                    S.op("act", lambda a: a.activation(out=rr[ri][:, 0:n], in_=rr[ri][:, 0:n], func=AF.Exp, scale=-0.5),
                         reads=[B_rr[ri]], writes=[B_rr[ri]])
                    S.op("dve", lambda v: v.scalar_tensor_tensor(
                        out=dst, in0=viewd(P, r), scalar=cst[:, gcol:gcol + 1], in1=viewd(rr[ri][:, 0:n], r),
                        op0=ALU.mult, op1=ALU.mult),
                        reads=[B_pp[i], B_rr[ri], B_cst], writes=[BD])

                stage_b.res = (("pp", i), ("sq", si))
                own_q.append(stage_b)

            def post_v(i, gd, g):
                n, r = gd["n"], gd["r"]
                BD = gd["B"]
                if g < 2:
                    vi = nxt("vT", 2)
                    need(("vT", vi))
                    S.op("act", lambda a: a.copy(out=view3(vT[vi][:, 0:n], r), in_=viewd(pp[i][:, 0:n], r)),
                         reads=[B_pp[i]], writes=[B_vT[vi]])
                    nch = n // 128
                    vdst = gd["vdst"]

                    def stage_b():
                        for c in range(nch):
                            S.op("pe", lambda t, c=c: t.transpose(paux[:, c * 128:(c + 1) * 128],
                                                                  vT[vi][:, c * 128:(c + 1) * 128], ident),
                                 reads=[B_vT[vi], B_mskb], writes=[B_paux])
                        src = paux[:, 0:n].rearrange("p (t c) -> p t c", c=128)
                        S.op("act", lambda a: a.copy(out=vdst, in_=src), reads=[B_paux], writes=[BD])

                    stage_b.res = (("vT", vi),)
                    own_q.append(stage_b)
                else:
                    v3dst = gd["v3dst"]
                    need(("vT3",))
                    S.op("act", lambda a: a.copy(out=v3dst, in_=viewd(pp[i][:, 0:n], r)),
                         reads=[B_pp[i]], writes=[B_vT3])
                    if gd["last"]:
                        vs = vS[gd["u"]]
                        for j in range(3):
                            def stage_t(j=j):
                                for c in range(8):
                                    S.op("pe", lambda t, c=c: t.transpose(
                                        paux[:, c * 128:(c + 1) * 128],
                                        vT3[:, (8 * j + c) * 128:(8 * j + c + 1) * 128], ident),
                                        reads=[B_vT3, B_mskb], writes=[B_paux])
                                src = paux[:, 0:1024].rearrange("p (t c) -> p t c", c=128)
                                S.op("act", lambda a: a.copy(out=vs[:, 8 * j:8 * j + 8, :], in_=src),
                                     reads=[B_paux], writes=[BD])
                            stage_t.res = (("vT3",),)
                            own_q.append(stage_t)

            def queue_rope(kind, g, u):
                st = kS[u] if kind == "k" else qS[u]
                BD = B_kS[u] if kind == "k" else B_qS[u]
                if kind == "q":
                    chunks = [(0, 1024, 2048, 3072, (1, 4, 16)[g])]
                elif g == 0:
                    chunks = [(0, 1152, 1920, 3072, 1)]
                elif g == 1:
                    chunks = [(0, 1536, 1536, 3072, 4)]
                else:
                    chunks = [(0, 1024, 0, 2048, 16, 0), (1024, 2048, 0, 2048, 16, 1), (2048, 3072, 2048, 3072, 16)]
                for ch in chunks:
                    stage_r_self = [None]

                    def stage_r(ch=ch, stage_r_self=stage_r_self):
                        c0, c1, t0, t1, r = ch[:5]
                        n = c1 - c0
                        ri = ctr["rot"] % 2
                        busy = [f for f in late_q if getattr(f, "rot", None) == ri]
                        if busy and not getattr(stage_r_self[0], "forced", False):
                            me = stage_r_self[0]
                            me.due = busy[-1].due + 1
                            me.forced = True
                            late_q.append(me)
                            return
                        while any(getattr(f, "rot", None) == ri for f in late_q):
                            pop_late(force=True)
                        ctr["rot"] += 1
                        rotb_ = rotbs[ri]
                        Br = B_rots[ri]
                        flat = st[0:32, c0:c1]
                        dv = flat if r == 1 else flat.rearrange("p (r l) -> p r l", r=(8 if len(ch) == 6 else r))
                        rb = rotb_[:, 0:n]
                        rv = rb if r == 1 else rb.rearrange("p (r l) -> p r l", r=(8 if len(ch) == 6 else r))

                        def tv(cs):
                            a_ = tab[:, cs, t0:t1]
                            if r > 1:
                                a_ = a_.rearrange("p (l r) -> p r l", r=r)
                            if len(ch) == 6:
                                a_ = a_[:, 8 * ch[5]:8 * ch[5] + 8, :]
                            return a_
                        S.op("sp", lambda s_: s_.dma_start(out=rotb_[0:16, 0:n], in_=st[16:32, c0:c1]),
                             reads=[BD], writes=[Br[0]], dsem=f"drot{2 * ri}")
                        S.op("sp", lambda s_: s_.dma_start(out=rotb_[16:32, 0:n], in_=st[0:16, c0:c1]),
                             reads=[BD], writes=[Br[1]], dsem=f"drot{2 * ri + 1}")

                        def stage_r2():
                            S.op("dve", lambda v: v.tensor_tensor(out=dv, in0=dv, in1=tv(0), op=ALU.mult),
                                 reads=[BD, B_tab, Br[0], Br[1]], writes=[BD])
                            S.op("dve", lambda v: v.tensor_tensor(out=rv, in0=rv, in1=tv(1), op=ALU.mult),
                                 reads=[Br[0], Br[1], B_tab], writes=[Br[0], Br[1]])
                            S.op("dve", lambda v: v.tensor_tensor(out=dv, in0=dv, in1=rv, op=ALU.add),
                                 reads=[BD, Br[0], Br[1]], writes=[BD])
                        stage_r2.due = pcount[0] + 3
                        stage_r2.rot = ri
                        late_q.append(stage_r2)
                    stage_r_self[0] = stage_r
                    own_q.append(stage_r)

            MA = mskb[:, M_A:M_A + 512]
            MB = mskb[:, M_B:M_B + 512]
            MC = mskb[:, M_C:M_C + 512]

            def attention(g, u, first, unit_id):
                ks, qs, vs = kS[u], qS[u], vS[u]
                Bk, Bq, Bv = B_kS[u], B_qS[u], B_vS[u]
                batches = []
                if g == 0:
                    for bi in range(4):
                        sc, pv = [], []
                        for qq in range(2):
                            i = 2 * bi + qq + 1
                            qap = qs[:, 128 * (i - 1):128 * i]
                            e0 = qq * 256
                            sc.append((ks[:, 128 * (i - 1):128 * i], qap, e0, 128))
                            sc.append((ks[:, 128 * i:128 * (i + 1)], qap, e0 + 128, 128))
                            pv.append((qq * 128,
                                       [(vs[:, i - 1, :], e0, 0, 128), (vs[:, i, :], e0 + 128, 0, 128)]))
                        batches.append((sc, MA if bi == 0 else MB, pv))
                    numview = lambda t, p: t[:, 256 * p:256 * p + 256]
                    pview = lambda t: t
                elif g == 1:
                    ks3 = ks[:, 0:1536].rearrange("p (r l) -> p r l", r=4)
                    qs3 = qs[:, 0:1024].rearrange("p (r l) -> p r l", r=4)
                    vs4 = vs[:, 0:12, :].rearrange("p (r t) c -> p r t c", r=4)
                    for r in range(4):
                        sc, pv = [], []
                        for qq in range(2):
                            i = qq + 1
                            qap = qs3[:, r, 128 * (i - 1):128 * i]
                            e0 = qq * 256
                            sc.append((ks3[:, r, 128 * (i - 1):128 * i], qap, e0, 128))
                            sc.append((ks3[:, r, 128 * i:128 * (i + 1)], qap, e0 + 128, 128))
                            pv.append((qq * 128,
                                       [(vs4[:, r, i - 1, :], e0, 0, 128), (vs4[:, r, i, :], e0 + 128, 0, 128)]))
                        batches.append((sc, MA, pv))
                    numview = lambda t, p: t[:].rearrange("p (l r) -> p r l", r=4)[:, p, :]
                    pview = lambda t: t
                else:
                    ksh = ks[:, 0:2048].rearrange("p (r l) -> p r l", r=16)
                    kso = ks[:, 2048:3072]
                    qso = qs[:, 0:1024]
                    for bi in range(4):
                        sc, pv = [], []
                        for pq in range(2):
                            pi = 2 * bi + pq
                            e0 = pq * 256
                            r0, r1 = 2 * pi, 2 * pi + 1
                            sc.append((ksh[:, r0, :], qso[:, 64 * r0:64 * r0 + 64], e0, 64))
                            sc.append((ksh[:, r1, :], qso[:, 64 * r1:64 * r1 + 64], e0 + 64, 64))
                            sc.append((kso[:, 128 * pi:128 * pi + 128], qso[:, 128 * pi:128 * pi + 128], e0 + 128, 128))
                            pv.append((pq * 128,
                                       [(vs[:, 16 + pi, :], e0 + 128, 0, 128),
                                        (vs[:, r0, :], e0, 0, 64),
                                        (vs[:, r1, :], e0 + 64, 64, 64)]))
                        batches.append((sc, MC, pv))
                    numview = lambda t, p: t[:].rearrange("p (l r) -> p r l", r=16)[:, 4 * p:4 * p + 4, :]
                    pview = lambda t: t.rearrange("p (r l) -> p r l", r=4)

                state = {}

                def mk_s(bi):
                    sc, mask, pv = batches[bi]

                    def st():
                        si = nxt("psc", 2)
                        for (kap, qap, e0, nq) in sc:
                            S.op("pe", lambda t, kap=kap, qap=qap, e0=e0, nq=nq: t.matmul(
                                psc[si][:, e0:e0 + nq], lhsT=kap, rhs=qap, start=True, stop=True),
                                reads=[Bk, Bq], writes=[B_psc[si]])
                        ei = nxt("E", NE)
                        state[bi] = ei
                        S.op("act", lambda a: a.activation(out=Eb[ei][:], in_=psc[si][:], func=AF.Exp, scale=SQ128),
                             reads=[B_psc[si]], writes=[B_E[ei]])
                        S.op("dve", lambda v: v.tensor_tensor(out=Eb[ei][:], in0=Eb[ei][:], in1=mask, op=ALU.mult),
                             reads=[B_E[ei], B_mskb], writes=[B_E[ei]])
                    return st

                def mk_p(bi):
                    sc, mask, pv = batches[bi]

                    def st():
                        ei = state[bi]
                        pi_ = nxt("pnd", 2)
                        pnd, B_pnd = pnds[pi_], B_pnds[pi_]
                        for (c0, jobs) in pv:
                            for which in range(2):
                                nj = len(jobs)
                                for ji, (vap, e0, oc, on) in enumerate(jobs):
                                    lhs = vap if which == 0 else ones[:]
                                    cb = 256 * which + c0 + oc
                                    S.op("pe", lambda t, cb=cb, on=on, lhs=lhs, e0=e0, ji=ji, nj=nj:
                                         t.matmul(pnd[:, cb:cb + on], lhsT=lhs, rhs=Eb[ei][:, e0:e0 + on],
                                                  start=(ji == 0), stop=(ji == nj - 1)),
                                         reads=[B_E[ei], Bv, B_ones], writes=[B_pnd])
                        p = bi
                        if first:
                            S.op("act", lambda a: a.copy(out=numview(NUMs, p), in_=pview(pnd[:, 0:256])),
                                 reads=[B_pnd], writes=[B_NUM])
                            S.op("act", lambda a: a.copy(out=numview(DENs, p), in_=pview(pnd[:, 256:512])),
                                 reads=[B_pnd], writes=[B_DEN])
                        else:
                            S.op("dve", lambda v: v.tensor_tensor(out=numview(NUMs, p), in0=pview(pnd[:, 0:256]),
                                                                  in1=numview(NUMs, p), op=ALU.add),
                                 reads=[B_pnd, B_NUM], writes=[B_NUM])
                            S.op("dve", lambda v: v.tensor_tensor(out=numview(DENs, p), in0=pview(pnd[:, 256:512]),
                                                                  in1=numview(DENs, p), op=ALU.add),
                                 reads=[B_pnd, B_DEN], writes=[B_DEN])
                    return st

                def pre():
                    flush_own()
                def mk_nop():
                    def nop():
                        pass
                    return nop
                order = [mk_nop() for _ in range(16 if g == 2 else 14)] + [pre, mk_s(0), mk_s(1), mk_p(0), mk_s(2), mk_p(1), mk_s(3),
                                                        mk_p(2), mk_p(3)]
                for f in order:
                    f.unit = unit_id
                bg_q.extend(order)

            uctr = 0
            for h in range(NHEAD):
                for g in range(3):
                    u = uctr % NU
                    uctr += 1
                    first_proj = True
                    while bg_q and getattr(bg_q[0], "unit", 0) <= uctr - NU:
                        flush_own()
                        pop_bg()
                    for kind, cbase, gcol in (("k", CK, C_KG), ("v", CV, None), ("q", CQ, C_QG)):
                        col0 = cbase + g * 1024 + h * 128
                        wt, Bw = load_wblock(col0)
                        for gd in groups_for(g, kind, u):
                            gd["B"] = {"k": B_kS[u], "v": B_vS[u], "q": B_qS[u]}[kind]
                            if first_proj and uctr > NU:
                                pass
                            i = project(pp, B_pp, ctr, wt, Bw, gd["mov"], gd["n"])
                            if first_proj:
                                first_proj = False
                            gd["u"] = u
                            if kind == "v":
                                post_v(i, gd, g)
                            else:
                                post_qk(i, gd, gcol)
                        if kind != "v":
                            queue_rope(kind, g, u)
                    attention(g, u, first=(g == 0), unit_id=uctr)
                wt, Bw = load_wblock(CZA + h * 128)
                for m in range(2):
                    i = project(pp, B_pp, ctr, wt, Bw, lambda kc, m=m: hTo[:, kc, 512 * m:512 * m + 512], 512)
                    zi = nxt("ez", 2)
                    need(("ez", zi))
                    S.op("act", lambda a, i=i, zi=zi: a.activation(out=ez[zi][:], in_=pp[i][:], func=AF.Exp, scale=-1.0),
                         reads=[B_pp[i]], writes=[B_ez[zi]])

                    def stage_z(i=i, zi=zi, m=m):
                        S.op("act", lambda a: a.activation(out=ez[zi][:], in_=ez[zi][:], func=AF.Ln, bias=1.0),
                             reads=[B_ez[zi]], writes=[B_ez[zi]])
                        S.op("act", lambda a: a.activation(out=ez[zi][:], in_=ez[zi][:], func=AF.Exp, scale=-1.0),
                             reads=[B_ez[zi]], writes=[B_ez[zi]])
                        S.op("dve", lambda v: v.tensor_tensor(out=szs[:, 512 * m:512 * m + 512], in0=pp[i][:],
                                                              in1=ez[zi][:], op=ALU.mult),
                             reads=[B_ez[zi], B_pp[i]], writes=[B_sz])
                    stage_z.res = (("pp", i), ("ez", zi))
                    own_q.append(stage_z)

                def fin(h=h):
                    flush_own()
                    S.op("act", lambda a: a.activation(out=DENs[:], in_=DENs[:], func=AF.Ln),
                         reads=[B_DEN], writes=[B_DEN])
                    S.op("act", lambda a: a.activation(out=DENs[:], in_=DENs[:], func=AF.Exp, scale=-1.0),
                         reads=[B_DEN], writes=[B_DEN])
                    S.op("dve", lambda v: v.tensor_tensor(out=NUMs[:], in0=NUMs[:], in1=DENs[:], op=ALU.mult),
                         reads=[B_NUM, B_DEN], writes=[B_NUM])
                    S.op("dve", lambda v: v.tensor_tensor(out=aT[:, h, :], in0=NUMs[:], in1=szs[:], op=ALU.mult),
                         reads=[B_NUM, B_sz], writes=[B_aT[h]])
                fin.unit = uctr
                bg_q.append(fin)
            for blk in range(8):
                wt, Bw = load_wblock(CU + blk * 128)
                i = project(pp, B_pp, ctr, wt, Bw, lambda kc: hTh[:, kc, THALO - 32:THALO], 32)
                S.op("act", lambda a, i=i, blk=blk: a.copy(out=uh[:, blk, :], in_=pp[i][:, 16:32]),
                     reads=[B_pp[i]], writes=[B_uh])
            flush_bg()
            S.emit_phase()
        eAB.close()

        with ExitStack() as ec:
            def sbc(name, shape, dt):
                return ec.enter_context(nc.sbuf_tensor("c_" + name, shape, dt))

            def psc_(name, shape, dt):
                return ec.enter_context(nc.psum_tensor("cp_" + name, shape, dt))

            UW = 16 + TOWN
            uext = [sbc(f"uext{i}", [128, UW], F32) for i in range(3)]
            B_ue = [Buf(f"ue{i}") for i in range(3)]
            dT = sbc("dT", [128, 8, TOWN], BF16)
            B_dT = [Buf(f"dT{i}") for i in range(8)]
            dfix = sbc("dfix", [128, 16], F32)
            B_dfix = Buf("dfix")
            pT = sbc("pT", [128, 8, TOWN], BF16)
            B_pT = [Buf(f"pT{i}") for i in range(8)]
            pm = sbc("pm", [128, 8, 256], BF16)
            B_pm = Buf("pm")
            mT = sbc("mT", [128, KC, TOWN], BF16)
            B_mT = [Buf(f"mT{i}") for i in range(KC)]
            tg = [sbc(f"tg{i}", [128, TOWN], F32) for i in range(3)]
            B_tg = [Buf(f"tg{i}") for i in range(3)]
            wb = [sbc(f"wb{i}", [128, 8, 128], BF16) for i in range(4)]
            B_wb = [Buf(f"wb{i}") for i in range(4)]
            wo = [sbc(f"wo{i}", [128, KC, 512], BF16) for i in range(2)]
            B_wo = [Buf("wo0"), Buf("wo1")]
            xr = [sbc(f"xr{i}", [128, 512], F32) for i in range(2)]
            B_xr = [Buf(f"xr{i}") for i in range(2)]
            ob = [sbc(f"ob{i}", [128, 512], F32) for i in range(2)]
            B_ob = [Buf(f"ob{i}") for i in range(2)]

            cpp = [psc_(f"cpp{i}", [128, 512], F32) for i in range(2)]
            B_cpp = [Buf("cpp0"), Buf("cpp1")]
            pya = [psc_(f"pya{i}", [128, 512], F32) for i in range(2)]
            B_pya = [Buf("pya0"), Buf("pya1")]
            pyp = [psc_(f"pyp{i}", [128, 512], F32) for i in range(2)]
            B_pyp = [Buf("pyp0"), Buf("pyp1")]
            pfo = [psc_(f"pfo{i}", [128, 512], F32) for i in range(2)]
            B_pfo = [Buf("pfo0"), Buf("pfo1")]

            cc = {"pp": 0, "wb": 0, "wo": 0, "xr": 0, "ob": 0, "pfo": 0, "pya": 0, "pyp": 0}

            def nx(name, n):
                i = cc[name] % n
                cc[name] += 1
                return i

            own_mov = lambda m: (lambda kc: hTo[:, kc, 512 * m:512 * m + 512])

            S.op("pool", lambda g: g.dma_start(out=pm[:], in_=pmaps.rearrange("g (k p) e -> p (g k) e", p=128)),
                 writes=[B_pm], dsem="dc4")

            for blk in range(8):
                gq = blk // 2
                ksz = 2 ** (gq + 1)
                wt, Bw = load_wblock(CU + blk * 128)
                S.op("act", lambda a, blk=blk: a.copy(out=uext[0][:, 0:16], in_=uh[:, blk, :]),
                     reads=[B_uh], writes=[B_ue[0]])
                for m in range(2):
                    i = project(cpp, B_cpp, cc, wt, Bw, own_mov(m), 512)
                    S.op("act", lambda a, i=i, m=m: a.copy(out=uext[0][:, 16 + 512 * m:16 + 512 * m + 512],
                                                           in_=cpp[i][:]),
                         reads=[B_cpp[i]], writes=[B_ue[0]])
                cur = 0
                sh = 1
                for step in range(gq + 1):
                    nxtb = 1 if cur != 1 else 2
                    lo = 2 * sh - 1
                    S.op("dve", lambda v, cur=cur, nxtb=nxtb, lo=lo, sh=sh: v.tensor_tensor(
                        out=uext[nxtb][:, lo:UW], in0=uext[cur][:, lo:UW], in1=uext[cur][:, lo - sh:UW - sh],
                        op=ALU.add),
                        reads=[B_ue[cur]], writes=[B_ue[nxtb]])
                    cur = nxtb
                    sh *= 2
                S.op("dve", lambda v, cur=cur, blk=blk, ksz=ksz: v.scalar_tensor_tensor(
                    out=dT[:, blk, :], in0=uext[cur][:, 16:UW], scalar=1.0 / ksz, in1=uext[0][:, 16:UW],
                    op0=ALU.mult, op1=ALU.subtract),
                    reads=[B_ue[cur], B_ue[0]], writes=[B_dT[blk]])
                S.op("dve", lambda v, cur=cur, gq=gq: v.tensor_tensor(
                    out=dfix[:], in0=uext[cur][:, 16:32], in1=cst[:, C_IC + 16 * gq:C_IC + 16 * gq + 16],
                    op=ALU.mult),
                    reads=[B_ue[cur], B_cst], writes=[B_dfix])
                S.op("dve", lambda v, blk=blk: v.tensor_tensor(
                    out=dT[:, blk, 0:16], in0=dfix[:], in1=uext[0][:, 16:32], op=ALU.subtract),
                    reads=[B_dfix, B_ue[0]], writes=[B_dT[blk]])
            for eb2 in range(8):
                gq, half = eb2 // 2, eb2 % 2
                wt, Bw = load_wblock(CZP + eb2 * 128)
                for m in range(2):
                    i = project(cpp, B_cpp, cc, wt, Bw, own_mov(m), 512)
                    fi = nx("pfo", 2)
                    for k2 in range(2):
                        S.op("pe", lambda t, fi=fi, gq=gq, k2=k2, half=half, m=m: t.matmul(
                            pfo[fi][:], lhsT=pm[:, 2 * gq + k2, 128 * half:128 * half + 128],
                            rhs=dT[:, 2 * gq + k2, 512 * m:512 * m + 512], start=(k2 == 0), stop=(k2 == 1)),
                            reads=[B_pm, B_dT[2 * gq + k2]], writes=[B_pfo[fi]])
                    S.op("act", lambda a, i=i, m=m: a.activation(out=tg[0][:, 512 * m:512 * m + 512], in_=cpp[i][:],
                                                                 func=AF.Tanh, scale=0.5),
                         reads=[B_cpp[i]], writes=[B_tg[0]])
                    S.op("dve", lambda v, i=i, m=m: v.scalar_tensor_tensor(
                        out=tg[0][:, 512 * m:512 * m + 512], in0=tg[0][:, 512 * m:512 * m + 512], scalar=1.0,
                        in1=cpp[i][:], op0=ALU.add, op1=ALU.mult),
                        reads=[B_tg[0], B_cpp[i]], writes=[B_tg[0]])
                    S.op("dve", lambda v, fi=fi, eb2=eb2, m=m: v.scalar_tensor_tensor(
                        out=pT[:, eb2, 512 * m:512 * m + 512], in0=pfo[fi][:], scalar=pshalf[:, eb2:eb2 + 1],
                        in1=tg[0][:, 512 * m:512 * m + 512], op0=ALU.mult, op1=ALU.mult),
                        reads=[B_pfo[fi], B_tg[0], B_misc], writes=[B_pT[eb2]])

            w_ba_v = w_ba.rearrange("(h p) n -> p h n", p=128)
            w_bp_v = w_bp.rearrange("(h p) n -> p h n", p=128)
            for db in range(KC):
                wa_i = nx("wb", 4)
                S.op("pool", lambda g, wa_i=wa_i, db=db: g.dma_start(out=wb[wa_i][:],
                                                                      in_=w_ba_v[:, :, 128 * db:128 * db + 128]),
                     writes=[B_wb[wa_i]], dsem=f"db{wa_i}")
                wp_i = nx("wb", 4)
                S.op("pool", lambda g, wp_i=wp_i, db=db: g.dma_start(out=wb[wp_i][:],
                                                                      in_=w_bp_v[:, :, 128 * db:128 * db + 128]),
                     writes=[B_wb[wp_i]], dsem=f"db{wp_i}")
                for which, cbase in ((0, CG), (1, CG + D)):
                    wt, Bw = load_wblock(cbase + db * 128)
                    tgi = 1 + which
                    for m in range(2):
                        i = project(cpp, B_cpp, cc, wt, Bw, own_mov(m), 512)
                        bcol = which * KC + db
                        S.op("act", lambda a, i=i, m=m, tgi=tgi, bcol=bcol: a.activation(
                            out=tg[tgi][:, 512 * m:512 * m + 512], in_=cpp[i][:], func=AF.Tanh,
                            bias=halfb[:, bcol:bcol + 1], scale=0.5),
                            reads=[B_cpp[i], B_misc], writes=[B_tg[tgi]])
                for m in range(2):
                    ya = nx("pya", 2)
                    for h in range(NHEAD):
                        S.op("pe", lambda t, ya=ya, h=h, m=m, wa_i=wa_i: t.matmul(
                            pya[ya][:], lhsT=wb[wa_i][:, h, :], rhs=aT[:, h, 512 * m:512 * m + 512],
                            start=(h == 0), stop=(h == NHEAD - 1)),
                            reads=[B_wb[wa_i], B_aT[h]], writes=[B_pya[ya]])
                    yp = nx("pyp", 2)
                    for e in range(8):
                        S.op("pe", lambda t, yp=yp, e=e, m=m, wp_i=wp_i: t.matmul(
                            pyp[yp][:], lhsT=wb[wp_i][:, e, :], rhs=pT[:, e, 512 * m:512 * m + 512],
                            start=(e == 0), stop=(e == 7)),
                            reads=[B_wb[wp_i], B_pT[e]], writes=[B_pyp[yp]])
                    S.op("dve", lambda v, ya=ya, m=m: v.scalar_tensor_tensor(
                        out=tg[1][:, 512 * m:512 * m + 512], in0=tg[1][:, 512 * m:512 * m + 512], scalar=1.0,
                        in1=pya[ya][:], op0=ALU.add, op1=ALU.mult),
                        reads=[B_tg[1], B_pya[ya]], writes=[B_tg[1]])
                    S.op("dve", lambda v, yp=yp, m=m: v.scalar_tensor_tensor(
                        out=tg[2][:, 512 * m:512 * m + 512], in0=tg[2][:, 512 * m:512 * m + 512], scalar=1.0,
                        in1=pyp[yp][:], op0=ALU.add, op1=ALU.mult),
                        reads=[B_tg[2], B_pyp[yp]], writes=[B_tg[2]])
                S.op("dve", lambda v, db=db: v.tensor_tensor(out=mT[:, db, :], in0=tg[1][:], in1=tg[2][:], op=ALU.add),
                     reads=[B_tg[1], B_tg[2]], writes=[B_mT[db]])

            w_out_v = w_out.rearrange("(k p) n -> p k n", p=128)
            for cg in range(4):
                wi = nx("wo", 2)
                S.op("pool", lambda g, wi=wi, cg=cg: g.dma_start(out=wo[wi][:],
                                                                  in_=w_out_v[:, :, 512 * cg:512 * cg + 512]),
                     writes=[B_wo[wi]], dsem=f"dwo{wi}")
                for tt in range(8):
                    xi = nx("xr", 2)
                    S.op("act", lambda a, xi=xi, tt=tt, cg=cg: a.dma_start(
                        out=xr[xi][:], in_=xh[THALO + 128 * tt:THALO + 128 * tt + 128, 512 * cg:512 * cg + 512]),
                        writes=[B_xr[xi]], dsem=f"dr{xi}")
                    fi = nx("pfo", 2)
                    for db in range(KC):
                        S.op("pe", lambda t, fi=fi, db=db, tt=tt, wi=wi: t.matmul(
                            pfo[fi][:], lhsT=mT[:, db, 128 * tt:128 * tt + 128], rhs=wo[wi][:, db, :],
                            start=(db == 0), stop=(db == KC - 1)),
                            reads=[B_mT[db], B_wo[wi]], writes=[B_pfo[fi]])
                    oi = nx("ob", 2)
                    S.op("dve", lambda v, fi=fi, xi=xi, oi=oi: v.scalar_tensor_tensor(
                        out=ob[oi][:], in0=pfo[fi][:], scalar=0.5, in1=xr[xi][:], op0=ALU.mult, op1=ALU.add),
                        reads=[B_pfo[fi], B_xr[xi]], writes=[B_ob[oi]])
                    S.op("sp", lambda s, oi=oi, tt=tt, cg=cg: s.dma_start(
                        out=out_d[128 * tt:128 * tt + 128, 512 * cg:512 * cg + 512], in_=ob[oi][:]),
                        reads=[B_ob[oi]], dsem=f"do{oi}")
            S.emit_phase()
    return nc


def _host_consts(c):
    i = np.arange(128)[:, None]
    m = np.arange(128)[None, :]
    cur = (i <= m).astype(np.float32)
    prev = (i >= m).astype(np.float32)
    halo = prev if c > 0 else np.zeros_like(prev)
    msk = np.zeros((128, NMSK), np.float32)
    msk[:, M_A:M_A + 512] = np.concatenate([halo, cur, prev, cur], axis=1)
    msk[:, M_B:M_B + 512] = np.concatenate([prev, cur, prev, cur], axis=1)
    i64 = np.arange(128)[:, None]
    m64 = np.arange(64)[None, :]
    if c == 0:
        halo3 = np.zeros((128, 64), np.float32)
    elif c == 1:
        halo3 = ((i64 >= m64) & (i64 >= 64)).astype(np.float32)
    else:
        halo3 = (i64 >= m64).astype(np.float32)
    bd = np.zeros((128, 128), np.float32)
    c64 = (np.arange(64)[:, None] <= np.arange(64)[None, :]).astype(np.float32)
    bd[0:64, 0:64] = c64
    bd[64:128, 64:128] = c64
    one = np.concatenate([halo3, halo3, bd], axis=1)
    msk[:, M_C:M_C + 512] = np.concatenate([one, one], axis=1)
    msk[:, M_ID:M_ID + 128] = np.eye(128, dtype=np.float32)
    rt = np.zeros((32, 32), np.float32)
    for cc in range(16):
        rt[cc + 16, cc] = -1.0
        rt[cc, cc + 16] = 1.0
    msk[0:32, M_RT:M_RT + 32] = rt
    start = 1024 * c
    pos = np.arange(start - THALO, start + TOWN).astype(np.float64)
    inv = 500000.0 ** (-np.arange(0, 32, 2, dtype=np.float64) / 32.0)
    ang = (pos[None, :].astype(np.float32) * inv[:, None].astype(np.float32)).astype(np.float64)
    tab = np.zeros((32, 2, NTOK), np.float32)
    tab[0:16, 0] = np.cos(ang)
    tab[16:32, 0] = np.cos(ang)
    tab[0:16, 1] = -np.sin(ang)
    tab[16:32, 1] = np.sin(ang)
    ic = np.zeros((4, 16), np.float32)
    for gq in range(4):
        k = 2 ** (gq + 1)
        if c == 0:
            ic[gq] = 1.0 / np.minimum(np.arange(16) + 1, k)
        else:
            ic[gq] = 1.0 / k
    return msk, tab, ic


_NC_CACHE = {}


def kernel(x, norm_gain, w_in, b_gates, q_norm_gain, k_norm_gain, pool_maps, pool_scale,
           w_branch_attn, w_branch_pool, w_out):
    x = np.asarray(x, np.float32)
    f = lambda a: np.ascontiguousarray(np.asarray(a, np.float32))
    w_in, w_ba, w_bp, w_o, pmaps = f(w_in), f(w_branch_attn), f(w_branch_pool), f(w_out), f(pool_maps)
    gain = np.ascontiguousarray(np.broadcast_to(f(norm_gain)[None, :], (128, D)))
    if "nc" not in _NC_CACHE:
        _NC_CACHE["nc"] = build_program()
    nc = _NC_CACHE["nc"]
    in_maps = []
    for core in range(8):
        b, c = core // 4, core % 4
        start = 1024 * c
        xh = np.zeros((NTOK, D), np.float32)
        lo = start - THALO
        src_lo = max(lo, 0)
        xh[src_lo - lo:] = x[b, src_lo:start + TOWN]
        msk, tab, ic = _host_consts(c)
        cst = np.zeros((128, NCST), np.float32)
        cst[:, C_QG] = f(q_norm_gain)
        cst[:, C_KG] = f(k_norm_gain)
        cst[:, C_PS:C_PS + 8] = f(pool_scale).reshape(8, 128).T
        cst[:, C_BG:C_BG + 32] = f(b_gates).reshape(32, 128).T
        cst[:, C_IC:C_IC + 64] = ic.reshape(1, 64)
        in_maps.append({"xh": xh, "w_in": w_in, "w_ba": w_ba, "w_bp": w_bp, "w_out": w_o, "pmaps": pmaps,
                        "gain": gain, "cst": cst, "msk": msk, "tab": tab})
    res = run_bass_kernel_spmd(nc, in_maps, core_ids=list(range(8)))
    out = np.zeros((2, SEQ, D), np.float32)
    for core in range(8):
        b, c = core // 4, core % 4
        out[b, 1024 * c:1024 * c + 1024] = res.results[core]["out"]
    return out
```
